# Optimizing a Trainium2 kernel written in Bass

```python
import math
import jax
import jax.numpy as jnp
from jax import lax
import numpy as np


D_MODEL = 1024
BATCH = 4
SEQ = 4096
DEPTH = 4

N_BRANCH = 3
CHUNK = 64
EPS = 1e-6
GLA_HEADS = 4
GLA_DK = 64
GLA_DV = 128
GLA_WIDTH = GLA_HEADS * GLA_DV
GLA_LOWRANK = 16
GLA_GATE_NORM = 16.0
S5_GROUP_CH = 16
S5_GROUPS = 32
S5_WIDTH = S5_GROUPS * S5_GROUP_CH
S5_STATE = 64
S5_DT_MIN = 0.001
S5_DT_MAX = 0.1
GDN_HEADS = 4
GDN_DK = 128
GDN_DV = 128
GDN_WIDTH = GDN_HEADS * GDN_DV
GDN_QKV = 2 * GDN_HEADS * GDN_DK + GDN_WIDTH
GDN_CONV = 4
D_FF = ((8 * D_MODEL + 3 * 256 - 1) // (3 * 256)) * 256
IN_SIZES = (GLA_HEADS * GLA_DK, GLA_HEADS * GLA_DK, GLA_WIDTH, GLA_LOWRANK, GLA_WIDTH,
            S5_WIDTH,
            GDN_QKV, GDN_HEADS, GDN_HEADS, GDN_WIDTH,
            N_BRANCH * D_MODEL)
D_IN = sum(IN_SIZES)

kernel_name = 'hybrid_gla_s5_gdn_adaln_block'


def rmsnorm(x, g):
    xf = x.astype(jnp.float32)
    y = xf * lax.rsqrt(jnp.mean(xf * xf, axis=-1, keepdims=True) + EPS)
    return (y * g.astype(jnp.float32)).astype(x.dtype)


def l2norm(x):
    return x * lax.rsqrt(jnp.sum(x * x, axis=-1, keepdims=True) + EPS)


def to_chunks(x):
    b, l, h, d = x.shape
    return x.reshape(b, l // CHUNK, CHUNK, h, d).transpose(0, 3, 1, 2, 4)


def to_chunks_scalar(x):
    b, l, h = x.shape
    return x.reshape(b, l // CHUNK, CHUNK, h).transpose(0, 3, 1, 2)


def from_chunks(x):
    b, h, n, c, d = x.shape
    return x.transpose(0, 2, 3, 1, 4).reshape(b, n * c, h, d)


def causal_dwconv(x, w):
    k = w.shape[0]
    return lax.conv_general_dilated(x, w[:, None, :], window_strides=(1,), padding=[(k - 1, 0)],
                                    dimension_numbers=('NWC', 'WIO', 'NWC'),
                                    feature_group_count=x.shape[-1])


def gla_mixer(q, k, v, lr, og, w_lr, b_lr, norm_g):
    f32 = jnp.float32
    bsz, l, _ = q.shape
    gk = jax.nn.log_sigmoid(lr.astype(f32) @ w_lr.astype(f32) + b_lr.astype(f32)) / GLA_GATE_NORM
    qc = to_chunks(q.astype(f32).reshape(bsz, l, GLA_HEADS, GLA_DK)) * GLA_DK ** -0.5
    kc = to_chunks(k.astype(f32).reshape(bsz, l, GLA_HEADS, GLA_DK))
    vc = to_chunks(v.astype(f32).reshape(bsz, l, GLA_HEADS, GLA_DV))
    gc = to_chunks(gk.reshape(bsz, l, GLA_HEADS, GLA_DK))
    bcum = jnp.cumsum(gc, axis=3)
    b_last = bcum[:, :, :, -1:, :]
    q_e = qc * jnp.exp(bcum)
    k_e = kc * jnp.exp(-bcum)
    idx = jnp.arange(CHUNK)
    incl = idx[:, None] >= idx[None, :]
    scores = jnp.where(incl, jnp.einsum('bhnik,bhnjk->bhnij', q_e, k_e), 0.0)
    o_intra = jnp.einsum('bhnij,bhnjv->bhniv', scores, vc)
    kv = jnp.einsum('bhnck,bhncv->bhnkv', kc * jnp.exp(b_last - bcum), vc)
    decay = jnp.exp(b_last[:, :, :, 0, :])

    def step(s, inp):
        d_n, kv_n = inp
        return d_n[..., None] * s + kv_n, s

    s0 = jnp.zeros((bsz, GLA_HEADS, GLA_DK, GLA_DV), f32)
    _, s_prev = lax.scan(step, s0, (jnp.moveaxis(decay, 2, 0), jnp.moveaxis(kv, 2, 0)))
    o_inter = jnp.einsum('bhnck,nbhkv->bhncv', q_e, s_prev)
    o = from_chunks(o_intra + o_inter)
    o = rmsnorm(o, norm_g) * jax.nn.silu(og.astype(f32).reshape(bsz, l, GLA_HEADS, GLA_DV))
    return o.reshape(bsz, l, GLA_WIDTH)


def s5_mixer(u, lam_re, lam_im, log_dt, b_re, b_im, c_re, c_im, d, w_glu):
    f32 = jnp.float32
    bsz, l, _ = u.shape
    dt = jnp.exp(log_dt.astype(f32))[:, None]
    lr, li = lam_re.astype(f32), lam_im.astype(f32)
    mag = jnp.exp(lr * dt)
    ab_re = mag * jnp.cos(li * dt)
    ab_im = mag * jnp.sin(li * dt)
    den = lr * lr + li * li
    nr, ni = ab_re - 1.0, ab_im
    w_re = ((nr * lr + ni * li) / den)[..., None]
    w_im = ((ni * lr - nr * li) / den)[..., None]
    br, bi = b_re.astype(f32), b_im.astype(f32)
    bb_re = w_re * br - w_im * bi
    bb_im = w_re * bi + w_im * br
    ug = u.astype(f32).reshape(bsz, l, S5_GROUPS, S5_GROUP_CH)
    bu_re = jnp.einsum('blgh,gph->blgp', ug, bb_re)
    bu_im = jnp.einsum('blgh,gph->blgp', ug, bb_im)
    a_re = jnp.broadcast_to(ab_re, (1, l, S5_GROUPS, S5_STATE))
    a_im = jnp.broadcast_to(ab_im, (1, l, S5_GROUPS, S5_STATE))

    def combine(e1, e2):
        ar1, ai1, br1, bi1 = e1
        ar2, ai2, br2, bi2 = e2
        return (ar2 * ar1 - ai2 * ai1, ar2 * ai1 + ai2 * ar1,
                ar2 * br1 - ai2 * bi1 + br2, ar2 * bi1 + ai2 * br1 + bi2)

    _, _, s_re, s_im = lax.associative_scan(combine, (a_re, a_im, bu_re, bu_im), axis=1)
    y = (jnp.einsum('blgp,ghp->blgh', s_re, c_re.astype(f32))
         - jnp.einsum('blgp,ghp->blgh', s_im, c_im.astype(f32))
         + d.astype(f32) * ug)
    y = jax.nn.gelu(y).reshape(bsz, l, S5_WIDTH)
    ga, gb = jnp.split(y @ w_glu.astype(f32), 2, axis=-1)
    return ga * jax.nn.sigmoid(gb)


def gdn_mixer(qkv, beta_pre, a_pre, og, conv_w, a_log, dt_bias, norm_g):
    f32 = jnp.float32
    bsz, l, _ = qkv.shape
    qkv = jax.nn.silu(causal_dwconv(qkv.astype(f32), conv_w.astype(f32)))
    q, k, v = jnp.split(qkv, [GDN_HEADS * GDN_DK, 2 * GDN_HEADS * GDN_DK], axis=-1)
    q = l2norm(q.reshape(bsz, l, GDN_HEADS, GDN_DK)) * GDN_DK ** -0.5
    k = l2norm(k.reshape(bsz, l, GDN_HEADS, GDN_DK))
    v = v.reshape(bsz, l, GDN_HEADS, GDN_DV)
    beta = jax.nn.sigmoid(beta_pre.astype(f32))
    g = -jnp.exp(a_log.astype(f32)) * jax.nn.softplus(a_pre.astype(f32) + dt_bias.astype(f32))
    qc, kc, vc = to_chunks(q), to_chunks(k), to_chunks(v)
    bc = to_chunks_scalar(beta)
    gam = jnp.cumsum(to_chunks_scalar(g), axis=-1)
    idx = jnp.arange(CHUNK)
    incl = idx[:, None] >= idx[None, :]
    strict = idx[:, None] > idx[None, :]
    decay_mask = jnp.exp(jnp.where(incl, gam[..., :, None] - gam[..., None, :], -jnp.inf))
    k_beta = kc * bc[..., None]
    v_beta = vc * bc[..., None]
    lower = jnp.where(strict, jnp.einsum('bhnik,bhnjk->bhnij', k_beta, kc) * decay_mask, 0.0)
    eye = jnp.eye(CHUNK, dtype=f32)
    rhs = jnp.concatenate([v_beta, k_beta * jnp.exp(gam)[..., None]], axis=-1)
    sol = lax.linalg.triangular_solve(eye + lower, rhs, left_side=True, lower=True,
                                      unit_diagonal=True)
    u_c, w_c = jnp.split(sol, [GDN_DV], axis=-1)
    attn = jnp.einsum('bhnik,bhnjk->bhnij', qc, kc) * decay_mask
    q_dec = qc * jnp.exp(gam)[..., None]
    gam_last = gam[..., -1:]
    k_dec = kc * jnp.exp(gam_last - gam)[..., None]
    chunk_decay = jnp.exp(gam_last[..., 0])

    def step(s, inp):
        q_n, k_n, u_n, w_n, a_n, d_n = inp
        v_new = u_n - jnp.einsum('bhck,bhkv->bhcv', w_n, s)
        o_n = jnp.einsum('bhck,bhkv->bhcv', q_n, s) + jnp.einsum('bhij,bhjv->bhiv', a_n, v_new)
        s = d_n[..., None, None] * s + jnp.einsum('bhck,bhcv->bhkv', k_n, v_new)
        return s, o_n

    xs = (jnp.moveaxis(q_dec, 2, 0), jnp.moveaxis(k_dec, 2, 0), jnp.moveaxis(u_c, 2, 0),
          jnp.moveaxis(w_c, 2, 0), jnp.moveaxis(attn, 2, 0), jnp.moveaxis(chunk_decay, 2, 0))
    s0 = jnp.zeros((bsz, GDN_HEADS, GDN_DK, GDN_DV), f32)
    _, o = lax.scan(step, s0, xs)
    o = from_chunks(jnp.moveaxis(o, 0, 2))
    o = rmsnorm(o, norm_g) * jax.nn.silu(og.astype(f32).reshape(bsz, l, GDN_HEADS, GDN_DV))
    return o.reshape(bsz, l, GDN_WIDTH)


def setup_inputs(seed: int = 0) -> dict:
    key = jax.random.key(seed)
    ks = jax.random.split(key, 32)
    f32 = jnp.float32

    def nrm(k, shape, scale):
        return jax.random.normal(k, shape, f32) * scale

    nl, d = DEPTH, D_MODEL
    x = nrm(ks[0], (BATCH, SEQ, d), 1.0)
    c = nrm(ks[1], (BATCH, d), 1.0)
    w_ada = nrm(ks[2], (nl, d, 6 * d), 0.1 * d ** -0.5)
    b_ada = nrm(ks[3], (nl, 6 * d), 0.02)
    norm1_g = 1.0 + nrm(ks[4], (nl, d), 0.02)
    w_in = nrm(ks[5], (nl, d, D_IN), d ** -0.5)
    gla_w_lr = nrm(ks[6], (nl, GLA_LOWRANK, GLA_HEADS * GLA_DK), GLA_LOWRANK ** -0.5)
    gla_b_lr = nrm(ks[7], (nl, GLA_HEADS * GLA_DK), 0.1)
    gla_norm_g = 1.0 + nrm(ks[8], (nl, GLA_DV), 0.02)
    s5_lambda_re = -0.5 + nrm(ks[9], (nl, S5_GROUPS, S5_STATE), 0.01)
    s5_lambda_im = (math.pi * jnp.arange(S5_STATE, dtype=f32)
                    + nrm(ks[10], (nl, S5_GROUPS, S5_STATE), 0.01))
    s5_log_dt = jax.random.uniform(ks[11], (nl, S5_GROUPS), f32,
                                   minval=math.log(S5_DT_MIN), maxval=math.log(S5_DT_MAX))
    s5_b_re = nrm(ks[12], (nl, S5_GROUPS, S5_STATE, S5_GROUP_CH), (2 * S5_GROUP_CH) ** -0.5)
    s5_b_im = nrm(ks[13], (nl, S5_GROUPS, S5_STATE, S5_GROUP_CH), (2 * S5_GROUP_CH) ** -0.5)
    s5_c_re = nrm(ks[14], (nl, S5_GROUPS, S5_GROUP_CH, S5_STATE), (2 * S5_STATE) ** -0.5)
    s5_c_im = nrm(ks[15], (nl, S5_GROUPS, S5_GROUP_CH, S5_STATE), (2 * S5_STATE) ** -0.5)
    s5_d = nrm(ks[16], (nl, S5_GROUPS, S5_GROUP_CH), 0.5)
    s5_w_glu = nrm(ks[17], (nl, S5_WIDTH, 2 * S5_WIDTH), S5_WIDTH ** -0.5)
    gdn_conv_w = nrm(ks[18], (nl, GDN_CONV, GDN_QKV), GDN_CONV ** -0.5)
    gdn_a_log = jnp.log(jax.random.uniform(ks[19], (nl, GDN_HEADS), f32, minval=1.0, maxval=16.0))
    gdn_dt = jnp.exp(jax.random.uniform(ks[20], (nl, GDN_HEADS), f32,
                                        minval=math.log(0.001), maxval=math.log(0.1)))
    gdn_dt_bias = gdn_dt + jnp.log(-jnp.expm1(-gdn_dt))
    gdn_norm_g = 1.0 + nrm(ks[21], (nl, GDN_DV), 0.02)
    w_branch_gla = nrm(ks[22], (nl, GLA_WIDTH, d), GLA_WIDTH ** -0.5)
    w_branch_s5 = nrm(ks[23], (nl, S5_WIDTH, d), S5_WIDTH ** -0.5)
    w_branch_gdn = nrm(ks[24], (nl, GDN_WIDTH, d), GDN_WIDTH ** -0.5)
    w_out = nrm(ks[25], (nl, d, d), d ** -0.5)
    norm2_g = 1.0 + nrm(ks[26], (nl, d), 0.02)
    w_ffn_in = nrm(ks[27], (nl, d, 2 * D_FF), d ** -0.5)
    w_ffn_out = nrm(ks[28], (nl, D_FF, d), D_FF ** -0.5)
    final_g = 1.0 + nrm(ks[29], (d,), 0.02)
    return {'x': x, 'c': c, 'w_ada': w_ada, 'b_ada': b_ada, 'norm1_g': norm1_g, 'w_in': w_in,
            'gla_w_lr': gla_w_lr, 'gla_b_lr': gla_b_lr, 'gla_norm_g': gla_norm_g,
            's5_lambda_re': s5_lambda_re, 's5_lambda_im': s5_lambda_im, 's5_log_dt': s5_log_dt,
            's5_b_re': s5_b_re, 's5_b_im': s5_b_im, 's5_c_re': s5_c_re, 's5_c_im': s5_c_im,
            's5_d': s5_d, 's5_w_glu': s5_w_glu,
            'gdn_conv_w': gdn_conv_w, 'gdn_a_log': gdn_a_log, 'gdn_dt_bias': gdn_dt_bias,
            'gdn_norm_g': gdn_norm_g,
            'w_branch_gla': w_branch_gla, 'w_branch_s5': w_branch_s5, 'w_branch_gdn': w_branch_gdn,
            'w_out': w_out, 'norm2_g': norm2_g, 'w_ffn_in': w_ffn_in, 'w_ffn_out': w_ffn_out,
            'final_g': final_g}


def reference(x, c, w_ada, b_ada, norm1_g, w_in, gla_w_lr, gla_b_lr, gla_norm_g,
              s5_lambda_re, s5_lambda_im, s5_log_dt, s5_b_re, s5_b_im, s5_c_re, s5_c_im,
              s5_d, s5_w_glu, gdn_conv_w, gdn_a_log, gdn_dt_bias, gdn_norm_g,
              w_branch_gla, w_branch_s5, w_branch_gdn, w_out, norm2_g, w_ffn_in, w_ffn_out,
              final_g):
    bsz, l, d = x.shape
    dtype = x.dtype
    split_points = np.cumsum(IN_SIZES)[:-1].tolist()
    cond = jax.nn.silu(c)
    for i in range(DEPTH):
        mod = cond @ w_ada[i] + b_ada[i]
        sh1, sc1, gt1, sh2, sc2, gt2 = jnp.split(mod[:, None, :], 6, axis=-1)
        h = rmsnorm(x, norm1_g[i]) * (1.0 + sc1) + sh1
        z = h @ w_in[i]
        (gq, gk, gv, glr, gog, s5u, dqkv, dbeta, da, dog, zg) = jnp.split(z, split_points, axis=-1)
        y_gla = gla_mixer(gq, gk, gv, glr, gog, gla_w_lr[i], gla_b_lr[i], gla_norm_g[i]).astype(dtype)
        y_s5 = s5_mixer(s5u, s5_lambda_re[i], s5_lambda_im[i], s5_log_dt[i], s5_b_re[i], s5_b_im[i],
                        s5_c_re[i], s5_c_im[i], s5_d[i], s5_w_glu[i]).astype(dtype)
        y_gdn = gdn_mixer(dqkv, dbeta, da, dog, gdn_conv_w[i], gdn_a_log[i], gdn_dt_bias[i],
                          gdn_norm_g[i]).astype(dtype)
        gates = jax.nn.sigmoid(zg.reshape(bsz, l, N_BRANCH, d))
        merged = (gates[:, :, 0] * (y_gla @ w_branch_gla[i])
                  + gates[:, :, 1] * (y_s5 @ w_branch_s5[i])
                  + gates[:, :, 2] * (y_gdn @ w_branch_gdn[i]))
        x = x + gt1 * (merged @ w_out[i])
        h = rmsnorm(x, norm2_g[i]) * (1.0 + sc2) + sh2
        a, b = jnp.split(h @ w_ffn_in[i], 2, axis=-1)
        x = x + gt2 * ((jax.nn.silu(a) * b) @ w_ffn_out[i])
    return rmsnorm(x, final_g)
```

```python
import contextlib
import numpy as np
import concourse.bass as bass
import concourse.mybir as mybir
from concourse.bass_utils import run_bass_kernel_spmd

F32 = mybir.dt.float32
F32R = mybir.dt.float32r
AF = mybir.ActivationFunctionType
ALU = mybir.AluOpType
AX = mybir.AxisListType

D = 1024
SEQ = 4096
DEPTH = 4
TB = 512
NB = SEQ // TB
KT = D // 128
EPS = 1e-6
DFF = 2816
NFF = DFF // 128
D_IN = 7192
O_GQ, O_GK, O_GV, O_GLR, O_GOG = 0, 256, 512, 1024, 1040
O_S5 = 1552
O_DQKV, O_DBETA, O_DA, O_DOG = 2064, 3600, 3604, 3608
O_ZG = 4120
PI2 = 6.283185307179586


class Tk:
    __slots__ = ("ap", "w", "r", "psum")

    def __init__(self, ap, psum=False):
        self.ap = ap
        self.w = None
        self.r = {}
        self.psum = psum


class Eng:
    def __init__(self, name, obj, sem, sync_self):
        self.name = name
        self.obj = obj
        self.sem = sem
        self.count = 0
        self.seen = {}
        self.sync_self = sync_self
        self.lanes = []
        self.k = 0


class KB:
    def __init__(self, nc, es):
        self.nc = nc
        self.es = es
        self.sems = {}
        mk = lambda n: es.enter_context(nc.semaphore(n))
        self.pe = Eng("pe", nc.tensor, mk("s_pe"), False)
        self.act = Eng("act", nc.scalar, mk("s_act"), True)
        self.dve = Eng("dve", nc.vector, mk("s_dve"), True)
        self.pool = Eng("pool", nc.gpsimd, mk("s_pool"), True)
        self.sp = Eng("sp", nc.sync, mk("s_sp"), False)
        for q, nl in ((self.sp, 8), (self.pool, 6)):
            for i in range(nl):
                q.lanes.append([mk(f"s_{q.name}_l{i}"), 0])
        self.n_sb = 0

    def sb(self, shape, dtype=F32, name=None):
        self.n_sb += 1
        t = self.es.enter_context(self.nc.sbuf_tensor(name or f"sb{self.n_sb}", list(shape), dtype))
        return t

    def psum(self, name):
        return self.es.enter_context(self.nc.psum_tensor(name, [128, 512], F32))

    def _deps(self, reads, writes, eng=None):
        need = {}

        def add(tok):
            s, v = tok
            k = id(s)
            if k not in need or need[k][1] < v:
                need[k] = (s, v)
        for t in reads:
            if t.w is not None:
                add(t.w)
            if t.psum:
                for tok in t.r.values():
                    if eng is None or tok[0] is not eng.sem:
                        add(tok)
        for t in writes:
            if t.w is not None:
                add(t.w)
            for tok in t.r.values():
                add(tok)
        return need

    def _wait(self, eng, need):
        for k, (s, v) in need.items():
            if s is eng.sem and not eng.sync_self:
                continue
            if eng.seen.get(k, 0) < v:
                eng.obj.wait_ge(s, v)
                eng.seen[k] = v

    def _mark(self, tok, reads, writes):
        k = id(tok[0])
        for t in reads:
            t.r[k] = tok
        for t in writes:
            t.w = tok
            t.r = {}

    def op(self, eng, emit, reads=(), writes=()):
        self._wait(eng, self._deps(reads, writes, eng))
        ins = emit(eng.obj)
        eng.count += 1
        ins.then_inc(eng.sem, 1)
        eng.seen[id(eng.sem)] = max(eng.seen.get(id(eng.sem), 0), 0)
        self._mark((eng.sem, eng.count), reads, writes)

    def dma(self, q, out_ap, in_ap, reads=(), writes=()):
        lane = q.lanes[q.k % len(q.lanes)]
        q.k += 1
        need = self._deps(reads, writes)
        if lane[1] > 0:
            k = id(lane[0])
            need[k] = (lane[0], lane[1])
        self._wait(q, need)
        ins = q.obj.dma_start(out=out_ap, in_=in_ap)
        lane[1] += 16
        ins.then_inc(lane[0], 16)
        self._mark((lane[0], lane[1]), reads, writes)

    def barrier(self):
        engs = [self.pe, self.act, self.dve, self.pool, self.sp]
        toks = []
        for e in engs:
            if e.count > 0:
                toks.append((e.sem, e.count))
            for l in e.lanes:
                if l[1] > 0:
                    toks.append((l[0], l[1]))
        for e in engs:
            need = {id(s): (s, v) for s, v in toks if s is not e.sem}
            self._wait(e, need)

    def wait_all(self, eng, tiles):
        need = {}
        for t in tiles:
            if t.w is not None:
                s, v = t.w
                if id(s) not in need or need[id(s)][1] < v:
                    need[id(s)] = (s, v)
        self._wait(eng, need)


def r_(ap):
    return ap.bitcast(F32R)


def build_program(depth=DEPTH, use=("gla", "s5", "gdn", "ffn")):
    nc = bass.Bass("TRN2", target_bir_lowering=False)
    dr = lambda n, s, k="ExternalInput", dt=F32: nc.dram_tensor(n, list(s), dt, kind=k).ap()
    x_in = dr("x", [SEQ, D])
    c_in = dr("c", [128, KT])
    w_ada = dr("w_ada", [depth, D, 6 * D])
    b_ada = dr("b_ada", [depth, 128, 48])
    n1g = dr("norm1_g", [depth, 128, KT])
    n2g = dr("norm2_g", [depth, 128, KT])
    fng = dr("final_g", [128, KT])
    w_in = dr("w_in", [depth, D, D_IN])
    w_out = dr("w_out", [depth, D, D])
    w_br = {"gla": dr("w_branch_gla", [depth, 512, D]), "s5": dr("w_branch_s5", [depth, 512, D]),
            "gdn": dr("w_branch_gdn", [depth, 512, D])}
    w_fi = dr("w_ffn_in", [depth, D, 2 * DFF])
    w_fo = dr("w_ffn_out", [depth, DFF, D])
    PRM = declare_mixer_params(dr, depth)
    y_out = dr("out", [SEQ, D], "ExternalOutput")
    xres = dr("xres", [D, SEQ], "Internal")

    with contextlib.ExitStack() as es:
        K = KB(nc, es)
        pe, act, dve, pool, sp = K.pe, K.act, K.dve, K.pool, K.sp
        psall = es.enter_context(nc.psum_tensor("psall", [128, 4096], F32))
        PS = [Tk(psall[:, 512 * i:512 * (i + 1)], psum=True) for i in range(8)]
        ps_rr = [0]

        def ps_next():
            ps_rr[0] = (ps_rr[0] + 1) % 6
            return PS[ps_rr[0]]

        def mm(o_t, o_ap, l_t, l_ap, r_t, r_ap, start=True, stop=True):
            K.op(pe, lambda e: e.matmul(o_ap, l_ap, r_ap, start=start, stop=stop),
                 reads=[l_t, r_t], writes=[o_t])

        def tr(o_t, o_ap, i_t, i_ap):
            K.op(pe, lambda e: e.transpose(out=o_ap, in_=i_ap, identity=ident.ap[:]),
                 reads=[i_t, ident], writes=[o_t])

        def A(func, o_t, o_ap, i_t, i_ap, scale=1.0, bias=None, rd=(), accum=None, wr=()):
            kw = {}
            if bias is not None:
                kw["bias"] = bias
            if accum is not None:
                kw["accum_out"] = accum
            K.op(act, lambda e: e.activation(out=o_ap, in_=i_ap, func=func, scale=scale, **kw),
                 reads=[i_t] + list(rd), writes=[o_t] + list(wr))

        def TT(eng, o_t, o_ap, a_t, a_ap, b_t, b_ap, op):
            K.op(eng, lambda e: e.tensor_tensor(out=o_ap, in0=a_ap, in1=b_ap, op=op),
                 reads=[a_t, b_t], writes=[o_t])

        def TS(eng, o_t, o_ap, a_t, a_ap, s1, s2, op0, op1=None, rd=()):
            if op1 is None:
                K.op(eng, lambda e: e.tensor_scalar(out=o_ap, in0=a_ap, scalar1=s1, scalar2=None, op0=op0),
                     reads=[a_t] + list(rd), writes=[o_t])
            else:
                K.op(eng, lambda e: e.tensor_scalar(out=o_ap, in0=a_ap, scalar1=s1, scalar2=s2, op0=op0, op1=op1),
                     reads=[a_t] + list(rd), writes=[o_t])

        def STT(o_t, o_ap, a_t, a_ap, scalar, b_t, b_ap, op0, op1, rd=()):
            K.op(dve, lambda e: e.scalar_tensor_tensor(out=o_ap, in0=a_ap, scalar=scalar, in1=b_ap, op0=op0, op1=op1),
                 reads=[a_t, b_t] + list(rd), writes=[o_t])

        def CP(eng, o_t, o_ap, i_t, i_ap):
            if eng is act:
                A(AF.Copy, o_t, o_ap, i_t, i_ap)
            else:
                K.op(eng, lambda e: e.tensor_copy(out=o_ap, in_=i_ap), reads=[i_t], writes=[o_t])

        def MS(eng, t, ap, val):
            K.op(eng, lambda e: e.memset(ap, val), writes=[t])

        ident = Tk(K.sb([128, 128], F32, "ident"))
        ones_r = Tk(K.sb([128, 128], F32, "ones_r"))
        eps_c = Tk(K.sb([128, 1], F32, "eps_c"))
        ones_f = Tk(K.sb([128, 128], F32, "ones_f"))
        MS(dve, ones_f, ones_f.ap[:], 1.0)
        CP(dve, ones_r, r_(ones_r.ap[:]), ones_f, ones_f.ap[:])
        MS(dve, eps_c, eps_c.ap[:], EPS)
        MS(dve, ident, ident.ap[:], 1.0)
        K.op(pool, lambda e: e.affine_select(out=ident.ap[:], in_=ident.ap[:], pattern=[[1, 128]],
                                             compare_op=ALU.is_equal, fill=0.0, base=0,
                                             channel_multiplier=-1), reads=[ident], writes=[ident])

        HB = Tk(K.sb([128, KT, TB], F32, "HB"))
        SLAB = [Tk(K.sb([128, KT * 512], F32, f"slab{i}")) for i in range(2)]
        slab_rr = [0]
        ARENA_R, ARENA_F = 11264, 12032
        arenaR = K.sb([128, ARENA_R], F32, "arenaR")
        arenaF = K.sb([128, ARENA_F], F32, "arenaF")
        MG = Tk(arenaR[:, ARENA_R - KT * TB:ARENA_R].rearrange("p (k t) -> p k t", k=KT))
        XB = Tk(arenaF[:, 0:KT * TB].rearrange("p (k t) -> p k t", k=KT))
        XB_N = KT * TB
        XR = [Tk(None) for _ in range(NB)]

        class Carver:
            def __init__(self, keep_x=True):
                self.o = {True: 0, False: XB_N if keep_x else 0}

            def get(self, n, r=False, parts=128):
                o = self.o[r]
                a = (arenaR if r else arenaF)[0:parts, o:o + n]
                self.o[r] = o + n
                assert self.o[r] <= (ARENA_R if r else ARENA_F), (r, self.o[r])
                return Tk(a)

            def mark(self):
                return dict(self.o)

            def reset(self, m):
                self.o = dict(m)

        cond = Tk(K.sb([128, KT], F32, "cond"))
        modc = Tk(K.sb([128, 48], F32, "modc"))
        bada = Tk(K.sb([128, 48], F32, "bada"))
        g1c = Tk(K.sb([128, KT], F32, "g1c"))
        g2c = Tk(K.sb([128, KT], F32, "g2c"))
        gfc = Tk(K.sb([128, KT], F32, "gfc"))
        a1 = Tk(K.sb([128, KT], F32, "a1"))
        a2 = Tk(K.sb([128, KT], F32, "a2"))
        zero_c = Tk(K.sb([128, 1], F32, "zero_c"))
        MS(dve, zero_c, zero_c.ap[:], 0.0)

        def slab_next():
            slab_rr[0] ^= 1
            return SLAB[slab_rr[0]]

        def slab_load(src_ap, n):
            t = slab_next()
            kk = src_ap.shape[0] // 128
            dst = t.ap[:, 0:kk * n].rearrange("p (k c) -> p k c", k=kk)
            K.dma(pool, r_(dst), src_ap.rearrange("(k p) c -> p k c", p=128), writes=[t])
            return t, dst

        def xview(b):
            return xres[:, b * TB:(b + 1) * TB].rearrange("(k p) t -> p k t", p=128)

        def load_x(b):
            K.dma(sp, XB.ap[:], xview(b), reads=[XR[b]], writes=[XB])

        def store_x(b):
            K.dma(sp, xview(b), XB.ap[:], reads=[XB], writes=[XR[b]])

        def norm_block(acol_t, acol_ap, bcol_t, bcol_ap, out_r=True, OUT=None):
            cv = Carver()
            SQ = cv.get(KT * TB, r=True)
            RS = cv.get(TB)
            sq3 = SQ.ap.rearrange("p (k t) -> p k t", k=KT)
            A(AF.Square, SQ, r_(sq3), XB, XB.ap[:])
            p = ps_next()
            for kt in range(KT):
                mm(p, p.ap[:, :], ones_r, r_(ones_r.ap[:]), SQ, r_(sq3[:, kt, :]), start=(kt == 0), stop=(kt == KT - 1))
            A(AF.Sqrt, RS, RS.ap, p, p.ap[:, :], scale=1.0 / D, bias=eps_c.ap[:], rd=[eps_c])
            K.op(dve, lambda e: e.reciprocal(out=RS.ap, in_=RS.ap), reads=[RS], writes=[RS])
            TT(dve, SQ, r_(sq3), XB, XB.ap[:], RS, RS.ap.unsqueeze(1).to_broadcast([128, KT, TB]), ALU.mult)
            OUT = OUT or HB
            for kt in range(KT):
                o = OUT.ap[:, kt, :]
                A(AF.Identity, OUT, r_(o) if out_r else o, SQ, sq3[:, kt, :], scale=acol_ap[:, kt:kt + 1],
                  bias=(bcol_ap[:, kt:kt + 1] if bcol_ap is not None else zero_c.ap[:]),
                  rd=[acol_t] + ([bcol_t] if bcol_t is not None else [zero_c]))

        K.dma(sp, cond.ap[:], c_in, writes=[cond])
        A(AF.Silu, cond, cond.ap[:], cond, cond.ap[:])
        K.dma(sp, gfc.ap[:], fng, writes=[gfc])
        XT = [Tk(XB.ap[:, 2 * i:2 * i + 2, :].rearrange("p a b -> p (a b)")) for i in range(4)]
        STG = [Tk(arenaF[:, XB_N + 1024 * i:XB_N + 1024 * (i + 1)].rearrange("p (k t) -> p k t", k=KT)) for i in range(4)]
        for tt in range(SEQ // 128):
            xt, sg = XT[tt % 4], STG[tt % 4]
            K.dma(sp, xt.ap, x_in[tt * 128:(tt + 1) * 128, :], writes=[xt])
            for half in range(2):
                p = ps_next()
                for q in range(4):
                    kt = half * 4 + q
                    tr(p, p.ap[:, q * 128:(q + 1) * 128], xt, xt.ap[:, kt * 128:(kt + 1) * 128])
                CP(act if half == 0 else dve, sg, sg.ap[:, half * 4:half * 4 + 4, :], p,
                   p.ap[:, :].rearrange("p (q t) -> p q t", q=4))
            K.dma(sp, xres[:, tt * 128:(tt + 1) * 128].rearrange("(k p) t -> p k t", p=128), sg.ap,
                  reads=[sg], writes=[XR[tt // 4]])
        K.barrier()

        MX = make_mixers(locals())

        for l in range(depth):
            K.dma(sp, bada.ap[:], b_ada[l], writes=[bada])
            K.dma(sp, g1c.ap[:], n1g[l], writes=[g1c])
            K.dma(sp, g2c.ap[:], n2g[l], writes=[g2c])
            pm = ps_next()
            WA = [Tk(arenaF[:, 4096 * i:4096 * (i + 1)]) for i in range(2)]
            for jg in range(12):
                t = WA[jg % 2]
                dst = t.ap[:, :].rearrange("p (k c) -> p k c", k=KT)
                K.dma(sp, dst, w_ada[l][:, jg * 512:(jg + 1) * 512].rearrange("(k p) c -> p k c", p=128), writes=[t])
                for q in range(4):
                    j = jg * 4 + q
                    for kt in range(KT):
                        mm(pm, pm.ap[:, j:j + 1], t, dst[:, kt, q * 128:(q + 1) * 128], cond, cond.ap[:, kt:kt + 1],
                           start=(kt == 0), stop=(kt == KT - 1))
            TT(dve, modc, modc.ap[:], pm, pm.ap[:, 0:48], bada, bada.ap[:], ALU.add)
            STT(a1, a1.ap[:], modc, modc.ap[:, 8:16], 1.0, g1c, g1c.ap[:], ALU.add, ALU.mult)
            STT(a2, a2.ap[:], modc, modc.ap[:, 32:40], 1.0, g2c, g2c.ap[:], ALU.add, ALU.mult)
            K.barrier()
            MX.layer_setup(l)

            for b in range(NB):
                if any(u in use for u in ("gla", "s5", "gdn")):
                    load_x(b)
                    norm_block(a1, a1.ap, modc, modc.ap[:, 0:8])
                    K.barrier()
                    first = True
                    for name in ("gla", "s5", "gdn"):
                        if name not in use:
                            continue
                        YB = MX.run(name, l, b)
                        bi = ("gla", "s5", "gdn").index(name)
                        for oc in range(KT):
                            if oc % 4 == 0:
                                zt, zv = slab_load(w_in[l][:, O_ZG + bi * D + (oc // 4) * 512:O_ZG + bi * D + (oc // 4 + 1) * 512], 512)
                                wt, wv = slab_load(w_br[name][l][:, (oc // 4) * 512:(oc // 4 + 1) * 512], 512)
                            pp, pz = ps_next(), ps_next()
                            for kt in range(4):
                                mm(pp, pp.ap[:, :], wt, r_(wv[:, kt, (oc % 4) * 128:(oc % 4 + 1) * 128]), YB, r_(YB.ap[:, kt, :]),
                                   start=(kt == 0), stop=(kt == 3))
                            for kt in range(KT):
                                mm(pz, pz.ap[:, :], zt, r_(zv[:, kt, (oc % 4) * 128:(oc % 4 + 1) * 128]), HB, r_(HB.ap[:, kt, :]),
                                   start=(kt == 0), stop=(kt == KT - 1))
                            g = MX.GT[oc % 2]
                            A(AF.Sigmoid, g, g.ap, pz, pz.ap[:, :])
                            if first:
                                TT(dve, MG, r_(MG.ap[:, oc, :]), g, g.ap, pp, pp.ap[:, :], ALU.mult)
                            else:
                                TT(dve, g, g.ap, g, g.ap, pp, pp.ap[:, :], ALU.mult)
                                TT(dve, MG, r_(MG.ap[:, oc, :]), MG, MG.ap[:, oc, :], g, g.ap, ALU.add)
                        first = False
                        K.barrier()
                    load_x(b)
                    for oc in range(KT):
                        if oc % 4 == 0:
                            wt, wv = slab_load(w_out[l][:, (oc // 4) * 512:(oc // 4 + 1) * 512], 512)
                        po = ps_next()
                        for kt in range(KT):
                            mm(po, po.ap[:, :], wt, r_(wv[:, kt, (oc % 4) * 128:(oc % 4 + 1) * 128]), MG, r_(MG.ap[:, kt, :]),
                               start=(kt == 0), stop=(kt == KT - 1))
                        STT(XB, XB.ap[:, oc, :], po, po.ap[:, :], modc.ap[:, 16 + oc:17 + oc], XB, XB.ap[:, oc, :],
                            ALU.mult, ALU.add, rd=[modc])
                    if "ffn" not in use:
                        store_x(b)
                    K.barrier()
                elif "ffn" in use:
                    load_x(b)
                if "ffn" in use:
                    norm_block(a2, a2.ap, modc, modc.ap[:, 24:32])
                    K.barrier()
                    cv = Carver()
                    UF = [cv.get(TB, r=True) for _ in range(NFF)]
                    SA = [cv.get(TB) for _ in range(2)]
                    for jg in range(NFF // 2):
                        t = slab_next()
                        dst = t.ap[:, :].rearrange("p (k c) -> p k c", k=KT)
                        K.dma(pool, r_(dst[:, :, 0:256]), w_fi[l][:, jg * 256:(jg + 1) * 256].rearrange("(k p) c -> p k c", p=128), writes=[t])
                        K.dma(pool, r_(dst[:, :, 256:512]), w_fi[l][:, DFF + jg * 256:DFF + (jg + 1) * 256].rearrange("(k p) c -> p k c", p=128), writes=[t])
                        for q in range(2):
                            j = jg * 2 + q
                            pa, pb = ps_next(), ps_next()
                            for kt in range(KT):
                                mm(pa, pa.ap[:, :], t, r_(dst[:, kt, q * 128:(q + 1) * 128]), HB, r_(HB.ap[:, kt, :]),
                                   start=(kt == 0), stop=(kt == KT - 1))
                            for kt in range(KT):
                                mm(pb, pb.ap[:, :], t, r_(dst[:, kt, 256 + q * 128:256 + (q + 1) * 128]), HB, r_(HB.ap[:, kt, :]),
                                   start=(kt == 0), stop=(kt == KT - 1))
                            s = SA[j % 2]
                            A(AF.Silu, s, s.ap, pa, pa.ap[:, :])
                            TT(dve, UF[j], r_(UF[j].ap), s, s.ap, pb, pb.ap[:, :], ALU.mult)
                    for oc in range(KT):
                        wt, wv = slab_load(w_fo[l][:, oc * 128:(oc + 1) * 128], 128)
                        po = ps_next()
                        for j in range(NFF):
                            mm(po, po.ap[:, :], wt, r_(wv[:, j, :]), UF[j], r_(UF[j].ap), start=(j == 0), stop=(j == NFF - 1))
                        STT(XB, XB.ap[:, oc, :], po, po.ap[:, :], modc.ap[:, 40 + oc:41 + oc], XB, XB.ap[:, oc, :],
                            ALU.mult, ALU.add, rd=[modc])
                    store_x(b)
                    K.barrier()

        for b in range(NB):
            load_x(b)
            HN = Tk(arenaF[:, XB_N + 512:XB_N + 512 + KT * TB].rearrange("p (k t) -> p k t", k=KT))
            norm_block(gfc, gfc.ap, None, None, out_r=False, OUT=HN)
            K.barrier()
            OT = [Tk(arenaF[:, 2 * XB_N + 512 + 1024 * i:2 * XB_N + 512 + 1024 * (i + 1)]) for i in range(2)]
            for tt in range(4):
                ot = OT[tt % 2]
                for half in range(2):
                    p = ps_next()
                    for q in range(4):
                        kt = half * 4 + q
                        tr(p, p.ap[:, q * 128:(q + 1) * 128], HN, HN.ap[:, kt, tt * 128:(tt + 1) * 128])
                    CP(act if half == 0 else dve, ot, ot.ap[:, half * 512:(half + 1) * 512], p, p.ap[:, :])
                K.dma(sp, y_out[b * TB + tt * 128:b * TB + (tt + 1) * 128, :], ot.ap, reads=[ot])
            K.barrier()
        K.barrier()
    return nc


def declare_mixer_params(dr, depth=DEPTH):
    P = {}
    P["gla_w_lr"] = dr("gla_w_lr", [depth, 16, 256])
    P["gla_b_lr"] = dr("gla_b_lr", [depth, 1, 256])
    P["gla_ng4"] = dr("gla_ng4", [depth, 1, 512])
    P["s5_lam_col"] = dr("s5_lam_col", [depth, 128, 16, 3])
    P["s5_BR"] = dr("s5_BR", [depth, 128, 16, 128])
    P["s5_BI"] = dr("s5_BI", [depth, 128, 16, 128])
    P["s5_CR"] = dr("s5_CR", [depth, 128, 16, 64])
    P["s5_CI"] = dr("s5_CI", [depth, 128, 16, 64])
    P["s5_dcol"] = dr("s5_dcol", [depth, 128, 4])
    P["s5_w_glu"] = dr("s5_w_glu", [depth, 512, 1024])
    P["gdn_cw"] = dr("gdn_cw", [depth, 128, 12, 4])
    P["gdn_ab"] = dr("gdn_ab", [depth, 1, 8])
    P["gdn_ng4"] = dr("gdn_ng4", [depth, 1, 512])
    return P


def layout_mixer_params(inp):
    f = lambda a: np.ascontiguousarray(np.asarray(a, np.float32))
    out = {}
    out["gla_w_lr"] = f(inp["gla_w_lr"])
    out["gla_b_lr"] = f(inp["gla_b_lr"]).reshape(DEPTH, 1, 256)
    out["gla_ng4"] = np.ascontiguousarray(np.tile(f(inp["gla_norm_g"]), (1, 4)).reshape(DEPTH, 1, 512))
    L = DEPTH
    lam = np.stack([f(inp["s5_lambda_re"]), f(inp["s5_lambda_im"]),
                    np.repeat(f(inp["s5_log_dt"])[:, :, None], 64, axis=2)], axis=-1)
    lam = lam.reshape(L, 16, 2, 64, 3).transpose(0, 2, 3, 1, 4).reshape(L, 128, 16, 3)
    out["s5_lam_col"] = np.ascontiguousarray(lam)
    for nm, src in (("s5_BR", "s5_b_re"), ("s5_BI", "s5_b_im")):
        bsrc = f(inp[src]).reshape(L, 4, 4, 2, 64, 16)
        arr = np.zeros((L, 4, 2, 16, 4, 4, 2, 64), np.float32)
        for q in range(4):
            for g2 in range(2):
                arr[:, q, g2, :, :, q, g2, :] = bsrc[:, :, q, g2].transpose(0, 3, 1, 2)
        out[nm] = np.ascontiguousarray(arr.reshape(L, 128, 16, 128))
    for nm, src in (("s5_CR", "s5_c_re"), ("s5_CI", "s5_c_im")):
        csrc = f(inp[src]).reshape(L, 8, 2, 2, 16, 64)
        arr = np.zeros((L, 2, 64, 8, 2, 2, 2, 16), np.float32)
        for ql in range(2):
            for g2 in range(2):
                arr[:, g2, :, :, ql, ql, g2, :] = csrc[:, :, ql, g2].transpose(0, 3, 1, 2)
        out[nm] = np.ascontiguousarray(arr.reshape(L, 128, 16, 64))
    out["s5_dcol"] = np.ascontiguousarray(f(inp["s5_d"]).reshape(L, 4, 128).transpose(0, 2, 1))
    out["s5_w_glu"] = f(inp["s5_w_glu"])
    out["gdn_cw"] = np.ascontiguousarray(f(inp["gdn_conv_w"]).reshape(L, 4, 12, 128).transpose(0, 3, 2, 1))
    out["gdn_ab"] = np.ascontiguousarray(np.concatenate([f(inp["gdn_a_log"]), f(inp["gdn_dt_bias"])], axis=1).reshape(L, 1, 8))
    out["gdn_ng4"] = np.ascontiguousarray(np.tile(f(inp["gdn_norm_g"]), (1, 4)).reshape(L, 1, 512))
    return out


S5DBG = set()


class Mixers:
    def __init__(self, env):
        self.env = env
        K = env["K"]
        self.GT = [Tk(K.sb([128, TB], F32, f"gt{i}")[:, :]) for i in range(2)]
        MS, dve, pool = env["MS"], env["dve"], env["pool"]
        def tri(name, val, mode, chunked=True):
            t = Tk(K.sb([128, 128], F32, name))
            MS(dve, t, t.ap[:], val)
            if mode == "upper":
                pat, cm, cmp_ = [[1, 128]], -1, ALU.is_ge
            elif mode == "lower_strict":
                pat, cm, cmp_ = [[-1, 128]], 1, ALU.is_gt
            else:
                pat, cm, cmp_ = [[-1, 128]], 1, ALU.is_ge
            K.op(pool, lambda e: e.affine_select(out=t.ap[:], in_=t.ap[:], pattern=pat, compare_op=cmp_,
                                                 fill=0.0, base=0, channel_multiplier=cm), reads=[t], writes=[t])
            if chunked:
                if mode == "upper":
                    MS(dve, t, t.ap[0:64, 64:128], 0.0)
                else:
                    MS(dve, t, t.ap[64:128, 0:64], 0.0)
            return t
        self.TRI_incl = tri("tri_incl", -1.0 / 16.0, "upper")
        self.TRI_after = tri("tri_after", -1.0 / 16.0, "lower_strict")
        self.CMASK = tri("cmask", 1.0, "upper")
        self.UPPER01 = tri("upper01", 1.0, "upper", chunked=False)
        self.MASKGT = tri("maskgt", 1.0, "lower_strict", chunked=False)
        self.AFTER01 = tri("after01", 1.0, "lower_strict")
        self.NEGM = tri("negm", 30000.0, "lower_incl")
        env["TS"](dve, self.NEGM, self.NEGM.ap[:], self.NEGM, self.NEGM.ap[:], -30000.0, None, ALU.add)
        self.CH01 = Tk(K.sb([128, 2], F32, "ch01"))
        MS(dve, self.CH01, self.CH01.ap[:], 0.0)
        MS(dve, self.CH01, self.CH01.ap[0:64, 0:1], 1.0)
        MS(dve, self.CH01, self.CH01.ap[64:128, 1:2], 1.0)
        Q16 = Tk(K.sb([8, 128], F32, "q16"))
        MS(dve, Q16, Q16.ap[:], 1.0)
        K.op(pool, lambda e: e.affine_select(out=Q16.ap[:], in_=Q16.ap[:], pattern=[[1, 128]], compare_op=ALU.is_ge,
                                             fill=0.0, base=0, channel_multiplier=-16), reads=[Q16], writes=[Q16])
        K.op(pool, lambda e: e.affine_select(out=Q16.ap[:], in_=Q16.ap[:], pattern=[[-1, 128]], compare_op=ALU.is_ge,
                                             fill=0.0, base=15, channel_multiplier=16), reads=[Q16], writes=[Q16])
        self.BD16 = Tk(K.sb([128, 128], F32, "bd16"))
        pq = env["ps_next"]()
        env["mm"](pq, pq.ap[:, 0:128], Q16, Q16.ap[:], Q16, Q16.ap[:])
        env["CP"](dve, self.BD16, self.BD16.ap[:], pq, pq.ap[:, 0:128])
        self.gdn_S = Tk(K.sb([128, 4, 128], F32, "gdn_S"))
        self.HIST = Tk(K.sb([128, 12, 3], F32, "gdn_hist"))
        self.CW = Tk(K.sb([128, 12, 4], F32, "gdn_cw_sb"))
        self.AB = Tk(K.sb([128, 8], F32, "gdn_ab_sb"))
        self.NEGA = Tk(K.sb([128, 4], F32, "gdn_nega"))
        self.GNG2 = Tk(K.sb([128, 512], F32, "gng2"))
        self.ones1 = Tk(K.sb([1, 128], F32, "ones1"))
        MS(dve, self.ones1, self.ones1.ap[:], 1.0)
        self.gla_S = Tk(K.sb([64, 4, 128], F32, "gla_S"))
        self.WLR = Tk(K.sb([16, 256], F32, "wlr"))
        self.BLR = Tk(K.sb([1, 256], F32, "blr"))
        self.GNG = Tk(K.sb([128, 512], F32, "gng"))
        sb2 = lambda shp, n: Tk(K.sb(shp, F32, n)[:])
        self.BwRe = sb2([128, 16, 128], "s5bwre"); self.BwIm = sb2([128, 16, 128], "s5bwim")
        self.CRe = sb2([128, 16, 64], "s5cre"); self.CImN = sb2([128, 16, 64], "s5cimn")
        self.COS = sb2([128, 16, 64], "s5cos"); self.SIN = sb2([128, 16, 64], "s5sin")
        self.Rb = sb2([128, 16, 64], "s5rb"); self.rcol = sb2([128, 16], "s5r")
        self.CAR = sb2([128, 2, 16], "s5car"); self.dcol = sb2([128, 4], "s5d")

    def layer_setup(self, l):
        e = self.env
        K, sp, PRM = e["K"], e["sp"], e["PRM"]
        K.dma(sp, self.WLR.ap[:], PRM["gla_w_lr"][l], writes=[self.WLR])
        K.dma(sp, self.BLR.ap[:], PRM["gla_b_lr"][l], writes=[self.BLR])
        K.dma(sp, self.GNG.ap[:], PRM["gla_ng4"][l].partition_broadcast(128), writes=[self.GNG])
        e["MS"](e["dve"], self.gla_S, self.gla_S.ap[:], 0.0)
        K.barrier()
        if "s5" in e["use"]:
            self.s5_setup(l)
            K.barrier()
        if "gdn" in e["use"]:
            K.dma(sp, self.CW.ap[:], PRM["gdn_cw"][l], writes=[self.CW])
            K.dma(sp, self.AB.ap[:], PRM["gdn_ab"][l].partition_broadcast(128), writes=[self.AB])
            K.dma(sp, self.GNG2.ap[:], PRM["gdn_ng4"][l].partition_broadcast(128), writes=[self.GNG2])
            e["A"](AF.Exp, self.NEGA, self.NEGA.ap[:], self.AB, self.AB.ap[:, 0:4])
            e["TS"](e["dve"], self.NEGA, self.NEGA.ap[:], self.NEGA, self.NEGA.ap[:], -1.0, None, ALU.mult)
            e["MS"](e["dve"], self.gdn_S, self.gdn_S.ap[:], 0.0)
            e["MS"](e["dve"], self.HIST, self.HIST.ap[:], 0.0)
            K.barrier()

    def s5_setup(self, l):
        e = self.env
        K, pe, act, dve, pool, sp, PRM = e["K"], e["pe"], e["act"], e["dve"], e["pool"], e["sp"], e["PRM"]
        mm, A, TT, TS, STT, CP, MS, ps_next = e["mm"], e["A"], e["TT"], e["TS"], e["STT"], e["CP"], e["MS"], e["ps_next"]
        nc = e["nc"]
        cv = e["Carver"](keep_x=False)
        LAM = cv.get(48); lam = LAM.ap.rearrange("p (a c) -> p a c", c=3)
        lr, li, ldt = lam[:, :, 0], lam[:, :, 1], lam[:, :, 2]
        K.dma(sp, lam, PRM["s5_lam_col"][l], writes=[LAM])
        K.dma(sp, self.dcol.ap, PRM["s5_dcol"][l], writes=[self.dcol])
        BR = cv.get(1024); BI = cv.get(1024)
        br3 = BR.ap.rearrange("p (a c) -> p a c", a=8); bi3 = BI.ap.rearrange("p (a c) -> p a c", a=8)
        K.dma(sp, self.CRe.ap, PRM["s5_CR"][l], writes=[self.CRe])
        K.dma(sp, self.CImN.ap, PRM["s5_CI"][l], writes=[self.CImN])
        TS(dve, self.CImN, self.CImN.ap, self.CImN, self.CImN.ap, -1.0, None, ALU.mult)
        W = cv.get(16 * 12); w = W.ap.rearrange("p (n c) -> p n c", c=16)
        dt, th, u, kf, f, g1, c1, s1, wre, wim, t1, t2 = [w[:, i, :] for i in range(12)]
        KI = Tk(K.sb([128, 16], mybir.dt.int32, f"s5ki{l}")[:])
        if "nosetup" in S5DBG:
            for t in (self.COS, self.SIN, self.Rb, self.CAR, self.BwRe, self.BwIm, self.rcol):
                MS(dve, t, t.ap, 0.25)
            return
        A(AF.Exp, W, dt, LAM, ldt)
        TT(dve, W, t1, LAM, lr, W, dt, ALU.mult)
        A(AF.Exp, self.rcol, self.rcol.ap, W, t1)
        TT(dve, W, th, LAM, li, W, dt, ALU.mult)

        def frac_sin(out_ap, off):
            TS(dve, W, u, W, th, 1.0 / PI2, off, ALU.mult, ALU.add)
            CP(dve, KI, KI.ap, W, u)
            CP(dve, W, kf, KI, KI.ap)
            TT(dve, W, f, W, u, W, kf, ALU.subtract)
            TS(dve, W, g1, W, f, 0.5, None, ALU.is_gt)
            TT(dve, W, f, W, f, W, g1, ALU.subtract)
            TS(dve, W, g1, W, f, -0.5, None, ALU.is_lt)
            TT(dve, W, f, W, f, W, g1, ALU.add)
            A(AF.Sin, W, out_ap, W, f, scale=PI2)
        if "nosin" in S5DBG:
            MS(dve, W, s1, 0.6); MS(dve, W, c1, 0.8)
        else:
            frac_sin(s1, 0.0)
            frac_sin(c1, 0.25)
        cos3, sin3 = self.COS.ap, self.SIN.ap
        CP(dve, self.COS, cos3[:, :, 0], W, c1)
        CP(dve, self.SIN, sin3[:, :, 0], W, s1)
        TB1 = cv.get(512); TB2 = cv.get(512); TB3 = cv.get(512); TB4 = cv.get(512)
        n = 1 if "nodbl" not in S5DBG else 64
        while n < 64:
            v16 = lambda t: t.ap.rearrange("p (a c) -> p a c", a=16)[:, :, 0:n]
            x1, x2, cn, sn = v16(TB1), v16(TB2), v16(TB3), v16(TB4)
            CP(dve, TB3, cn, self.COS, cos3[:, :, n - 1:n].to_broadcast([128, 16, n]))
            CP(dve, TB4, sn, self.SIN, sin3[:, :, n - 1:n].to_broadcast([128, 16, n]))
            TT(dve, TB1, x1, self.COS, cos3[:, :, 0:n], TB3, cn, ALU.mult)
            TT(dve, TB2, x2, self.SIN, sin3[:, :, 0:n], TB4, sn, ALU.mult)
            TT(dve, TB1, x1, TB1, x1, TB2, x2, ALU.subtract)
            TT(dve, TB2, x2, self.SIN, sin3[:, :, 0:n], TB3, cn, ALU.mult)
            TT(dve, TB3, cn, self.COS, cos3[:, :, 0:n], TB4, sn, ALU.mult)
            TT(dve, TB2, x2, TB2, x2, TB3, cn, ALU.add)
            CP(dve, self.COS, cos3[:, :, n:2 * n], TB1, x1)
            CP(dve, self.SIN, sin3[:, :, n:2 * n], TB2, x2)
            n *= 2
        MS(dve, self.Rb, self.Rb.ap, 0.0)
        CP(dve, self.Rb, self.Rb.ap[:, :, 1:64], self.rcol, self.rcol.ap.unsqueeze(2).to_broadcast([128, 16, 63]))
        MS(dve, self.CAR, self.CAR.ap, 0.0)
        TT(dve, W, t1, self.rcol, self.rcol.ap, W, c1, ALU.mult)
        TS(dve, W, t1, W, t1, -1.0, None, ALU.add)
        TT(dve, W, t2, self.rcol, self.rcol.ap, W, s1, ALU.mult)
        TT(dve, W, u, LAM, lr, LAM, lr, ALU.mult)
        TT(dve, W, kf, LAM, li, LAM, li, ALU.mult)
        TT(dve, W, u, W, u, W, kf, ALU.add)
        K.op(dve, lambda en: en.reciprocal(out=u, in_=u), reads=[W], writes=[W])
        TT(dve, W, wre, W, t1, LAM, lr, ALU.mult)
        TT(dve, W, kf, W, t2, LAM, li, ALU.mult)
        TT(dve, W, wre, W, wre, W, kf, ALU.add)
        TT(dve, W, wre, W, wre, W, u, ALU.mult)
        TT(dve, W, wim, W, t2, LAM, lr, ALU.mult)
        TT(dve, W, kf, W, t1, LAM, li, ALU.mult)
        TT(dve, W, wim, W, wim, W, kf, ALU.subtract)
        TT(dve, W, wim, W, wim, W, u, ALU.mult)
        DG = [cv.get(128) for _ in range(2)]
        ident, ones_f = e["ident"], e["ones_f"]
        psall, PS = e["psall"], e["PS"]
        WR = cv.get(1024); WI = cv.get(1024); T3 = cv.get(1024)
        for hh in range(2):
            K.dma(sp, br3, PRM["s5_BR"][l][:, 8 * hh:8 * hh + 8, :], writes=[BR])
            K.dma(sp, bi3, PRM["s5_BI"][l][:, 8 * hh:8 * hh + 8, :], writes=[BI])
            for which, (src, base, pts) in enumerate(((wre, 0, [PS[0], PS[1]]), (wim, 1024, [PS[2], PS[3]]))):
                for j in range(8):
                    pi = 8 * hh + j
                    dg = DG[pi % 2]
                    TS(dve, dg, dg.ap, ident, ident.ap[:], src[:, pi:pi + 1], None, ALU.mult, rd=[W])
                    mm(pts[j // 4], psall[:, base + j * 128:base + (j + 1) * 128], ones_f, ones_f.ap[:], dg, dg.ap)
            K.op(act, lambda en: en.activation(out=WR.ap, in_=psall[:, 0:1024], func=AF.Copy), reads=[PS[0], PS[1]], writes=[WR])
            K.op(act, lambda en: en.activation(out=WI.ap, in_=psall[:, 1024:2048], func=AF.Copy), reads=[PS[2], PS[3]], writes=[WI])
            bwre = self.BwRe.ap[:, 8 * hh:8 * hh + 8, :].rearrange("p a c -> p (a c)")
            bwim = self.BwIm.ap[:, 8 * hh:8 * hh + 8, :].rearrange("p a c -> p (a c)")
            TT(dve, self.BwRe, bwre, WR, WR.ap, BR, BR.ap, ALU.mult)
            TT(dve, T3, T3.ap, WI, WI.ap, BI, BI.ap, ALU.mult)
            TT(dve, self.BwRe, bwre, self.BwRe, bwre, T3, T3.ap, ALU.subtract)
            TT(dve, self.BwIm, bwim, WR, WR.ap, BI, BI.ap, ALU.mult)
            TT(dve, T3, T3.ap, WI, WI.ap, BR, BR.ap, ALU.mult)
            TT(dve, self.BwIm, bwim, self.BwIm, bwim, T3, T3.ap, ALU.add)

    def run(self, name, l, b):
        return getattr(self, "run_" + name)(l, b)

    def run_gla(self, l, b):
        e = self.env
        K, pe, act, dve, pool, sp = e["K"], e["pe"], e["act"], e["dve"], e["pool"], e["sp"]
        mm, tr, A, TT, TS, STT, CP, MS = e["mm"], e["tr"], e["A"], e["TT"], e["TS"], e["STT"], e["CP"], e["MS"]
        ps_next, slab_load, HB, w_in = e["ps_next"], e["slab_load"], e["HB"], e["w_in"]
        cv = e["Carver"](keep_x=False)
        QE = cv.get(4 * TB, parts=64); KE = cv.get(4 * TB, parts=64)
        qe = QE.ap.rearrange("p (h t) -> p h t", h=4); ke = KE.ap.rearrange("p (h t) -> p h t", h=4)
        KTM = [cv.get(256) for _ in range(4)]
        LT = [cv.get(256) for _ in range(4)]
        VS = [cv.get(512) for _ in range(2)]
        SG = [cv.get(512) for _ in range(2)]
        YT = [cv.get(512) for _ in range(2)]
        EB = cv.get(512, parts=64); ENB = cv.get(512, parts=64)
        EA = cv.get(256)
        SCM = [cv.get(128) for _ in range(2)]
        DEC = cv.get(32, parts=64)
        dec = DEC.ap.rearrange("p (h c) -> p h c", h=4)
        SSQ = cv.get(8)
        LRT = cv.get(512, parts=16)
        JK = cv.get(128)
        YB = cv.get(4 * TB, r=True)
        yb = YB.ap.rearrange("p (h t) -> p h t", h=4)
        S = self.gla_S
        wt, wv = slab_load(w_in[l][:, O_GQ:O_GQ + 512], 512)
        for h in range(4):
            for which, dst, sc in ((0, qe, 0.125), (1, ke, 1.0)):
                p = ps_next()
                for kt in range(KT):
                    mm(p, p.ap[0:64, :], wt, r_(wv[:, kt, which * 256 + h * 64:which * 256 + (h + 1) * 64]), HB, r_(HB.ap[:, kt, :]),
                       start=(kt == 0), stop=(kt == KT - 1))
                A(AF.Copy, QE if which == 0 else KE, dst[:, h, :], p, p.ap[0:64, :], scale=sc)
        for tt in range(4):
            p = ps_next()
            for kt in range(KT):
                mm(p, p.ap[:, 0:256], HB, r_(HB.ap[:, kt, tt * 128:(tt + 1) * 128]), wt, r_(wv[:, kt, 256:512]),
                   start=(kt == 0), stop=(kt == KT - 1))
            CP(dve, KTM[tt], KTM[tt].ap, p, p.ap[:, 0:256])
        wt2, wv2 = slab_load(w_in[l][:, O_GLR:O_GLR + 16], 16)
        p = ps_next()
        for kt in range(KT):
            mm(p, p.ap[0:16, :], wt2, r_(wv2[:, kt, :]), HB, r_(HB.ap[:, kt, :]), start=(kt == 0), stop=(kt == KT - 1))
        CP(act, LRT, LRT.ap, p, p.ap[0:16, :])
        for tt in range(4):
            p = ps_next()
            mm(p, p.ap[:, 0:256], LRT, LRT.ap[:, tt * 128:(tt + 1) * 128], self.WLR, self.WLR.ap[:], start=True, stop=False)
            mm(p, p.ap[:, 0:256], self.ones1, self.ones1.ap[:], self.BLR, self.BLR.ap[:], start=False, stop=True)
            A(AF.Exp, LT[tt], LT[tt].ap, p, p.ap[:, 0:256], scale=-1.0)
            A(AF.Ln, LT[tt], LT[tt].ap, LT[tt], LT[tt].ap, bias=1.0)
            p2 = ps_next()
            mm(p2, p2.ap[:, 0:256], self.TRI_after, self.TRI_after.ap[:], LT[tt], LT[tt].ap)
            A(AF.Exp, EA, EA.ap, p2, p2.ap[:, 0:256])
            TT(dve, KTM[tt], KTM[tt].ap, KTM[tt], KTM[tt].ap, EA, EA.ap, ALU.mult)
        for h in range(4):
            p = ps_next()
            for tt in range(4):
                mm(p, p.ap[0:64, tt * 128:(tt + 1) * 128], LT[tt], LT[tt].ap[:, h * 64:(h + 1) * 64], self.TRI_incl, self.TRI_incl.ap[:])
            A(AF.Exp, EB, EB.ap, p, p.ap[0:64, :])
            A(AF.Exp, ENB, ENB.ap, p, p.ap[0:64, :], scale=-1.0)
            TT(dve, QE, qe[:, h, :], QE, qe[:, h, :], EB, EB.ap, ALU.mult)
            TT(dve, KE, ke[:, h, :], KE, ke[:, h, :], ENB, ENB.ap, ALU.mult)
            CP(dve, DEC, dec[:, h, :], EB, EB.ap.rearrange("p (c t) -> p c t", t=64)[:, :, 63])
        wtv, wvv = slab_load(w_in[l][:, O_GV:O_GV + 512], 512)
        wto, wvo = slab_load(w_in[l][:, O_GOG:O_GOG + 512], 512)
        for tt in range(4):
            vs, sg, yt = VS[tt % 2], SG[tt % 2], YT[tt % 2]
            ts_ = slice(tt * 128, (tt + 1) * 128)
            p = ps_next()
            for kt in range(KT):
                mm(p, p.ap[:, :], HB, r_(HB.ap[:, kt, ts_]), wtv, r_(wvv[:, kt, :]), start=(kt == 0), stop=(kt == KT - 1))
            CP(act, vs, vs.ap, p, p.ap[:, :])
            p = ps_next()
            for kt in range(KT):
                mm(p, p.ap[:, :], HB, r_(HB.ap[:, kt, ts_]), wto, r_(wvo[:, kt, :]), start=(kt == 0), stop=(kt == KT - 1))
            A(AF.Silu, sg, sg.ap, p, p.ap[:, :])
            TT(dve, sg, sg.ap, sg, sg.ap, self.GNG, self.GNG.ap[:], ALU.mult)
            po = e["PS"][6 + tt % 2]
            for h in range(4):
                hs = slice(h * 128, (h + 1) * 128)
                pa = ps_next()
                mm(pa, pa.ap[:, 0:128], KE, ke[:, h, ts_], QE, qe[:, h, ts_])
                scm = SCM[h % 2]
                TT(dve, scm, scm.ap, pa, pa.ap[:, 0:128], self.CMASK, self.CMASK.ap[:], ALU.mult)
                mm(po, po.ap[:, hs], scm, scm.ap, vs, vs.ap[:, hs], start=True, stop=False)
                for c in range(2):
                    cs = slice(tt * 128 + c * 64, tt * 128 + (c + 1) * 64)
                    rows = slice(c * 64, (c + 1) * 64)
                    mm(po, po.ap[rows, hs], QE, qe[:, h, cs], S, S.ap[:, h, :], start=False, stop=True)
                    pk = ps_next()
                    mm(pk, pk.ap[0:64, 0:128], KTM[tt], KTM[tt].ap[rows, h * 64:(h + 1) * 64], vs, vs.ap[rows, hs])
                    ch = tt * 2 + c
                    STT(S, S.ap[:, h, :], S, S.ap[:, h, :], dec[:, h, ch:ch + 1], pk, pk.ap[0:64, 0:128], ALU.mult, ALU.add, rd=[DEC])
            for h in range(4):
                hs = slice(h * 128, (h + 1) * 128)
                A(AF.Square, JK, JK.ap, po, po.ap[:, hs], accum=SSQ.ap[:, h:h + 1], wr=[SSQ])
            A(AF.Sqrt, SSQ, SSQ.ap[:, 4:8], SSQ, SSQ.ap[:, 0:4], scale=1.0 / 128.0, bias=e["eps_c"].ap[:], rd=[e["eps_c"]])
            K.op(dve, lambda en: en.reciprocal(out=SSQ.ap[:, 4:8], in_=SSQ.ap[:, 4:8]), reads=[SSQ], writes=[SSQ])
            for h in range(4):
                hs = slice(h * 128, (h + 1) * 128)
                STT(yt, yt.ap[:, hs], po, po.ap[:, hs], SSQ.ap[:, 4 + h:5 + h], sg, sg.ap[:, hs], ALU.mult, ALU.mult, rd=[SSQ])
            pt = ps_next()
            for h in range(4):
                tr(pt, pt.ap[:, h * 128:(h + 1) * 128], yt, yt.ap[:, h * 128:(h + 1) * 128])
            CP(act, YB, r_(yb[:, :, ts_]), pt, pt.ap[:, :].rearrange("p (h t) -> p h t", h=4))
        return Tk(yb) if False else _view(YB, yb)


def _s5_run(self, l, b):
    e = self.env
    K, pe, act, dve, pool, sp = e["K"], e["pe"], e["act"], e["dve"], e["pool"], e["sp"]
    mm, A, TT, TS, STT, CP, MS = e["mm"], e["A"], e["TT"], e["TS"], e["STT"], e["CP"], e["MS"]
    ps_next, slab_load, HB, w_in, PS, psall, PRM = e["ps_next"], e["slab_load"], e["HB"], e["w_in"], e["PS"], e["psall"], e["PRM"]
    cv = e["Carver"](keep_x=False)
    UT = cv.get(4 * TB); ut = UT.ap.rearrange("p (c t) -> p c t", c=4)
    YS = cv.get(4 * TB); ys = YS.ap.rearrange("p (c t) -> p c t", c=4)
    GIN = cv.get(2048); G = cv.get(2048); H = cv.get(2048); TMP = cv.get(1024)
    ginre, ginim = GIN.ap[:, 0:1024], GIN.ap[:, 1024:2048]
    gre, gim = G.ap[:, 0:1024], G.ap[:, 1024:2048]
    hre, him = H.ap[:, 0:1024], H.ap[:, 1024:2048]
    v3 = lambda ap: ap.rearrange("p (a t) -> p a t", a=16)
    cosf = self.COS.ap.rearrange("p a t -> p (a t)"); sinf = self.SIN.ap.rearrange("p a t -> p (a t)")
    rbf = self.Rb.ap.rearrange("p a t -> p (a t)")
    YG = cv.get(4 * TB, r=True); yg = YG.ap.rearrange("p (c t) -> p c t", c=4)
    YB = cv.get(4 * TB, r=True); yb = YB.ap.rearrange("p (c t) -> p c t", c=4)
    wt, wv = slab_load(w_in[l][:, O_S5:O_S5 + 512], 512)
    for ct in range(4):
        p = ps_next()
        for kt in range(KT):
            mm(p, p.ap[:, :], wt, r_(wv[:, kt, ct * 128:(ct + 1) * 128]), HB, r_(HB.ap[:, kt, :]), start=(kt == 0), stop=(kt == KT - 1))
        CP(act if ct % 2 == 0 else dve, UT, ut[:, ct, :], p, p.ap[:, :])
    K.barrier()
    bure, buim = psall[:, 0:1024], psall[:, 1024:2048]
    PRE, PIM = [PS[0], PS[1]], [PS[2], PS[3]]
    for sc in range(TB // 64 if "noloop" not in S5DBG else 0):
        cs = slice(sc * 64, (sc + 1) * 64)
        for pi in range(16):
            a = pi // 4
            mm(PS[pi // 8], bure[:, pi * 64:(pi + 1) * 64], self.BwRe, self.BwRe.ap[:, pi, :], UT, ut[:, a, cs])
            mm(PS[2 + pi // 8], buim[:, pi * 64:(pi + 1) * 64], self.BwIm, self.BwIm.ap[:, pi, :], UT, ut[:, a, cs])

        if "st1" in S5DBG:
            continue
        def tt2(o_t, o_ap, a_ts, a_ap, b_t, b_ap, op):
            K.op(dve, lambda en: en.tensor_tensor(out=o_ap, in0=a_ap, in1=b_ap, op=op), reads=list(a_ts) + [b_t], writes=[o_t])
        tt2(GIN, ginre, PRE, bure, self.COS, cosf, ALU.mult)
        tt2(TMP, TMP.ap, PIM, buim, self.SIN, sinf, ALU.mult)
        TT(dve, GIN, ginre, GIN, ginre, TMP, TMP.ap, ALU.add)
        tt2(GIN, ginim, PIM, buim, self.COS, cosf, ALU.mult)
        tt2(TMP, TMP.ap, PRE, bure, self.SIN, sinf, ALU.mult)
        TT(dve, GIN, ginim, GIN, ginim, TMP, TMP.ap, ALU.subtract)
        if "st2" in S5DBG:
            continue
        for ri, gap in ((0, ginre), (1, ginim)):
            TT(dve, TMP, TMP.ap[:, 0:16], self.CAR, self.CAR.ap[:, ri, :], self.rcol, self.rcol.ap, ALU.mult)
            TT(dve, GIN, v3(gap)[:, :, 0], GIN, v3(gap)[:, :, 0], TMP, TMP.ap[:, 0:16], ALU.add)
        if "st3" in S5DBG:
            continue
        for gap, oap in ((ginre, gre), (ginim, gim)):
            K.op(dve, lambda en, gap=gap, oap=oap: en.tensor_tensor_scan(out=oap, data0=rbf, data1=gap, initial=0.0, op0=ALU.mult, op1=ALU.add),
                 reads=[GIN, self.Rb], writes=[G])
        if "st4" in S5DBG:
            continue
        TT(dve, H, hre, G, gre, self.COS, cosf, ALU.mult)
        TT(dve, TMP, TMP.ap, G, gim, self.SIN, sinf, ALU.mult)
        TT(dve, H, hre, H, hre, TMP, TMP.ap, ALU.subtract)
        TT(dve, H, him, G, gim, self.COS, cosf, ALU.mult)
        TT(dve, TMP, TMP.ap, G, gre, self.SIN, sinf, ALU.mult)
        TT(dve, H, him, H, him, TMP, TMP.ap, ALU.add)
        CP(dve, self.CAR, self.CAR.ap[:, 0, :], H, v3(hre)[:, :, 63])
        CP(dve, self.CAR, self.CAR.ap[:, 1, :], H, v3(him)[:, :, 63])
        if "st5" in S5DBG:
            continue
        py = PS[4 + sc % 2]
        for ct in range(4):
            for half in range(2):
                o = py.ap[64 * half:64 * half + 64, ct * 64:(ct + 1) * 64]
                for ql in range(2):
                    pi = 4 * ct + 2 * half + ql
                    mm(py, o, self.CRe, self.CRe.ap[:, pi, :], H, v3(hre)[:, pi, :], start=(ql == 0), stop=False)
                    mm(py, o, self.CImN, self.CImN.ap[:, pi, :], H, v3(him)[:, pi, :], start=False, stop=(ql == 1))
        if "st6" in S5DBG:
            continue
        for ct in range(4):
            STT(YS, ys[:, ct, cs], UT, ut[:, ct, cs], self.dcol.ap[:, ct:ct + 1], py, py.ap[:, ct * 64:(ct + 1) * 64],
                ALU.mult, ALU.add, rd=[self.dcol])
    K.barrier()
    T1 = Tk(GIN.ap)
    A(AF.Square, T1, T1.ap, YS, YS.ap)
    TS(dve, T1, T1.ap, T1, T1.ap, 0.044715, 1.0, ALU.mult, ALU.add)
    TT(dve, T1, T1.ap, T1, T1.ap, YS, YS.ap, ALU.mult)
    A(AF.Sigmoid, T1, T1.ap, T1, T1.ap, scale=1.5957691216057308)
    TT(dve, YG, r_(YG.ap), T1, T1.ap, YS, YS.ap, ALU.mult)
    wa, wva = slab_load(PRM["s5_w_glu"][l][:, 0:512], 512)
    wb, wvb = slab_load(PRM["s5_w_glu"][l][:, 512:1024], 512)
    for j in range(4):
        pa, pb = ps_next(), ps_next()
        for kt in range(4):
            mm(pa, pa.ap[:, :], wa, r_(wva[:, kt, j * 128:(j + 1) * 128]), YG, r_(yg[:, kt, :]), start=(kt == 0), stop=(kt == 3))
        for kt in range(4):
            mm(pb, pb.ap[:, :], wb, r_(wvb[:, kt, j * 128:(j + 1) * 128]), YG, r_(yg[:, kt, :]), start=(kt == 0), stop=(kt == 3))
        g = self.GT[j % 2]
        A(AF.Sigmoid, g, g.ap, pb, pb.ap[:, :])
        TT(dve, YB, r_(yb[:, j, :]), g, g.ap, pa, pa.ap[:, :], ALU.mult)
    return _view(YB, yb)


Mixers.run_s5 = _s5_run


def _gdn_run(self, l, b):
    e = self.env
    K, pe, act, dve, pool, sp = e["K"], e["pe"], e["act"], e["dve"], e["pool"], e["sp"]
    mm, tr, A, TT, TS, STT, CP, MS = e["mm"], e["tr"], e["A"], e["TT"], e["TS"], e["STT"], e["CP"], e["MS"]
    ps_next, slab_load, HB, w_in, PS = e["ps_next"], e["slab_load"], e["HB"], e["w_in"], e["PS"]
    ident, ones_f, eps_c = e["ident"], e["ones_f"], e["eps_c"]
    cv = e["Carver"](keep_x=False)
    QK = cv.get(8 * TB); qk = QK.ap.rearrange("p (c t) -> p c t", c=8)
    VTM = cv.get(4 * 512); vtm = VTM.ap.rearrange("p (t c) -> p t c", t=4)
    SM = [cv.get(64) for _ in range(4)]
    YB = cv.get(4 * TB, r=True); yb = YB.ap.rearrange("p (h t) -> p h t", h=4)
    mk = cv.mark()
    XC = cv.get(520); VB = cv.get(512); SQ = cv.get(512); RSs = cv.get(512); GBC = cv.get(128)
    S, hist, cw = self.gdn_S, self.HIST.ap, self.CW.ap
    for g3 in range(3):
        wt, wv = slab_load(w_in[l][:, O_DQKV + g3 * 512:O_DQKV + (g3 + 1) * 512], 512)
        for c4 in range(4):
            ct = g3 * 4 + c4
            p = ps_next()
            for kt in range(KT):
                mm(p, p.ap[:, :], wt, r_(wv[:, kt, c4 * 128:(c4 + 1) * 128]), HB, r_(HB.ap[:, kt, :]), start=(kt == 0), stop=(kt == KT - 1))
            CP(act, XC, XC.ap[:, 3:515], p, p.ap[:, :])
            CP(dve, XC, XC.ap[:, 0:3], self.HIST, hist[:, ct, :])
            dt_, dap = (QK, qk[:, ct, :]) if g3 < 2 else (VB, VB.ap)
            TS(dve, dt_, dap, XC, XC.ap[:, 3:515], cw[:, ct, 3:4], None, ALU.mult, rd=[self.CW])
            for j in (2, 1, 0):
                STT(dt_, dap, XC, XC.ap[:, j:j + 512], cw[:, ct, j:j + 1], dt_, dap, ALU.mult, ALU.add, rd=[self.CW])
            CP(dve, self.HIST, hist[:, ct, :], XC, XC.ap[:, 512:515])
            A(AF.Silu, dt_, dap, dt_, dap)
            if g3 == 2:
                pt = ps_next()
                for tt in range(4):
                    tr(pt, pt.ap[:, tt * 128:(tt + 1) * 128], VB, VB.ap[:, tt * 128:(tt + 1) * 128])
                CP(act, VTM, vtm[:, :, c4 * 128:(c4 + 1) * 128], pt, pt.ap[:, :].rearrange("p (t c) -> p t c", t=4))
    if "g1" in S5DBG:
        return _view(YB, yb)
    for ct in range(8):
        A(AF.Square, SQ, SQ.ap, QK, qk[:, ct, :])
        p = ps_next()
        mm(p, p.ap[:, :], ones_f, ones_f.ap[:], SQ, SQ.ap)
        A(AF.Sqrt, RSs, RSs.ap, p, p.ap[:, :], bias=eps_c.ap[:], rd=[eps_c])
        K.op(dve, lambda en: en.reciprocal(out=RSs.ap, in_=RSs.ap), reads=[RSs], writes=[RSs])
        if ct < 4:
            STT(QK, qk[:, ct, :], QK, qk[:, ct, :], 128.0 ** -0.5, RSs, RSs.ap, ALU.mult, ALU.mult)
        else:
            TT(dve, QK, qk[:, ct, :], QK, qk[:, ct, :], RSs, RSs.ap, ALU.mult)
    if "g2" in S5DBG:
        return _view(YB, yb)
    wt2, wv2 = slab_load(w_in[l][:, O_DBETA:O_DBETA + 8], 8)
    for tt in range(4):
        sm = SM[tt]
        p = ps_next()
        for kt in range(KT):
            mm(p, p.ap[:, 0:8], HB, r_(HB.ap[:, kt, tt * 128:(tt + 1) * 128]), wt2, r_(wv2[:, kt, :]), start=(kt == 0), stop=(kt == KT - 1))
        A(AF.Sigmoid, sm, sm.ap[:, 0:4], p, p.ap[:, 0:4])
        TT(dve, sm, sm.ap[:, 52:56], p, p.ap[:, 4:8], self.AB, self.AB.ap[:, 4:8], ALU.add)
        A(AF.Exp, sm, sm.ap[:, 52:56], sm, sm.ap[:, 52:56])
        A(AF.Ln, sm, sm.ap[:, 52:56], sm, sm.ap[:, 52:56], bias=1.0)
        TT(dve, sm, sm.ap[:, 4:8], sm, sm.ap[:, 52:56], self.NEGA, self.NEGA.ap[:], ALU.mult)
        p2 = ps_next()
        mm(p2, p2.ap[:, 0:4], self.AFTER01, self.AFTER01.ap[:], sm, sm.ap[:, 4:8])
        mm(p2, p2.ap[:, 4:8], self.CMASK, self.CMASK.ap[:], sm, sm.ap[:, 4:8])
        A(AF.Exp, sm, sm.ap[:, 8:16], p2, p2.ap[:, 0:8])
        TT(dve, sm, sm.ap[:, 16:20], sm, sm.ap[:, 0:4], sm, sm.ap[:, 12:16], ALU.mult)
        p3 = ps_next()
        for h in range(4):
            TS(dve, GBC, GBC.ap, ones_f, ones_f.ap[:], sm.ap[:, 4 + h:5 + h], None, ALU.mult, rd=[sm])
            mm(p3, p3.ap[:, 2 * h:2 * h + 2], GBC, GBC.ap, self.CH01, self.CH01.ap[:])
        A(AF.Exp, sm, sm.ap[:, 20:28], p3, p3.ap[:, 0:8])
    K.barrier()
    if "g3" in S5DBG:
        return _view(YB, yb)
    cv.reset(mk)
    KTMt = cv.get(512); O = cv.get(512); YT = cv.get(512); SGt = cv.get(512); JK = cv.get(128); SSQ = cv.get(8)
    s1, s2, s3, s4, s5, s6, s7, s8, s9, s10, s11 = [cv.get(128) for _ in range(11)]
    r1, r2, r3 = [cv.get(256) for _ in range(3)]
    wto, wvo = slab_load(w_in[l][:, O_DOG:O_DOG + 512], 512)
    f1 = lambda t: t.ap
    pcopy = [0]

    def evac(dst, src_t, src_ap):
        pcopy[0] += 1
        CP(act if pcopy[0] % 2 else dve, dst, dst.ap[:, 0:src_ap.shape[1]] if False else dst.ap, src_t, src_ap)
    for tt in range(4):
        sm = SM[tt]
        cs = slice(tt * 128, (tt + 1) * 128)
        pt = ps_next()
        for h in range(4):
            tr(pt, pt.ap[:, h * 128:(h + 1) * 128], QK, qk[:, 4 + h, cs])
        CP(act, KTMt, KTMt.ap, pt, pt.ap[:, :])
        for h in range(4):
            hs = slice(h * 128, (h + 1) * 128)
            beta, gcol, egl, egam, be = (sm.ap[:, h:h + 1], sm.ap[:, 4 + h:5 + h], sm.ap[:, 8 + h:9 + h],
                                         sm.ap[:, 12 + h:13 + h], sm.ap[:, 16 + h:17 + h])
            UG, EL, ATT, Lm, Ld, Lo, Md, ATT_T, Ld2, Y0, Y1 = s1, s2, s3, s4, s5, s6, s7, s8, s9, s10, s11
            TS(dve, UG, UG.ap, self.UPPER01, self.UPPER01.ap[:], gcol, None, ALU.mult, rd=[sm])
            pD = ps_next()
            mm(pD, pD.ap[:, 0:128], UG, UG.ap, self.MASKGT, self.MASKGT.ap[:], start=True, stop=False)
            mm(pD, pD.ap[:, 0:128], ident, ident.ap[:], self.NEGM, self.NEGM.ap[:], start=False, stop=True)
            A(AF.Exp, EL, EL.ap, pD, pD.ap[:, 0:128])
            if "h1" in S5DBG:
                continue
            pG = ps_next()
            mm(pG, pG.ap[:, 0:128], QK, qk[:, 4 + h, cs], QK, qk[:, 4 + h, cs])
            mm(pG, pG.ap[:, 128:256], QK, qk[:, h, cs], QK, qk[:, 4 + h, cs])
            TT(dve, ATT, ATT.ap, pG, pG.ap[:, 128:256], EL, EL.ap, ALU.mult)
            STT(Lm, Lm.ap, pG, pG.ap[:, 0:128], beta, EL, EL.ap, ALU.mult, ALU.mult, rd=[sm])
            TT(dve, Lm, Lm.ap, Lm, Lm.ap, self.AFTER01, self.AFTER01.ap[:], ALU.mult)
            TT(dve, Ld, Ld.ap, Lm, Lm.ap, self.BD16, self.BD16.ap[:], ALU.mult)
            TT(dve, Lo, Lo.ap, Lm, Lm.ap, Ld, Ld.ap, ALU.subtract)
            if "h2" in S5DBG:
                continue
            pT = ps_next()
            tr(pT, pT.ap[:, 0:128], Ld, Ld.ap)
            if "h2a" in S5DBG:
                continue
            tr(pT, pT.ap[:, 128:256], ATT, ATT.ap)
            if "h2b" in S5DBG:
                continue
            CP(act, Md, Md.ap, pT, pT.ap[:, 0:128])
            if "h2c" in S5DBG:
                continue
            TT(dve, Y0, Y0.ap, ident, ident.ap[:], Md, Md.ap, ALU.subtract)
            if "h2d" in S5DBG:
                continue
            CP(act, ATT_T, ATT_T.ap, pT, pT.ap[:, 128:256])
            if "h3" in S5DBG:
                continue
            Md2, Md4, Ld4, Ld8 = s1, s2, s4, s3
            p = ps_next()
            mm(p, p.ap[:, 0:128], Ld, Ld.ap, Md, Md.ap)
            if "h3a" in S5DBG:
                continue
            mm(p, p.ap[:, 128:256], Md, Md.ap, Ld, Ld.ap)
            if "h3b" in S5DBG:
                continue
            CP(act, Md2, Md2.ap, p, p.ap[:, 0:128]); CP(dve, Ld2, Ld2.ap, p, p.ap[:, 128:256])
            p = ps_next()
            if "h3c" in S5DBG:
                continue
            mm(p, p.ap[:, 0:128], Ld2, Ld2.ap, Md2, Md2.ap)
            mm(p, p.ap[:, 128:256], Md2, Md2.ap, Ld2, Ld2.ap)
            CP(act, Md4, Md4.ap, p, p.ap[:, 0:128]); CP(dve, Ld4, Ld4.ap, p, p.ap[:, 128:256])
            p = ps_next()
            mm(p, p.ap[:, 0:128], Md4, Md4.ap, Ld4, Ld4.ap)
            CP(act, Ld8, Ld8.ap, p, p.ap[:, 0:128])
            if "h4" in S5DBG:
                continue
            Ya, Yb = Y0, Y1
            for Lp in (Ld2, Ld4, Ld8):
                p = ps_next()
                mm(p, p.ap[:, 0:128], Lp, Lp.ap, Ya, Ya.ap)
                TT(dve, Yb, Yb.ap, Ya, Ya.ap, p, p.ap[:, 0:128], ALU.add)
                Ya, Yb = Yb, Ya
            DinvT = Ya
            Pm = s1
            p = ps_next()
            mm(p, p.ap[:, 0:128], Lo, Lo.ap, DinvT, DinvT.ap)
            CP(act, Pm, Pm.ap, p, p.ap[:, 0:128])
            if "h5" in S5DBG:
                continue
            R, Z0, A1 = r1, r2, r3
            TS(dve, R, R.ap[:, 0:128], VTM, vtm[:, tt, hs], beta, None, ALU.mult, rd=[sm])
            TS(dve, R, R.ap[:, 128:256], KTMt, KTMt.ap[:, hs], be, None, ALU.mult, rd=[sm])
            p = ps_next()
            mm(p, p.ap[:, 0:256], DinvT, DinvT.ap, R, R.ap)
            CP(act, Z0, Z0.ap, p, p.ap[:, 0:256])
            p = ps_next()
            mm(p, p.ap[:, 0:256], Pm, Pm.ap, Z0, Z0.ap)
            CP(act, A1, A1.ap, p, p.ap[:, 0:256])
            Z1 = r1
            p = ps_next()
            mm(p, p.ap[:, 0:256], Pm, Pm.ap, A1, A1.ap)
            TT(dve, Z1, Z1.ap, Z0, Z0.ap, p, p.ap[:, 0:256], ALU.add)
            Z2 = r2
            p = ps_next()
            mm(p, p.ap[:, 0:256], Pm, Pm.ap, Z1, Z1.ap)
            STT(Z2, Z2.ap, p, p.ap[:, 0:256], -1.0, Z1, Z1.ap, ALU.mult, ALU.add)
            if "h6" in S5DBG:
                continue
            WT, KD, VN, TMPI = s2, s3, s4, s5
            p = ps_next()
            tr(p, p.ap[:, 0:128], Z2, Z2.ap[:, 128:256])
            CP(act, WT, WT.ap, p, p.ap[:, 0:128])
            TS(dve, KD, KD.ap, KTMt, KTMt.ap[:, hs], egl, None, ALU.mult, rd=[sm])
            if "h7" in S5DBG:
                continue
            pv, pq_, pn = ps_next(), ps_next(), ps_next()
            for c in range(2):
                rows = slice(c * 64, (c + 1) * 64)
                ccs = slice(tt * 128 + c * 64, tt * 128 + (c + 1) * 64)
                mm(pv, pv.ap[rows, 0:128], WT, WT.ap[:, c * 64:(c + 1) * 64], S, S.ap[:, h, :])
                STT(VN, VN.ap[rows, :], pv, pv.ap[rows, 0:128], -1.0, Z2, Z2.ap[rows, 0:128], ALU.mult, ALU.add)
                mm(pq_, pq_.ap[rows, 0:128], QK, qk[:, h, ccs], S, S.ap[:, h, :])
                mm(pn, pn.ap[rows, 0:128], ATT_T, ATT_T.ap[rows, c * 64:(c + 1) * 64], VN, VN.ap[rows, :])
                pS = ps_next()
                mm(pS, pS.ap[:, 0:128], KD, KD.ap[rows, :], VN, VN.ap[rows, :])
                dcol = sm.ap[:, 20 + 2 * h + c:21 + 2 * h + c]
                STT(S, S.ap[:, h, :], S, S.ap[:, h, :], dcol, pS, pS.ap[:, 0:128], ALU.mult, ALU.add, rd=[sm])
            A(AF.Copy, TMPI, TMPI.ap, pq_, pq_.ap[:, 0:128], scale=egam, rd=[sm])
            TT(dve, O, O.ap[:, hs], TMPI, TMPI.ap, pn, pn.ap[:, 0:128], ALU.add)
        p = ps_next()
        for kt in range(KT):
            mm(p, p.ap[:, :], HB, r_(HB.ap[:, kt, cs]), wto, r_(wvo[:, kt, :]), start=(kt == 0), stop=(kt == KT - 1))
        A(AF.Silu, SGt, SGt.ap, p, p.ap[:, :])
        TT(dve, SGt, SGt.ap, SGt, SGt.ap, self.GNG2, self.GNG2.ap[:], ALU.mult)
        for h in range(4):
            hs = slice(h * 128, (h + 1) * 128)
            A(AF.Square, JK, JK.ap, O, O.ap[:, hs], accum=SSQ.ap[:, h:h + 1], wr=[SSQ])
        A(AF.Sqrt, SSQ, SSQ.ap[:, 4:8], SSQ, SSQ.ap[:, 0:4], scale=1.0 / 128.0, bias=eps_c.ap[:], rd=[eps_c])
        K.op(dve, lambda en: en.reciprocal(out=SSQ.ap[:, 4:8], in_=SSQ.ap[:, 4:8]), reads=[SSQ], writes=[SSQ])
        for h in range(4):
            hs = slice(h * 128, (h + 1) * 128)
            STT(YT, YT.ap[:, hs], O, O.ap[:, hs], SSQ.ap[:, 4 + h:5 + h], SGt, SGt.ap[:, hs], ALU.mult, ALU.mult, rd=[SSQ])
        pt = ps_next()
        for h in range(4):
            tr(pt, pt.ap[:, h * 128:(h + 1) * 128], YT, YT.ap[:, h * 128:(h + 1) * 128])
        CP(act, YB, r_(yb[:, :, cs]), pt, pt.ap[:, :].rearrange("p (h t) -> p h t", h=4))
    return _view(YB, yb)


Mixers.run_gdn = _gdn_run


def _view(t, ap):
    t.ap = ap
    return t


def make_mixers(env):
    return Mixers(env)


_CACHE = {}


def _lay_cols(v, n):
    return np.ascontiguousarray(np.asarray(v, np.float32).reshape(n, 128).T)


RUN_DEPTH = DEPTH
RUN_USE = ("gla", "s5", "gdn", "ffn")


def kernel(**inp):
    depth = RUN_DEPTH
    key = (RUN_USE, depth)
    if key not in _CACHE:
        _CACHE[key] = build_program(depth, RUN_USE)
    nc = _CACHE[key]
    f = lambda a: np.ascontiguousarray(np.asarray(a, np.float32)[:depth])
    shared = {
        "w_ada": f(inp["w_ada"]), "w_in": f(inp["w_in"]), "w_out": f(inp["w_out"]),
        "w_branch_gla": f(inp["w_branch_gla"]), "w_branch_s5": f(inp["w_branch_s5"]),
        "w_branch_gdn": f(inp["w_branch_gdn"]), "w_ffn_in": f(inp["w_ffn_in"]), "w_ffn_out": f(inp["w_ffn_out"]),
        "b_ada": np.stack([_lay_cols(inp["b_ada"][l], 48) for l in range(depth)]),
        "norm1_g": np.stack([_lay_cols(inp["norm1_g"][l], KT) for l in range(depth)]),
        "norm2_g": np.stack([_lay_cols(inp["norm2_g"][l], KT) for l in range(depth)]),
        "final_g": _lay_cols(inp["final_g"], KT),
    }
    shared.update({k: v[:depth] for k, v in layout_mixer_params(inp).items()})
    x = np.ascontiguousarray(np.asarray(inp["x"], np.float32))
    c = np.ascontiguousarray(np.asarray(inp["c"], np.float32))
    in_maps = []
    for core in range(8):
        bi = core % 4
        m = dict(shared)
        m["x"] = x[bi]
        m["c"] = _lay_cols(c[bi], KT)
        in_maps.append(m)
    res = run_bass_kernel_spmd(nc, in_maps, core_ids=list(range(8)))
    out = np.stack([np.asarray(res.results[bi]["out"], np.float32) for bi in range(4)], axis=0)
    return out
```

```python
import contextlib
import numpy as np
import concourse.bass as bass
import concourse.mybir as mybir
from concourse.bass_utils import run_bass_kernel_spmd

F32 = mybir.dt.float32
F32R = mybir.dt.float32r
AF = mybir.ActivationFunctionType
ALU = mybir.AluOpType
AX = mybir.AxisListType

D = 1024
SEQ = 4096
DEPTH = 4
TB = 512
NB = SEQ // TB
KT = D // 128
EPS = 1e-6
DFF = 2816
NFF = DFF // 128
D_IN = 7192
O_GQ, O_GK, O_GV, O_GLR, O_GOG = 0, 256, 512, 1024, 1040
O_S5 = 1552
O_DQKV, O_DBETA, O_DA, O_DOG = 2064, 3600, 3604, 3608
O_ZG = 4120
PI2 = 6.283185307179586


class Tk:
    __slots__ = ("ap", "w", "r", "psum")

    def __init__(self, ap, psum=False):
        self.ap = ap
        self.w = None
        self.r = {}
        self.psum = psum


class Eng:
    def __init__(self, name, obj, sem, sync_self):
        self.name = name
        self.obj = obj
        self.sem = sem
        self.count = 0
        self.seen = {}
        self.sync_self = sync_self
        self.lanes = []
        self.k = 0


SYNC_SELF = True


class KB:
    def __init__(self, nc, es):
        self.nc = nc
        self.es = es
        self.sems = {}
        mk = lambda n: es.enter_context(nc.semaphore(n))
        self.pe = Eng("pe", nc.tensor, mk("s_pe"), False)
        self.act = Eng("act", nc.scalar, mk("s_act"), SYNC_SELF)
        self.dve = Eng("dve", nc.vector, mk("s_dve"), SYNC_SELF)
        self.pool = Eng("pool", nc.gpsimd, mk("s_pool"), True)
        self.sp = Eng("sp", nc.sync, mk("s_sp"), False)
        for q, nl in ((self.sp, 8), (self.pool, 6)):
            for i in range(nl):
                q.lanes.append([mk(f"s_{q.name}_l{i}"), 0])
        self.n_sb = 0

    def sb(self, shape, dtype=F32, name=None):
        self.n_sb += 1
        t = self.es.enter_context(self.nc.sbuf_tensor(name or f"sb{self.n_sb}", list(shape), dtype))
        return t

    def psum(self, name):
        return self.es.enter_context(self.nc.psum_tensor(name, [128, 512], F32))

    def _deps(self, reads, writes, eng=None):
        need = {}

        def add(tok):
            s, v = tok
            k = id(s)
            if k not in need or need[k][1] < v:
                need[k] = (s, v)
        for t in reads:
            if t.w is not None:
                add(t.w)
            if t.psum:
                for tok in t.r.values():
                    if eng is None or tok[0] is not eng.sem:
                        add(tok)
        for t in writes:
            if t.w is not None:
                add(t.w)
            for tok in t.r.values():
                add(tok)
        return need

    def _wait(self, eng, need):
        for k, (s, v) in need.items():
            if s is eng.sem and not eng.sync_self:
                continue
            if eng.seen.get(k, 0) < v:
                eng.obj.wait_ge(s, v)
                eng.seen[k] = v

    def _mark(self, tok, reads, writes):
        k = id(tok[0])
        for t in reads:
            t.r[k] = tok
        for t in writes:
            t.w = tok
            t.r = {}

    def op(self, eng, emit, reads=(), writes=()):
        self._wait(eng, self._deps(reads, writes, eng))
        ins = emit(eng.obj)
        eng.count += 1
        ins.then_inc(eng.sem, 1)
        eng.seen[id(eng.sem)] = max(eng.seen.get(id(eng.sem), 0), 0)
        self._mark((eng.sem, eng.count), reads, writes)

    def dma(self, q, out_ap, in_ap, reads=(), writes=()):
        lane = q.lanes[q.k % len(q.lanes)]
        q.k += 1
        need = self._deps(reads, writes)
        if lane[1] > 0:
            k = id(lane[0])
            need[k] = (lane[0], lane[1])
        self._wait(q, need)
        ins = q.obj.dma_start(out=out_ap, in_=in_ap)
        lane[1] += 16
        ins.then_inc(lane[0], 16)
        self._mark((lane[0], lane[1]), reads, writes)

    def barrier(self):
        engs = [self.pe, self.act, self.dve, self.pool, self.sp]
        toks = []
        for e in engs:
            if e.count > 0:
                toks.append((e.sem, e.count))
            for l in e.lanes:
                if l[1] > 0:
                    toks.append((l[0], l[1]))
        for e in engs:
            need = {id(s): (s, v) for s, v in toks if s is not e.sem}
            self._wait(e, need)

    def wait_all(self, eng, tiles):
        need = {}
        for t in tiles:
            if t.w is not None:
                s, v = t.w
                if id(s) not in need or need[id(s)][1] < v:
                    need[id(s)] = (s, v)
        self._wait(eng, need)


def r_(ap):
    return ap.bitcast(F32R)


def build_program(depth=DEPTH, use=("gla", "s5", "gdn", "ffn")):
    nc = bass.Bass("TRN2", target_bir_lowering=False)
    dr = lambda n, s, k="ExternalInput", dt=F32: nc.dram_tensor(n, list(s), dt, kind=k).ap()
    x_in = dr("x", [SEQ, D])
    c_in = dr("c", [128, KT])
    w_ada = dr("w_ada", [depth, D, 6 * D])
    b_ada = dr("b_ada", [depth, 128, 48])
    n1g = dr("norm1_g", [depth, 128, KT])
    n2g = dr("norm2_g", [depth, 128, KT])
    fng = dr("final_g", [128, KT])
    w_in = dr("w_in", [depth, D, D_IN])
    w_out = dr("w_out", [depth, D, D])
    w_br = {"gla": dr("w_branch_gla", [depth, 512, D]), "s5": dr("w_branch_s5", [depth, 512, D]),
            "gdn": dr("w_branch_gdn", [depth, 512, D])}
    w_fi = dr("w_ffn_in", [depth, D, 2 * DFF])
    w_fo = dr("w_ffn_out", [depth, DFF, D])
    PRM = declare_mixer_params(dr, depth)
    y_out = dr("out", [SEQ, D], "ExternalOutput")
    xres = dr("xres", [D, SEQ], "Internal")

    with contextlib.ExitStack() as es:
        K = KB(nc, es)
        pe, act, dve, pool, sp = K.pe, K.act, K.dve, K.pool, K.sp
        psall = es.enter_context(nc.psum_tensor("psall", [128, 4096], F32))
        PS = [Tk(psall[:, 512 * i:512 * (i + 1)], psum=True) for i in range(8)]
        ps_rr = [0]

        def ps_next():
            ps_rr[0] = (ps_rr[0] + 1) % 6
            return PS[ps_rr[0]]

        def mm(o_t, o_ap, l_t, l_ap, r_t, r_ap, start=True, stop=True):
            K.op(pe, lambda e: e.matmul(o_ap, l_ap, r_ap, start=start, stop=stop),
                 reads=[l_t, r_t], writes=[o_t])

        def tr(o_t, o_ap, i_t, i_ap):
            K.op(pe, lambda e: e.transpose(out=o_ap, in_=i_ap, identity=ident.ap[:]),
                 reads=[i_t, ident], writes=[o_t])

        def A(func, o_t, o_ap, i_t, i_ap, scale=1.0, bias=None, rd=(), accum=None, wr=()):
            kw = {}
            if bias is not None:
                kw["bias"] = bias
            if accum is not None:
                kw["accum_out"] = accum
            K.op(act, lambda e: e.activation(out=o_ap, in_=i_ap, func=func, scale=scale, **kw),
                 reads=[i_t] + list(rd), writes=[o_t] + list(wr))

        def TT(eng, o_t, o_ap, a_t, a_ap, b_t, b_ap, op):
            K.op(eng, lambda e: e.tensor_tensor(out=o_ap, in0=a_ap, in1=b_ap, op=op),
                 reads=[a_t, b_t], writes=[o_t])

        def TS(eng, o_t, o_ap, a_t, a_ap, s1, s2, op0, op1=None, rd=()):
            if op1 is None:
                K.op(eng, lambda e: e.tensor_scalar(out=o_ap, in0=a_ap, scalar1=s1, scalar2=None, op0=op0),
                     reads=[a_t] + list(rd), writes=[o_t])
            else:
                K.op(eng, lambda e: e.tensor_scalar(out=o_ap, in0=a_ap, scalar1=s1, scalar2=s2, op0=op0, op1=op1),
                     reads=[a_t] + list(rd), writes=[o_t])

        def STT(o_t, o_ap, a_t, a_ap, scalar, b_t, b_ap, op0, op1, rd=()):
            K.op(dve, lambda e: e.scalar_tensor_tensor(out=o_ap, in0=a_ap, scalar=scalar, in1=b_ap, op0=op0, op1=op1),
                 reads=[a_t, b_t] + list(rd), writes=[o_t])

        def CP(eng, o_t, o_ap, i_t, i_ap):
            if eng is act:
                A(AF.Copy, o_t, o_ap, i_t, i_ap)
            else:
                K.op(eng, lambda e: e.tensor_copy(out=o_ap, in_=i_ap), reads=[i_t], writes=[o_t])

        def MS(eng, t, ap, val):
            K.op(eng, lambda e: e.memset(ap, val), writes=[t])

        ident = Tk(K.sb([128, 128], F32, "ident"))
        ones_r = Tk(K.sb([128, 128], F32, "ones_r"))
        eps_c = Tk(K.sb([128, 1], F32, "eps_c"))
        ones_f = Tk(K.sb([128, 128], F32, "ones_f"))
        MS(dve, ones_f, ones_f.ap[:], 1.0)
        CP(dve, ones_r, r_(ones_r.ap[:]), ones_f, ones_f.ap[:])
        MS(dve, eps_c, eps_c.ap[:], EPS)
        MS(dve, ident, ident.ap[:], 1.0)
        K.op(pool, lambda e: e.affine_select(out=ident.ap[:], in_=ident.ap[:], pattern=[[1, 128]],
                                             compare_op=ALU.is_equal, fill=0.0, base=0,
                                             channel_multiplier=-1), reads=[ident], writes=[ident])

        HB = Tk(K.sb([128, KT, TB], F32, "HB"))
        SLAB = [Tk(K.sb([128, KT * 512], F32, f"slab{i}")) for i in range(2)]
        slab_rr = [0]
        ARENA_R, ARENA_F = 11264, 13056
        arenaR = K.sb([128, ARENA_R], F32, "arenaR")
        arenaF = K.sb([128, ARENA_F], F32, "arenaF")
        MG = Tk(arenaR[:, ARENA_R - KT * TB:ARENA_R].rearrange("p (k t) -> p k t", k=KT))
        XB = Tk(arenaF[:, 0:KT * TB].rearrange("p (k t) -> p k t", k=KT))
        XB_N = KT * TB
        XR = [Tk(None) for _ in range(NB)]

        class Carver:
            def __init__(self, keep_x=True):
                self.o = {True: 0, False: XB_N if keep_x else 0}

            def get(self, n, r=False, parts=128):
                o = self.o[r]
                a = (arenaR if r else arenaF)[0:parts, o:o + n]
                self.o[r] = o + n
                assert self.o[r] <= (ARENA_R if r else ARENA_F), (r, self.o[r])
                return Tk(a)

            def mark(self):
                return dict(self.o)

            def reset(self, m):
                self.o = dict(m)

        cond = Tk(K.sb([128, KT], F32, "cond"))
        modc = Tk(K.sb([128, 48], F32, "modc"))
        bada = Tk(K.sb([128, 48], F32, "bada"))
        g1c = Tk(K.sb([128, KT], F32, "g1c"))
        g2c = Tk(K.sb([128, KT], F32, "g2c"))
        gfc = Tk(K.sb([128, KT], F32, "gfc"))
        a1 = Tk(K.sb([128, KT], F32, "a1"))
        a2 = Tk(K.sb([128, KT], F32, "a2"))
        zero_c = Tk(K.sb([128, 1], F32, "zero_c"))
        MS(dve, zero_c, zero_c.ap[:], 0.0)

        def slab_next():
            slab_rr[0] ^= 1
            return SLAB[slab_rr[0]]

        def slab_load(src_ap, n):
            t = slab_next()
            kk = src_ap.shape[0] // 128
            dst = t.ap[:, 0:kk * n].rearrange("p (k c) -> p k c", k=kk)
            K.dma(pool, r_(dst), src_ap.rearrange("(k p) c -> p k c", p=128), writes=[t])
            return t, dst

        def xview(b):
            return xres[:, b * TB:(b + 1) * TB].rearrange("(k p) t -> p k t", p=128)

        def load_x(b):
            K.dma(sp, XB.ap[:], xview(b), reads=[XR[b]], writes=[XB])

        def store_x(b):
            K.dma(sp, xview(b), XB.ap[:], reads=[XB], writes=[XR[b]])

        def norm_block(acol_t, acol_ap, bcol_t, bcol_ap, out_r=True, OUT=None):
            cv = Carver()
            SQ = cv.get(KT * TB, r=True)
            RS = cv.get(TB)
            sq3 = SQ.ap.rearrange("p (k t) -> p k t", k=KT)
            A(AF.Square, SQ, r_(sq3), XB, XB.ap[:])
            p = ps_next()
            for kt in range(KT):
                mm(p, p.ap[:, :], ones_r, r_(ones_r.ap[:]), SQ, r_(sq3[:, kt, :]), start=(kt == 0), stop=(kt == KT - 1))
            A(AF.Sqrt, RS, RS.ap, p, p.ap[:, :], scale=1.0 / D, bias=eps_c.ap[:], rd=[eps_c])
            K.op(dve, lambda e: e.reciprocal(out=RS.ap, in_=RS.ap), reads=[RS], writes=[RS])
            TT(dve, SQ, r_(sq3), XB, XB.ap[:], RS, RS.ap.unsqueeze(1).to_broadcast([128, KT, TB]), ALU.mult)
            OUT = OUT or HB
            for kt in range(KT):
                o = OUT.ap[:, kt, :]
                A(AF.Identity, OUT, r_(o) if out_r else o, SQ, sq3[:, kt, :], scale=acol_ap[:, kt:kt + 1],
                  bias=(bcol_ap[:, kt:kt + 1] if bcol_ap is not None else zero_c.ap[:]),
                  rd=[acol_t] + ([bcol_t] if bcol_t is not None else [zero_c]))

        K.dma(sp, cond.ap[:], c_in, writes=[cond])
        A(AF.Silu, cond, cond.ap[:], cond, cond.ap[:])
        K.dma(sp, gfc.ap[:], fng, writes=[gfc])
        XT = [Tk(XB.ap[:, 2 * i:2 * i + 2, :].rearrange("p a b -> p (a b)")) for i in range(4)]
        STG = [Tk(arenaF[:, XB_N + 1024 * i:XB_N + 1024 * (i + 1)].rearrange("p (k t) -> p k t", k=KT)) for i in range(4)]
        for tt in range(SEQ // 128):
            xt, sg = XT[tt % 4], STG[tt % 4]
            K.dma(sp, xt.ap, x_in[tt * 128:(tt + 1) * 128, :], writes=[xt])
            for half in range(2):
                p = ps_next()
                for q in range(4):
                    kt = half * 4 + q
                    tr(p, p.ap[:, q * 128:(q + 1) * 128], xt, xt.ap[:, kt * 128:(kt + 1) * 128])
                CP(act if half == 0 else dve, sg, sg.ap[:, half * 4:half * 4 + 4, :], p,
                   p.ap[:, :].rearrange("p (q t) -> p q t", q=4))
            K.dma(sp, xres[:, tt * 128:(tt + 1) * 128].rearrange("(k p) t -> p k t", p=128), sg.ap,
                  reads=[sg], writes=[XR[tt // 4]])
        K.barrier()

        MX = make_mixers(locals())

        for l in range(depth):
            K.dma(sp, bada.ap[:], b_ada[l], writes=[bada])
            K.dma(sp, g1c.ap[:], n1g[l], writes=[g1c])
            K.dma(sp, g2c.ap[:], n2g[l], writes=[g2c])
            pm = ps_next()
            WA = [Tk(arenaF[:, 4096 * i:4096 * (i + 1)]) for i in range(2)]
            for jg in range(12):
                t = WA[jg % 2]
                dst = t.ap[:, :].rearrange("p (k c) -> p k c", k=KT)
                K.dma(sp, dst, w_ada[l][:, jg * 512:(jg + 1) * 512].rearrange("(k p) c -> p k c", p=128), writes=[t])
                for q in range(4):
                    j = jg * 4 + q
                    for kt in range(KT):
                        mm(pm, pm.ap[:, j:j + 1], t, dst[:, kt, q * 128:(q + 1) * 128], cond, cond.ap[:, kt:kt + 1],
                           start=(kt == 0), stop=(kt == KT - 1))
            TT(dve, modc, modc.ap[:], pm, pm.ap[:, 0:48], bada, bada.ap[:], ALU.add)
            STT(a1, a1.ap[:], modc, modc.ap[:, 8:16], 1.0, g1c, g1c.ap[:], ALU.add, ALU.mult)
            STT(a2, a2.ap[:], modc, modc.ap[:, 32:40], 1.0, g2c, g2c.ap[:], ALU.add, ALU.mult)
            K.barrier()
            MX.layer_setup(l)

            for b in range(NB):
                if any(u in use for u in ("gla", "s5", "gdn")):
                    load_x(b)
                    norm_block(a1, a1.ap, modc, modc.ap[:, 0:8])
                    K.barrier()
                    first = True
                    for name in ("gla", "s5", "gdn"):
                        if name not in use:
                            continue
                        YB = MX.run(name, l, b)
                        bi = ("gla", "s5", "gdn").index(name)
                        for oc in range(KT):
                            if oc % 4 == 0:
                                zt, zv = slab_load(w_in[l][:, O_ZG + bi * D + (oc // 4) * 512:O_ZG + bi * D + (oc // 4 + 1) * 512], 512)
                                wt, wv = slab_load(w_br[name][l][:, (oc // 4) * 512:(oc // 4 + 1) * 512], 512)
                            pp, pz = ps_next(), ps_next()
                            for kt in range(4):
                                mm(pp, pp.ap[:, :], wt, r_(wv[:, kt, (oc % 4) * 128:(oc % 4 + 1) * 128]), YB, r_(YB.ap[:, kt, :]),
                                   start=(kt == 0), stop=(kt == 3))
                            for kt in range(KT):
                                mm(pz, pz.ap[:, :], zt, r_(zv[:, kt, (oc % 4) * 128:(oc % 4 + 1) * 128]), HB, r_(HB.ap[:, kt, :]),
                                   start=(kt == 0), stop=(kt == KT - 1))
                            g = MX.GT[oc % 2]
                            A(AF.Sigmoid, g, g.ap, pz, pz.ap[:, :])
                            if first:
                                TT(dve, MG, r_(MG.ap[:, oc, :]), g, g.ap, pp, pp.ap[:, :], ALU.mult)
                            else:
                                TT(dve, g, g.ap, g, g.ap, pp, pp.ap[:, :], ALU.mult)
                                TT(dve, MG, r_(MG.ap[:, oc, :]), MG, MG.ap[:, oc, :], g, g.ap, ALU.add)
                        first = False
                        K.barrier()
                    load_x(b)
                    for oc in range(KT):
                        if oc % 4 == 0:
                            wt, wv = slab_load(w_out[l][:, (oc // 4) * 512:(oc // 4 + 1) * 512], 512)
                        po = ps_next()
                        for kt in range(KT):
                            mm(po, po.ap[:, :], wt, r_(wv[:, kt, (oc % 4) * 128:(oc % 4 + 1) * 128]), MG, r_(MG.ap[:, kt, :]),
                               start=(kt == 0), stop=(kt == KT - 1))
                        STT(XB, XB.ap[:, oc, :], po, po.ap[:, :], modc.ap[:, 16 + oc:17 + oc], XB, XB.ap[:, oc, :],
                            ALU.mult, ALU.add, rd=[modc])
                    if "ffn" not in use:
                        store_x(b)
                    K.barrier()
                elif "ffn" in use:
                    load_x(b)
                if "ffn" in use:
                    norm_block(a2, a2.ap, modc, modc.ap[:, 24:32])
                    K.barrier()
                    cv = Carver()
                    UF = [cv.get(TB, r=True) for _ in range(NFF)]
                    SA = [cv.get(TB) for _ in range(2)]
                    for jg in range(NFF // 2):
                        t = slab_next()
                        dst = t.ap[:, :].rearrange("p (k c) -> p k c", k=KT)
                        K.dma(pool, r_(dst[:, :, 0:256]), w_fi[l][:, jg * 256:(jg + 1) * 256].rearrange("(k p) c -> p k c", p=128), writes=[t])
                        K.dma(pool, r_(dst[:, :, 256:512]), w_fi[l][:, DFF + jg * 256:DFF + (jg + 1) * 256].rearrange("(k p) c -> p k c", p=128), writes=[t])
                        for q in range(2):
                            j = jg * 2 + q
                            pa, pb = ps_next(), ps_next()
                            for kt in range(KT):
                                mm(pa, pa.ap[:, :], t, r_(dst[:, kt, q * 128:(q + 1) * 128]), HB, r_(HB.ap[:, kt, :]),
                                   start=(kt == 0), stop=(kt == KT - 1))
                            for kt in range(KT):
                                mm(pb, pb.ap[:, :], t, r_(dst[:, kt, 256 + q * 128:256 + (q + 1) * 128]), HB, r_(HB.ap[:, kt, :]),
                                   start=(kt == 0), stop=(kt == KT - 1))
                            s = SA[j % 2]
                            A(AF.Silu, s, s.ap, pa, pa.ap[:, :])
                            TT(dve, UF[j], r_(UF[j].ap), s, s.ap, pb, pb.ap[:, :], ALU.mult)
                    for oc in range(KT):
                        wt, wv = slab_load(w_fo[l][:, oc * 128:(oc + 1) * 128], 128)
                        po = ps_next()
                        for j in range(NFF):
                            mm(po, po.ap[:, :], wt, r_(wv[:, j, :]), UF[j], r_(UF[j].ap), start=(j == 0), stop=(j == NFF - 1))
                        STT(XB, XB.ap[:, oc, :], po, po.ap[:, :], modc.ap[:, 40 + oc:41 + oc], XB, XB.ap[:, oc, :],
                            ALU.mult, ALU.add, rd=[modc])
                    store_x(b)
                    K.barrier()

        for b in range(NB):
            load_x(b)
            HN = Tk(arenaF[:, XB_N + 512:XB_N + 512 + KT * TB].rearrange("p (k t) -> p k t", k=KT))
            norm_block(gfc, gfc.ap, None, None, out_r=False, OUT=HN)
            K.barrier()
            OT = [Tk(arenaF[:, 2 * XB_N + 512 + 1024 * i:2 * XB_N + 512 + 1024 * (i + 1)]) for i in range(2)]
            for tt in range(4):
                ot = OT[tt % 2]
                for half in range(2):
                    p = ps_next()
                    for q in range(4):
                        kt = half * 4 + q
                        tr(p, p.ap[:, q * 128:(q + 1) * 128], HN, HN.ap[:, kt, tt * 128:(tt + 1) * 128])
                    CP(act if half == 0 else dve, ot, ot.ap[:, half * 512:(half + 1) * 512], p, p.ap[:, :])
                K.dma(sp, y_out[b * TB + tt * 128:b * TB + (tt + 1) * 128, :], ot.ap, reads=[ot])
            K.barrier()
        K.barrier()
    return nc


def declare_mixer_params(dr, depth=DEPTH):
    P = {}
    P["gla_w_lr"] = dr("gla_w_lr", [depth, 16, 256])
    P["gla_b_lr"] = dr("gla_b_lr", [depth, 1, 256])
    P["gla_ng4"] = dr("gla_ng4", [depth, 1, 512])
    P["s5_lam_col"] = dr("s5_lam_col", [depth, 128, 16, 3])
    P["s5_BR"] = dr("s5_BR", [depth, 128, 16, 128])
    P["s5_BI"] = dr("s5_BI", [depth, 128, 16, 128])
    P["s5_CR"] = dr("s5_CR", [depth, 128, 16, 64])
    P["s5_CI"] = dr("s5_CI", [depth, 128, 16, 64])
    P["s5_dcol"] = dr("s5_dcol", [depth, 128, 4])
    P["s5_w_glu"] = dr("s5_w_glu", [depth, 512, 1024])
    P["gdn_cw"] = dr("gdn_cw", [depth, 128, 12, 4])
    P["gdn_ab"] = dr("gdn_ab", [depth, 1, 8])
    P["gdn_ng4"] = dr("gdn_ng4", [depth, 1, 512])
    return P


def layout_mixer_params(inp):
    f = lambda a: np.ascontiguousarray(np.asarray(a, np.float32))
    out = {}
    out["gla_w_lr"] = f(inp["gla_w_lr"])
    out["gla_b_lr"] = f(inp["gla_b_lr"]).reshape(DEPTH, 1, 256)
    out["gla_ng4"] = np.ascontiguousarray(np.tile(f(inp["gla_norm_g"]), (1, 4)).reshape(DEPTH, 1, 512))
    L = DEPTH
    lam = np.stack([f(inp["s5_lambda_re"]), f(inp["s5_lambda_im"]),
                    np.repeat(f(inp["s5_log_dt"])[:, :, None], 64, axis=2)], axis=-1)
    lam = lam.reshape(L, 16, 2, 64, 3).transpose(0, 2, 3, 1, 4).reshape(L, 128, 16, 3)
    out["s5_lam_col"] = np.ascontiguousarray(lam)
    for nm, src in (("s5_BR", "s5_b_re"), ("s5_BI", "s5_b_im")):
        bsrc = f(inp[src]).reshape(L, 4, 4, 2, 64, 16)
        arr = np.zeros((L, 4, 2, 16, 4, 4, 2, 64), np.float32)
        for q in range(4):
            for g2 in range(2):
                arr[:, q, g2, :, :, q, g2, :] = bsrc[:, :, q, g2].transpose(0, 3, 1, 2)
        out[nm] = np.ascontiguousarray(arr.reshape(L, 128, 16, 128))
    for nm, src in (("s5_CR", "s5_c_re"), ("s5_CI", "s5_c_im")):
        csrc = f(inp[src]).reshape(L, 8, 2, 2, 16, 64)
        arr = np.zeros((L, 2, 64, 8, 2, 2, 2, 16), np.float32)
        for ql in range(2):
            for g2 in range(2):
                arr[:, g2, :, :, ql, ql, g2, :] = csrc[:, :, ql, g2].transpose(0, 3, 1, 2)
        out[nm] = np.ascontiguousarray(arr.reshape(L, 128, 16, 64))
    out["s5_dcol"] = np.ascontiguousarray(f(inp["s5_d"]).reshape(L, 4, 128).transpose(0, 2, 1))
    out["s5_w_glu"] = f(inp["s5_w_glu"])
    out["gdn_cw"] = np.ascontiguousarray(f(inp["gdn_conv_w"]).reshape(L, 4, 12, 128).transpose(0, 3, 2, 1))
    out["gdn_ab"] = np.ascontiguousarray(np.concatenate([f(inp["gdn_a_log"]), f(inp["gdn_dt_bias"])], axis=1).reshape(L, 1, 8))
    out["gdn_ng4"] = np.ascontiguousarray(np.tile(f(inp["gdn_norm_g"]), (1, 4)).reshape(L, 1, 512))
    return out


S5DBG = set()


class Mixers:
    def __init__(self, env):
        self.env = env
        K = env["K"]
        self.GT = [Tk(K.sb([128, TB], F32, f"gt{i}")[:, :]) for i in range(2)]
        MS, dve, pool = env["MS"], env["dve"], env["pool"]
        def tri(name, val, mode, chunked=True):
            t = Tk(K.sb([128, 128], F32, name))
            MS(dve, t, t.ap[:], val)
            if mode == "upper":
                pat, cm, cmp_ = [[1, 128]], -1, ALU.is_ge
            elif mode == "lower_strict":
                pat, cm, cmp_ = [[-1, 128]], 1, ALU.is_gt
            else:
                pat, cm, cmp_ = [[-1, 128]], 1, ALU.is_ge
            K.op(pool, lambda e: e.affine_select(out=t.ap[:], in_=t.ap[:], pattern=pat, compare_op=cmp_,
                                                 fill=0.0, base=0, channel_multiplier=cm), reads=[t], writes=[t])
            if chunked:
                if mode == "upper":
                    MS(dve, t, t.ap[0:64, 64:128], 0.0)
                else:
                    MS(dve, t, t.ap[64:128, 0:64], 0.0)
            return t
        self.TRI_incl = tri("tri_incl", -1.0 / 16.0, "upper")
        self.TRI_after = tri("tri_after", -1.0 / 16.0, "lower_strict")
        self.CMASK = tri("cmask", 1.0, "upper")
        self.UPPER01 = tri("upper01", 1.0, "upper", chunked=False)
        self.MASKGT = tri("maskgt", 1.0, "lower_strict", chunked=False)
        self.AFTER01 = tri("after01", 1.0, "lower_strict")
        self.NEGM = tri("negm", 30000.0, "lower_incl")
        env["TS"](dve, self.NEGM, self.NEGM.ap[:], self.NEGM, self.NEGM.ap[:], -30000.0, None, ALU.add)
        self.CH01 = Tk(K.sb([128, 2], F32, "ch01"))
        MS(dve, self.CH01, self.CH01.ap[:], 0.0)
        MS(dve, self.CH01, self.CH01.ap[0:64, 0:1], 1.0)
        MS(dve, self.CH01, self.CH01.ap[64:128, 1:2], 1.0)
        Q16 = Tk(K.sb([8, 128], F32, "q16"))
        MS(dve, Q16, Q16.ap[:], 1.0)
        K.op(pool, lambda e: e.affine_select(out=Q16.ap[:], in_=Q16.ap[:], pattern=[[1, 128]], compare_op=ALU.is_ge,
                                             fill=0.0, base=0, channel_multiplier=-16), reads=[Q16], writes=[Q16])
        K.op(pool, lambda e: e.affine_select(out=Q16.ap[:], in_=Q16.ap[:], pattern=[[-1, 128]], compare_op=ALU.is_ge,
                                             fill=0.0, base=15, channel_multiplier=16), reads=[Q16], writes=[Q16])
        self.BD16 = Tk(K.sb([128, 128], F32, "bd16"))
        pq = env["ps_next"]()
        env["mm"](pq, pq.ap[:, 0:128], Q16, Q16.ap[:], Q16, Q16.ap[:])
        env["CP"](dve, self.BD16, self.BD16.ap[:], pq, pq.ap[:, 0:128])
        self.gdn_S = Tk(K.sb([128, 4, 128], F32, "gdn_S"))
        self.gdn_Sh = [Tk(self.gdn_S.ap[:, h, :]) for h in range(4)]
        self.HIST = Tk(K.sb([128, 12, 3], F32, "gdn_hist"))
        self.CW = Tk(K.sb([128, 12, 4], F32, "gdn_cw_sb"))
        self.AB = Tk(K.sb([128, 8], F32, "gdn_ab_sb"))
        self.NEGA = Tk(K.sb([128, 4], F32, "gdn_nega"))
        self.GNG2 = Tk(K.sb([128, 512], F32, "gng2"))
        self.ones1 = Tk(K.sb([1, 128], F32, "ones1"))
        MS(dve, self.ones1, self.ones1.ap[:], 1.0)
        self.gla_S = Tk(K.sb([64, 4, 128], F32, "gla_S"))
        self.WLR = Tk(K.sb([16, 256], F32, "wlr"))
        self.BLR = Tk(K.sb([1, 256], F32, "blr"))
        self.GNG = Tk(K.sb([128, 512], F32, "gng"))
        sb2 = lambda shp, n: Tk(K.sb(shp, F32, n)[:])
        self.BwRe = sb2([128, 16, 128], "s5bwre"); self.BwIm = sb2([128, 16, 128], "s5bwim")
        self.CRe = sb2([128, 16, 64], "s5cre"); self.CImN = sb2([128, 16, 64], "s5cimn")
        self.COS = sb2([128, 16, 64], "s5cos"); self.SIN = sb2([128, 16, 64], "s5sin")
        self.Rb = sb2([128, 16, 64], "s5rb"); self.rcol = sb2([128, 16], "s5r")
        self.CAR = sb2([128, 2, 16], "s5car"); self.dcol = sb2([128, 4], "s5d")

    def layer_setup(self, l):
        e = self.env
        K, sp, PRM = e["K"], e["sp"], e["PRM"]
        K.dma(sp, self.WLR.ap[:], PRM["gla_w_lr"][l], writes=[self.WLR])
        K.dma(sp, self.BLR.ap[:], PRM["gla_b_lr"][l], writes=[self.BLR])
        K.dma(sp, self.GNG.ap[:], PRM["gla_ng4"][l].partition_broadcast(128), writes=[self.GNG])
        e["MS"](e["dve"], self.gla_S, self.gla_S.ap[:], 0.0)
        K.barrier()
        if "s5" in e["use"]:
            self.s5_setup(l)
            K.barrier()
        if "gdn" in e["use"]:
            K.dma(sp, self.CW.ap[:], PRM["gdn_cw"][l], writes=[self.CW])
            K.dma(sp, self.AB.ap[:], PRM["gdn_ab"][l].partition_broadcast(128), writes=[self.AB])
            K.dma(sp, self.GNG2.ap[:], PRM["gdn_ng4"][l].partition_broadcast(128), writes=[self.GNG2])
            e["A"](AF.Exp, self.NEGA, self.NEGA.ap[:], self.AB, self.AB.ap[:, 0:4])
            e["TS"](e["dve"], self.NEGA, self.NEGA.ap[:], self.NEGA, self.NEGA.ap[:], -1.0, None, ALU.mult)
            e["MS"](e["dve"], self.gdn_S, self.gdn_S.ap[:], 0.0)
            e["MS"](e["dve"], self.HIST, self.HIST.ap[:], 0.0)
            K.barrier()

    def s5_setup(self, l):
        e = self.env
        K, pe, act, dve, pool, sp, PRM = e["K"], e["pe"], e["act"], e["dve"], e["pool"], e["sp"], e["PRM"]
        mm, A, TT, TS, STT, CP, MS, ps_next = e["mm"], e["A"], e["TT"], e["TS"], e["STT"], e["CP"], e["MS"], e["ps_next"]
        nc = e["nc"]
        cv = e["Carver"](keep_x=False)
        LAM = cv.get(48); lam = LAM.ap.rearrange("p (a c) -> p a c", c=3)
        lr, li, ldt = lam[:, :, 0], lam[:, :, 1], lam[:, :, 2]
        K.dma(sp, lam, PRM["s5_lam_col"][l], writes=[LAM])
        K.dma(sp, self.dcol.ap, PRM["s5_dcol"][l], writes=[self.dcol])
        BR = cv.get(1024); BI = cv.get(1024)
        br3 = BR.ap.rearrange("p (a c) -> p a c", a=8); bi3 = BI.ap.rearrange("p (a c) -> p a c", a=8)
        K.dma(sp, self.CRe.ap, PRM["s5_CR"][l], writes=[self.CRe])
        K.dma(sp, self.CImN.ap, PRM["s5_CI"][l], writes=[self.CImN])
        TS(dve, self.CImN, self.CImN.ap, self.CImN, self.CImN.ap, -1.0, None, ALU.mult)
        W = cv.get(16 * 12); w = W.ap.rearrange("p (n c) -> p n c", c=16)
        dt, th, u, kf, f, g1, c1, s1, wre, wim, t1, t2 = [w[:, i, :] for i in range(12)]
        KI = Tk(K.sb([128, 16], mybir.dt.int32, f"s5ki{l}")[:])
        if "nosetup" in S5DBG:
            for t in (self.COS, self.SIN, self.Rb, self.CAR, self.BwRe, self.BwIm, self.rcol):
                MS(dve, t, t.ap, 0.25)
            return
        A(AF.Exp, W, dt, LAM, ldt)
        TT(dve, W, t1, LAM, lr, W, dt, ALU.mult)
        A(AF.Exp, self.rcol, self.rcol.ap, W, t1)
        TT(dve, W, th, LAM, li, W, dt, ALU.mult)

        def frac_sin(out_ap, off):
            TS(dve, W, u, W, th, 1.0 / PI2, off, ALU.mult, ALU.add)
            CP(dve, KI, KI.ap, W, u)
            CP(dve, W, kf, KI, KI.ap)
            TT(dve, W, f, W, u, W, kf, ALU.subtract)
            TS(dve, W, g1, W, f, 0.5, None, ALU.is_gt)
            TT(dve, W, f, W, f, W, g1, ALU.subtract)
            TS(dve, W, g1, W, f, -0.5, None, ALU.is_lt)
            TT(dve, W, f, W, f, W, g1, ALU.add)
            A(AF.Sin, W, out_ap, W, f, scale=PI2)
        if "nosin" in S5DBG:
            MS(dve, W, s1, 0.6); MS(dve, W, c1, 0.8)
        else:
            frac_sin(s1, 0.0)
            frac_sin(c1, 0.25)
        cos3, sin3 = self.COS.ap, self.SIN.ap
        CP(dve, self.COS, cos3[:, :, 0], W, c1)
        CP(dve, self.SIN, sin3[:, :, 0], W, s1)
        TB1 = cv.get(512); TB2 = cv.get(512); TB3 = cv.get(512); TB4 = cv.get(512)
        n = 1 if "nodbl" not in S5DBG else 64
        while n < 64:
            v16 = lambda t: t.ap.rearrange("p (a c) -> p a c", a=16)[:, :, 0:n]
            x1, x2, cn, sn = v16(TB1), v16(TB2), v16(TB3), v16(TB4)
            CP(dve, TB3, cn, self.COS, cos3[:, :, n - 1:n].to_broadcast([128, 16, n]))
            CP(dve, TB4, sn, self.SIN, sin3[:, :, n - 1:n].to_broadcast([128, 16, n]))
            TT(dve, TB1, x1, self.COS, cos3[:, :, 0:n], TB3, cn, ALU.mult)
            TT(dve, TB2, x2, self.SIN, sin3[:, :, 0:n], TB4, sn, ALU.mult)
            TT(dve, TB1, x1, TB1, x1, TB2, x2, ALU.subtract)
            TT(dve, TB2, x2, self.SIN, sin3[:, :, 0:n], TB3, cn, ALU.mult)
            TT(dve, TB3, cn, self.COS, cos3[:, :, 0:n], TB4, sn, ALU.mult)
            TT(dve, TB2, x2, TB2, x2, TB3, cn, ALU.add)
            CP(dve, self.COS, cos3[:, :, n:2 * n], TB1, x1)
            CP(dve, self.SIN, sin3[:, :, n:2 * n], TB2, x2)
            n *= 2
        MS(dve, self.Rb, self.Rb.ap, 0.0)
        CP(dve, self.Rb, self.Rb.ap[:, :, 1:64], self.rcol, self.rcol.ap.unsqueeze(2).to_broadcast([128, 16, 63]))
        MS(dve, self.CAR, self.CAR.ap, 0.0)
        TT(dve, W, t1, self.rcol, self.rcol.ap, W, c1, ALU.mult)
        TS(dve, W, t1, W, t1, -1.0, None, ALU.add)
        TT(dve, W, t2, self.rcol, self.rcol.ap, W, s1, ALU.mult)
        TT(dve, W, u, LAM, lr, LAM, lr, ALU.mult)
        TT(dve, W, kf, LAM, li, LAM, li, ALU.mult)
        TT(dve, W, u, W, u, W, kf, ALU.add)
        K.op(dve, lambda en: en.reciprocal(out=u, in_=u), reads=[W], writes=[W])
        TT(dve, W, wre, W, t1, LAM, lr, ALU.mult)
        TT(dve, W, kf, W, t2, LAM, li, ALU.mult)
        TT(dve, W, wre, W, wre, W, kf, ALU.add)
        TT(dve, W, wre, W, wre, W, u, ALU.mult)
        TT(dve, W, wim, W, t2, LAM, lr, ALU.mult)
        TT(dve, W, kf, W, t1, LAM, li, ALU.mult)
        TT(dve, W, wim, W, wim, W, kf, ALU.subtract)
        TT(dve, W, wim, W, wim, W, u, ALU.mult)
        DG = [cv.get(128) for _ in range(2)]
        ident, ones_f = e["ident"], e["ones_f"]
        psall, PS = e["psall"], e["PS"]
        WR = cv.get(1024); WI = cv.get(1024); T3 = cv.get(1024)
        for hh in range(2):
            K.dma(sp, br3, PRM["s5_BR"][l][:, 8 * hh:8 * hh + 8, :], writes=[BR])
            K.dma(sp, bi3, PRM["s5_BI"][l][:, 8 * hh:8 * hh + 8, :], writes=[BI])
            for which, (src, base, pts) in enumerate(((wre, 0, [PS[0], PS[1]]), (wim, 1024, [PS[2], PS[3]]))):
                for j in range(8):
                    pi = 8 * hh + j
                    dg = DG[pi % 2]
                    TS(dve, dg, dg.ap, ident, ident.ap[:], src[:, pi:pi + 1], None, ALU.mult, rd=[W])
                    mm(pts[j // 4], psall[:, base + j * 128:base + (j + 1) * 128], ones_f, ones_f.ap[:], dg, dg.ap)
            K.op(act, lambda en: en.activation(out=WR.ap, in_=psall[:, 0:1024], func=AF.Copy), reads=[PS[0], PS[1]], writes=[WR])
            K.op(act, lambda en: en.activation(out=WI.ap, in_=psall[:, 1024:2048], func=AF.Copy), reads=[PS[2], PS[3]], writes=[WI])
            bwre = self.BwRe.ap[:, 8 * hh:8 * hh + 8, :].rearrange("p a c -> p (a c)")
            bwim = self.BwIm.ap[:, 8 * hh:8 * hh + 8, :].rearrange("p a c -> p (a c)")
            TT(dve, self.BwRe, bwre, WR, WR.ap, BR, BR.ap, ALU.mult)
            TT(dve, T3, T3.ap, WI, WI.ap, BI, BI.ap, ALU.mult)
            TT(dve, self.BwRe, bwre, self.BwRe, bwre, T3, T3.ap, ALU.subtract)
            TT(dve, self.BwIm, bwim, WR, WR.ap, BI, BI.ap, ALU.mult)
            TT(dve, T3, T3.ap, WI, WI.ap, BR, BR.ap, ALU.mult)
            TT(dve, self.BwIm, bwim, self.BwIm, bwim, T3, T3.ap, ALU.add)

    def run(self, name, l, b):
        return getattr(self, "run_" + name)(l, b)

    def run_gla(self, l, b):
        e = self.env
        K, pe, act, dve, pool, sp = e["K"], e["pe"], e["act"], e["dve"], e["pool"], e["sp"]
        mm, tr, A, TT, TS, STT, CP, MS = e["mm"], e["tr"], e["A"], e["TT"], e["TS"], e["STT"], e["CP"], e["MS"]
        ps_next, slab_load, HB, w_in = e["ps_next"], e["slab_load"], e["HB"], e["w_in"]
        cv = e["Carver"](keep_x=False)
        QE = cv.get(4 * TB, parts=64); KE = cv.get(4 * TB, parts=64)
        qe = QE.ap.rearrange("p (h t) -> p h t", h=4); ke = KE.ap.rearrange("p (h t) -> p h t", h=4)
        KTM = [cv.get(256) for _ in range(4)]
        LT = [cv.get(256) for _ in range(4)]
        VS = [cv.get(512) for _ in range(2)]
        SG = [cv.get(512) for _ in range(2)]
        YT = [cv.get(512) for _ in range(2)]
        EB = cv.get(512, parts=64); ENB = cv.get(512, parts=64)
        EA = cv.get(256)
        SCM = [cv.get(128) for _ in range(2)]
        DEC = cv.get(32, parts=64)
        dec = DEC.ap.rearrange("p (h c) -> p h c", h=4)
        SSQ = cv.get(8)
        LRT = cv.get(512, parts=16)
        JK = cv.get(128)
        YB = cv.get(4 * TB, r=True)
        yb = YB.ap.rearrange("p (h t) -> p h t", h=4)
        S = self.gla_S
        wt, wv = slab_load(w_in[l][:, O_GQ:O_GQ + 512], 512)
        for h in range(4):
            for which, dst, sc in ((0, qe, 0.125), (1, ke, 1.0)):
                p = ps_next()
                for kt in range(KT):
                    mm(p, p.ap[0:64, :], wt, r_(wv[:, kt, which * 256 + h * 64:which * 256 + (h + 1) * 64]), HB, r_(HB.ap[:, kt, :]),
                       start=(kt == 0), stop=(kt == KT - 1))
                A(AF.Copy, QE if which == 0 else KE, dst[:, h, :], p, p.ap[0:64, :], scale=sc)
        for tt in range(4):
            p = ps_next()
            for kt in range(KT):
                mm(p, p.ap[:, 0:256], HB, r_(HB.ap[:, kt, tt * 128:(tt + 1) * 128]), wt, r_(wv[:, kt, 256:512]),
                   start=(kt == 0), stop=(kt == KT - 1))
            CP(dve, KTM[tt], KTM[tt].ap, p, p.ap[:, 0:256])
        wt2, wv2 = slab_load(w_in[l][:, O_GLR:O_GLR + 16], 16)
        p = ps_next()
        for kt in range(KT):
            mm(p, p.ap[0:16, :], wt2, r_(wv2[:, kt, :]), HB, r_(HB.ap[:, kt, :]), start=(kt == 0), stop=(kt == KT - 1))
        CP(act, LRT, LRT.ap, p, p.ap[0:16, :])
        for tt in range(4):
            p = ps_next()
            mm(p, p.ap[:, 0:256], LRT, LRT.ap[:, tt * 128:(tt + 1) * 128], self.WLR, self.WLR.ap[:], start=True, stop=False)
            mm(p, p.ap[:, 0:256], self.ones1, self.ones1.ap[:], self.BLR, self.BLR.ap[:], start=False, stop=True)
            A(AF.Exp, LT[tt], LT[tt].ap, p, p.ap[:, 0:256], scale=-1.0)
            A(AF.Ln, LT[tt], LT[tt].ap, LT[tt], LT[tt].ap, bias=1.0)
            p2 = ps_next()
            mm(p2, p2.ap[:, 0:256], self.TRI_after, self.TRI_after.ap[:], LT[tt], LT[tt].ap)
            A(AF.Exp, EA, EA.ap, p2, p2.ap[:, 0:256])
            TT(dve, KTM[tt], KTM[tt].ap, KTM[tt], KTM[tt].ap, EA, EA.ap, ALU.mult)
        for h in range(4):
            p = ps_next()
            for tt in range(4):
                mm(p, p.ap[0:64, tt * 128:(tt + 1) * 128], LT[tt], LT[tt].ap[:, h * 64:(h + 1) * 64], self.TRI_incl, self.TRI_incl.ap[:])
            A(AF.Exp, EB, EB.ap, p, p.ap[0:64, :])
            A(AF.Exp, ENB, ENB.ap, p, p.ap[0:64, :], scale=-1.0)
            TT(dve, QE, qe[:, h, :], QE, qe[:, h, :], EB, EB.ap, ALU.mult)
            TT(dve, KE, ke[:, h, :], KE, ke[:, h, :], ENB, ENB.ap, ALU.mult)
            CP(dve, DEC, dec[:, h, :], EB, EB.ap.rearrange("p (c t) -> p c t", t=64)[:, :, 63])
        wtv, wvv = slab_load(w_in[l][:, O_GV:O_GV + 512], 512)
        wto, wvo = slab_load(w_in[l][:, O_GOG:O_GOG + 512], 512)
        for tt in range(4):
            vs, sg, yt = VS[tt % 2], SG[tt % 2], YT[tt % 2]
            ts_ = slice(tt * 128, (tt + 1) * 128)
            p = ps_next()
            for kt in range(KT):
                mm(p, p.ap[:, :], HB, r_(HB.ap[:, kt, ts_]), wtv, r_(wvv[:, kt, :]), start=(kt == 0), stop=(kt == KT - 1))
            CP(act, vs, vs.ap, p, p.ap[:, :])
            p = ps_next()
            for kt in range(KT):
                mm(p, p.ap[:, :], HB, r_(HB.ap[:, kt, ts_]), wto, r_(wvo[:, kt, :]), start=(kt == 0), stop=(kt == KT - 1))
            A(AF.Silu, sg, sg.ap, p, p.ap[:, :])
            TT(dve, sg, sg.ap, sg, sg.ap, self.GNG, self.GNG.ap[:], ALU.mult)
            po = e["PS"][6 + tt % 2]
            for h in range(4):
                hs = slice(h * 128, (h + 1) * 128)
                pa = ps_next()
                mm(pa, pa.ap[:, 0:128], KE, ke[:, h, ts_], QE, qe[:, h, ts_])
                scm = SCM[h % 2]
                TT(dve, scm, scm.ap, pa, pa.ap[:, 0:128], self.CMASK, self.CMASK.ap[:], ALU.mult)
                mm(po, po.ap[:, hs], scm, scm.ap, vs, vs.ap[:, hs], start=True, stop=False)
                for c in range(2):
                    cs = slice(tt * 128 + c * 64, tt * 128 + (c + 1) * 64)
                    rows = slice(c * 64, (c + 1) * 64)
                    mm(po, po.ap[rows, hs], QE, qe[:, h, cs], S, S.ap[:, h, :], start=False, stop=True)
                    pk = ps_next()
                    mm(pk, pk.ap[0:64, 0:128], KTM[tt], KTM[tt].ap[rows, h * 64:(h + 1) * 64], vs, vs.ap[rows, hs])
                    ch = tt * 2 + c
                    STT(S, S.ap[:, h, :], S, S.ap[:, h, :], dec[:, h, ch:ch + 1], pk, pk.ap[0:64, 0:128], ALU.mult, ALU.add, rd=[DEC])
            for h in range(4):
                hs = slice(h * 128, (h + 1) * 128)
                A(AF.Square, JK, JK.ap, po, po.ap[:, hs], accum=SSQ.ap[:, h:h + 1], wr=[SSQ])
            A(AF.Sqrt, SSQ, SSQ.ap[:, 4:8], SSQ, SSQ.ap[:, 0:4], scale=1.0 / 128.0, bias=e["eps_c"].ap[:], rd=[e["eps_c"]])
            K.op(dve, lambda en: en.reciprocal(out=SSQ.ap[:, 4:8], in_=SSQ.ap[:, 4:8]), reads=[SSQ], writes=[SSQ])
            for h in range(4):
                hs = slice(h * 128, (h + 1) * 128)
                STT(yt, yt.ap[:, hs], po, po.ap[:, hs], SSQ.ap[:, 4 + h:5 + h], sg, sg.ap[:, hs], ALU.mult, ALU.mult, rd=[SSQ])
            pt = ps_next()
            for h in range(4):
                tr(pt, pt.ap[:, h * 128:(h + 1) * 128], yt, yt.ap[:, h * 128:(h + 1) * 128])
            CP(act, YB, r_(yb[:, :, ts_]), pt, pt.ap[:, :].rearrange("p (h t) -> p h t", h=4))
        return Tk(yb) if False else _view(YB, yb)


def _s5_run(self, l, b):
    e = self.env
    K, pe, act, dve, pool, sp = e["K"], e["pe"], e["act"], e["dve"], e["pool"], e["sp"]
    mm, A, TT, TS, STT, CP, MS = e["mm"], e["A"], e["TT"], e["TS"], e["STT"], e["CP"], e["MS"]
    ps_next, slab_load, HB, w_in, PS, psall, PRM = e["ps_next"], e["slab_load"], e["HB"], e["w_in"], e["PS"], e["psall"], e["PRM"]
    cv = e["Carver"](keep_x=False)
    UT = cv.get(4 * TB); ut = UT.ap.rearrange("p (c t) -> p c t", c=4)
    YS = cv.get(4 * TB); ys = YS.ap.rearrange("p (c t) -> p c t", c=4)
    GIN = cv.get(2048); G = cv.get(2048); H = cv.get(2048); TMP = cv.get(1024); TMP2 = cv.get(1024)
    ginre, ginim = GIN.ap[:, 0:1024], GIN.ap[:, 1024:2048]
    gre, gim = G.ap[:, 0:1024], G.ap[:, 1024:2048]
    hre, him = H.ap[:, 0:1024], H.ap[:, 1024:2048]
    v3 = lambda ap: ap.rearrange("p (a t) -> p a t", a=16)
    cosf = self.COS.ap.rearrange("p a t -> p (a t)"); sinf = self.SIN.ap.rearrange("p a t -> p (a t)")
    rbf = self.Rb.ap.rearrange("p a t -> p (a t)")
    YG = cv.get(4 * TB, r=True); yg = YG.ap.rearrange("p (c t) -> p c t", c=4)
    YB = cv.get(4 * TB, r=True); yb = YB.ap.rearrange("p (c t) -> p c t", c=4)
    wt, wv = slab_load(w_in[l][:, O_S5:O_S5 + 512], 512)
    for ct in range(4):
        p = ps_next()
        for kt in range(KT):
            mm(p, p.ap[:, :], wt, r_(wv[:, kt, ct * 128:(ct + 1) * 128]), HB, r_(HB.ap[:, kt, :]), start=(kt == 0), stop=(kt == KT - 1))
        CP(act if ct % 2 == 0 else dve, UT, ut[:, ct, :], p, p.ap[:, :])
    bure, buim = psall[:, 0:1024], psall[:, 1024:2048]
    PRE, PIM = [PS[0], PS[1]], [PS[2], PS[3]]
    for sc in range(TB // 64 if "noloop" not in S5DBG else 0):
        cs = slice(sc * 64, (sc + 1) * 64)
        for pi in range(16):
            a = pi // 4
            mm(PS[pi // 8], bure[:, pi * 64:(pi + 1) * 64], self.BwRe, self.BwRe.ap[:, pi, :], UT, ut[:, a, cs])
            mm(PS[2 + pi // 8], buim[:, pi * 64:(pi + 1) * 64], self.BwIm, self.BwIm.ap[:, pi, :], UT, ut[:, a, cs])

        if "st1" in S5DBG:
            continue
        def tt2(o_t, o_ap, a_ts, a_ap, b_t, b_ap, op):
            K.op(dve, lambda en: en.tensor_tensor(out=o_ap, in0=a_ap, in1=b_ap, op=op), reads=list(a_ts) + [b_t], writes=[o_t])
        tt2(GIN, ginre, PRE, bure, self.COS, cosf, ALU.mult)
        tt2(TMP, TMP.ap, PIM, buim, self.SIN, sinf, ALU.mult)
        TT(dve, GIN, ginre, GIN, ginre, TMP, TMP.ap, ALU.add)
        tt2(GIN, ginim, PIM, buim, self.COS, cosf, ALU.mult)
        tt2(TMP, TMP.ap, PRE, bure, self.SIN, sinf, ALU.mult)
        TT(dve, GIN, ginim, GIN, ginim, TMP, TMP.ap, ALU.subtract)
        if "st2" in S5DBG:
            continue
        for ri, gap in ((0, ginre), (1, ginim)):
            TT(dve, TMP, TMP.ap[:, 0:16], self.CAR, self.CAR.ap[:, ri, :], self.rcol, self.rcol.ap, ALU.mult)
            TT(dve, GIN, v3(gap)[:, :, 0], GIN, v3(gap)[:, :, 0], TMP, TMP.ap[:, 0:16], ALU.add)
        if "st3" in S5DBG:
            continue
        for gap, oap in ((ginre, gre), (ginim, gim)):
            K.op(dve, lambda en, gap=gap, oap=oap: en.tensor_tensor_scan(out=oap, data0=rbf, data1=gap, initial=0.0, op0=ALU.mult, op1=ALU.add),
                 reads=[GIN, self.Rb], writes=[G])
        if "st4" in S5DBG:
            continue
        TT(dve, H, hre, G, gre, self.COS, cosf, ALU.mult)
        TT(dve, TMP2, TMP2.ap, G, gim, self.SIN, sinf, ALU.mult)
        TT(dve, H, hre, H, hre, TMP2, TMP2.ap, ALU.subtract)
        TT(dve, H, him, G, gim, self.COS, cosf, ALU.mult)
        TT(dve, TMP2, TMP2.ap, G, gre, self.SIN, sinf, ALU.mult)
        TT(dve, H, him, H, him, TMP2, TMP2.ap, ALU.add)
        CP(dve, self.CAR, self.CAR.ap[:, 0, :], H, v3(hre)[:, :, 63])
        CP(dve, self.CAR, self.CAR.ap[:, 1, :], H, v3(him)[:, :, 63])
        py = PS[4 + sc % 2]
        for ct in range(4):
            for half in range(2):
                o = py.ap[64 * half:64 * half + 64, ct * 64:(ct + 1) * 64]
                for ql in range(2):
                    pi = 4 * ct + 2 * half + ql
                    mm(py, o, self.CRe, self.CRe.ap[:, pi, :], H, v3(hre)[:, pi, :], start=(ql == 0), stop=False)
                    mm(py, o, self.CImN, self.CImN.ap[:, pi, :], H, v3(him)[:, pi, :], start=False, stop=(ql == 1))
        if "st6" in S5DBG:
            continue
        for ct in range(4):
            STT(YS, ys[:, ct, cs], UT, ut[:, ct, cs], self.dcol.ap[:, ct:ct + 1], py, py.ap[:, ct * 64:(ct + 1) * 64],
                ALU.mult, ALU.add, rd=[self.dcol])
    T1 = Tk(GIN.ap)
    A(AF.Square, T1, T1.ap, YS, YS.ap)
    TS(dve, T1, T1.ap, T1, T1.ap, 0.044715, 1.0, ALU.mult, ALU.add)
    TT(dve, T1, T1.ap, T1, T1.ap, YS, YS.ap, ALU.mult)
    A(AF.Sigmoid, T1, T1.ap, T1, T1.ap, scale=1.5957691216057308)
    TT(dve, YG, r_(YG.ap), T1, T1.ap, YS, YS.ap, ALU.mult)
    wa, wva = slab_load(PRM["s5_w_glu"][l][:, 0:512], 512)
    wb, wvb = slab_load(PRM["s5_w_glu"][l][:, 512:1024], 512)
    for j in range(4):
        pa, pb = ps_next(), ps_next()
        for kt in range(4):
            mm(pa, pa.ap[:, :], wa, r_(wva[:, kt, j * 128:(j + 1) * 128]), YG, r_(yg[:, kt, :]), start=(kt == 0), stop=(kt == 3))
        for kt in range(4):
            mm(pb, pb.ap[:, :], wb, r_(wvb[:, kt, j * 128:(j + 1) * 128]), YG, r_(yg[:, kt, :]), start=(kt == 0), stop=(kt == 3))
        g = self.GT[j % 2]
        A(AF.Sigmoid, g, g.ap, pb, pb.ap[:, :])
        TT(dve, YB, r_(yb[:, j, :]), g, g.ap, pa, pa.ap[:, :], ALU.mult)
    return _view(YB, yb)


Mixers.run_s5 = _s5_run


def _gdn_run(self, l, b):
    e = self.env
    K, pe, act, dve, pool, sp = e["K"], e["pe"], e["act"], e["dve"], e["pool"], e["sp"]
    mm, tr, A, TT, TS, STT, CP, MS = e["mm"], e["tr"], e["A"], e["TT"], e["TS"], e["STT"], e["CP"], e["MS"]
    ps_next, slab_load, HB, w_in, PS = e["ps_next"], e["slab_load"], e["HB"], e["w_in"], e["PS"]
    ident, ones_f, eps_c = e["ident"], e["ones_f"], e["eps_c"]
    cv = e["Carver"](keep_x=False)
    QK = cv.get(8 * TB); qk = QK.ap.rearrange("p (c t) -> p c t", c=8)
    VTM = cv.get(4 * 512); vtm = VTM.ap.rearrange("p (t c) -> p t c", t=4)
    SM = [cv.get(64) for _ in range(4)]
    YB = cv.get(4 * TB, r=True); yb = YB.ap.rearrange("p (h t) -> p h t", h=4)
    mk = cv.mark()
    XC = cv.get(520); VB = cv.get(512); SQ = cv.get(512); RSs = cv.get(512); GBC = cv.get(128)
    S, hist, cw = self.gdn_S, self.HIST.ap, self.CW.ap
    for g3 in range(3):
        wt, wv = slab_load(w_in[l][:, O_DQKV + g3 * 512:O_DQKV + (g3 + 1) * 512], 512)
        for c4 in range(4):
            ct = g3 * 4 + c4
            p = ps_next()
            for kt in range(KT):
                mm(p, p.ap[:, :], wt, r_(wv[:, kt, c4 * 128:(c4 + 1) * 128]), HB, r_(HB.ap[:, kt, :]), start=(kt == 0), stop=(kt == KT - 1))
            CP(act, XC, XC.ap[:, 3:515], p, p.ap[:, :])
            CP(dve, XC, XC.ap[:, 0:3], self.HIST, hist[:, ct, :])
            dt_, dap = (QK, qk[:, ct, :]) if g3 < 2 else (VB, VB.ap)
            TS(dve, dt_, dap, XC, XC.ap[:, 3:515], cw[:, ct, 3:4], None, ALU.mult, rd=[self.CW])
            for j in (2, 1, 0):
                STT(dt_, dap, XC, XC.ap[:, j:j + 512], cw[:, ct, j:j + 1], dt_, dap, ALU.mult, ALU.add, rd=[self.CW])
            CP(dve, self.HIST, hist[:, ct, :], XC, XC.ap[:, 512:515])
            A(AF.Silu, dt_, dap, dt_, dap)
            if g3 == 2:
                pt = ps_next()
                for tt in range(4):
                    tr(pt, pt.ap[:, tt * 128:(tt + 1) * 128], VB, VB.ap[:, tt * 128:(tt + 1) * 128])
                CP(act, VTM, vtm[:, :, c4 * 128:(c4 + 1) * 128], pt, pt.ap[:, :].rearrange("p (t c) -> p t c", t=4))
    if "g1" in S5DBG:
        return _view(YB, yb)
    for ct in range(8):
        A(AF.Square, SQ, SQ.ap, QK, qk[:, ct, :])
        p = ps_next()
        mm(p, p.ap[:, :], ones_f, ones_f.ap[:], SQ, SQ.ap)
        A(AF.Sqrt, RSs, RSs.ap, p, p.ap[:, :], bias=eps_c.ap[:], rd=[eps_c])
        K.op(dve, lambda en: en.reciprocal(out=RSs.ap, in_=RSs.ap), reads=[RSs], writes=[RSs])
        if ct < 4:
            STT(QK, qk[:, ct, :], QK, qk[:, ct, :], 128.0 ** -0.5, RSs, RSs.ap, ALU.mult, ALU.mult)
        else:
            TT(dve, QK, qk[:, ct, :], QK, qk[:, ct, :], RSs, RSs.ap, ALU.mult)
    if "g2" in S5DBG:
        return _view(YB, yb)
    wt2, wv2 = slab_load(w_in[l][:, O_DBETA:O_DBETA + 8], 8)
    for tt in range(4):
        sm = SM[tt]
        p = ps_next()
        for kt in range(KT):
            mm(p, p.ap[:, 0:8], HB, r_(HB.ap[:, kt, tt * 128:(tt + 1) * 128]), wt2, r_(wv2[:, kt, :]), start=(kt == 0), stop=(kt == KT - 1))
        A(AF.Sigmoid, sm, sm.ap[:, 0:4], p, p.ap[:, 0:4])
        TT(dve, sm, sm.ap[:, 52:56], p, p.ap[:, 4:8], self.AB, self.AB.ap[:, 4:8], ALU.add)
        A(AF.Exp, sm, sm.ap[:, 52:56], sm, sm.ap[:, 52:56])
        A(AF.Ln, sm, sm.ap[:, 52:56], sm, sm.ap[:, 52:56], bias=1.0)
        TT(dve, sm, sm.ap[:, 4:8], sm, sm.ap[:, 52:56], self.NEGA, self.NEGA.ap[:], ALU.mult)
        p2 = ps_next()
        mm(p2, p2.ap[:, 0:4], self.AFTER01, self.AFTER01.ap[:], sm, sm.ap[:, 4:8])
        mm(p2, p2.ap[:, 4:8], self.CMASK, self.CMASK.ap[:], sm, sm.ap[:, 4:8])
        A(AF.Exp, sm, sm.ap[:, 8:16], p2, p2.ap[:, 0:8])
        TT(dve, sm, sm.ap[:, 16:20], sm, sm.ap[:, 0:4], sm, sm.ap[:, 12:16], ALU.mult)
        p3 = ps_next()
        for h in range(4):
            TS(dve, GBC, GBC.ap, ones_f, ones_f.ap[:], sm.ap[:, 4 + h:5 + h], None, ALU.mult, rd=[sm])
            mm(p3, p3.ap[:, 2 * h:2 * h + 2], GBC, GBC.ap, self.CH01, self.CH01.ap[:])
        A(AF.Exp, sm, sm.ap[:, 20:28], p3, p3.ap[:, 0:8])
    K.barrier()
    if "g3" in S5DBG:
        return _view(YB, yb)
    cv.reset(mk)
    KTMt = cv.get(512); O = cv.get(512); YT = cv.get(512); SGt = cv.get(512); JK = cv.get(128); SSQ = cv.get(8)
    SLOTS = [[cv.get(128) for _ in range(11)] + [cv.get(256) for _ in range(3)] for _ in range(2)]
    wto, wvo = slab_load(w_in[l][:, O_DOG:O_DOG + 512], 512)
    f1 = lambda t: t.ap
    pcopy = [0]

    def evac(dst, src_t, src_ap):
        pcopy[0] += 1
        CP(act if pcopy[0] % 2 else dve, dst, dst.ap[:, 0:src_ap.shape[1]] if False else dst.ap, src_t, src_ap)
    for tt in range(4):
        sm = SM[tt]
        cs = slice(tt * 128, (tt + 1) * 128)
        pt = ps_next()
        for h in range(4):
            tr(pt, pt.ap[:, h * 128:(h + 1) * 128], QK, qk[:, 4 + h, cs])
        CP(act, KTMt, KTMt.ap, pt, pt.ap[:, :])
        def head_chain(tt, h, par, slots, sm=sm, cs=cs):
            Sh = self.gdn_Sh[h]
            hs = slice(h * 128, (h + 1) * 128)
            beta, gcol, egl, egam, be = (sm.ap[:, h:h + 1], sm.ap[:, 4 + h:5 + h], sm.ap[:, 8 + h:9 + h],
                                         sm.ap[:, 12 + h:13 + h], sm.ap[:, 16 + h:17 + h])
            s1, s2, s3, s4, s5, s6, s7, s8, s9, s10, s11, r1, r2, r3 = slots
            UG, EL, ATT, Lm, Ld, Lo, Md, ATT_T, Ld2, Y0, Y1 = s1, s2, s3, s4, s5, s6, s7, s8, s9, s10, s11
            TS(dve, UG, UG.ap, self.UPPER01, self.UPPER01.ap[:], gcol, None, ALU.mult, rd=[sm])
            pD = ps_next()
            mm(pD, pD.ap[:, 0:128], UG, UG.ap, self.MASKGT, self.MASKGT.ap[:], start=True, stop=False)
            mm(pD, pD.ap[:, 0:128], ident, ident.ap[:], self.NEGM, self.NEGM.ap[:], start=False, stop=True)
            A(AF.Exp, EL, EL.ap, pD, pD.ap[:, 0:128])
            yield
            pG = ps_next()
            mm(pG, pG.ap[:, 0:128], QK, qk[:, 4 + h, cs], QK, qk[:, 4 + h, cs])
            mm(pG, pG.ap[:, 128:256], QK, qk[:, h, cs], QK, qk[:, 4 + h, cs])
            TT(dve, ATT, ATT.ap, pG, pG.ap[:, 128:256], EL, EL.ap, ALU.mult)
            STT(Lm, Lm.ap, pG, pG.ap[:, 0:128], beta, EL, EL.ap, ALU.mult, ALU.mult, rd=[sm])
            TT(dve, Lm, Lm.ap, Lm, Lm.ap, self.AFTER01, self.AFTER01.ap[:], ALU.mult)
            TT(dve, Ld, Ld.ap, Lm, Lm.ap, self.BD16, self.BD16.ap[:], ALU.mult)
            TT(dve, Lo, Lo.ap, Lm, Lm.ap, Ld, Ld.ap, ALU.subtract)
            yield
            pT = ps_next()
            tr(pT, pT.ap[:, 0:128], Ld, Ld.ap)
            tr(pT, pT.ap[:, 128:256], ATT, ATT.ap)
            CP(act, Md, Md.ap, pT, pT.ap[:, 0:128])
            TT(dve, Y0, Y0.ap, ident, ident.ap[:], Md, Md.ap, ALU.subtract)
            CP(act, ATT_T, ATT_T.ap, pT, pT.ap[:, 128:256])
            yield
            Md2, Md4, Ld4, Ld8 = s1, s2, s4, s3
            p = ps_next()
            mm(p, p.ap[:, 0:128], Ld, Ld.ap, Md, Md.ap)
            mm(p, p.ap[:, 128:256], Md, Md.ap, Ld, Ld.ap)
            CP(act, Md2, Md2.ap, p, p.ap[:, 0:128]); CP(dve, Ld2, Ld2.ap, p, p.ap[:, 128:256])
            yield
            p = ps_next()
            mm(p, p.ap[:, 0:128], Ld2, Ld2.ap, Md2, Md2.ap)
            mm(p, p.ap[:, 128:256], Md2, Md2.ap, Ld2, Ld2.ap)
            CP(act, Md4, Md4.ap, p, p.ap[:, 0:128]); CP(dve, Ld4, Ld4.ap, p, p.ap[:, 128:256])
            yield
            p = ps_next()
            mm(p, p.ap[:, 0:128], Md4, Md4.ap, Ld4, Ld4.ap)
            CP(act, Ld8, Ld8.ap, p, p.ap[:, 0:128])
            yield
            Ya, Yb = Y0, Y1
            for Lp in (Ld2, Ld4, Ld8):
                p = ps_next()
                mm(p, p.ap[:, 0:128], Lp, Lp.ap, Ya, Ya.ap)
                TT(dve, Yb, Yb.ap, Ya, Ya.ap, p, p.ap[:, 0:128], ALU.add)
                yield
                Ya, Yb = Yb, Ya
            DinvT = Ya
            Pm = s1
            p = ps_next()
            mm(p, p.ap[:, 0:128], Lo, Lo.ap, DinvT, DinvT.ap)
            CP(act, Pm, Pm.ap, p, p.ap[:, 0:128])
            yield
            R, Z0, A1 = r1, r2, r3
            TS(dve, R, R.ap[:, 0:128], VTM, vtm[:, tt, hs], beta, None, ALU.mult, rd=[sm])
            TS(dve, R, R.ap[:, 128:256], KTMt, KTMt.ap[:, hs], be, None, ALU.mult, rd=[sm])
            p = ps_next()
            mm(p, p.ap[:, 0:256], DinvT, DinvT.ap, R, R.ap)
            CP(act, Z0, Z0.ap, p, p.ap[:, 0:256])
            yield
            p = ps_next()
            mm(p, p.ap[:, 0:256], Pm, Pm.ap, Z0, Z0.ap)
            CP(act, A1, A1.ap, p, p.ap[:, 0:256])
            yield
            Z1 = r1
            p = ps_next()
            mm(p, p.ap[:, 0:256], Pm, Pm.ap, A1, A1.ap)
            TT(dve, Z1, Z1.ap, Z0, Z0.ap, p, p.ap[:, 0:256], ALU.add)
            yield
            Z2 = r2
            p = ps_next()
            mm(p, p.ap[:, 0:256], Pm, Pm.ap, Z1, Z1.ap)
            STT(Z2, Z2.ap, p, p.ap[:, 0:256], -1.0, Z1, Z1.ap, ALU.mult, ALU.add)
            yield
            WT, KD, VN, TMPI = s2, s3, s4, s5
            p = ps_next()
            tr(p, p.ap[:, 0:128], Z2, Z2.ap[:, 128:256])
            CP(act, WT, WT.ap, p, p.ap[:, 0:128])
            TS(dve, KD, KD.ap, KTMt, KTMt.ap[:, hs], egl, None, ALU.mult, rd=[sm])
            yield
            pb = PS[6 + par]
            for c in range(2):
                rows = slice(c * 64, (c + 1) * 64)
                ccs = slice(tt * 128 + c * 64, tt * 128 + (c + 1) * 64)
                mm(pb, pb.ap[rows, 0:128], WT, WT.ap[:, c * 64:(c + 1) * 64], Sh, Sh.ap)
                STT(VN, VN.ap[rows, :], pb, pb.ap[rows, 0:128], -1.0, Z2, Z2.ap[rows, 0:128], ALU.mult, ALU.add)
                mm(pb, pb.ap[rows, 128:256], QK, qk[:, h, ccs], Sh, Sh.ap)
                mm(pb, pb.ap[rows, 256:384], ATT_T, ATT_T.ap[rows, c * 64:(c + 1) * 64], VN, VN.ap[rows, :])
                pS = ps_next()
                mm(pS, pS.ap[:, 0:128], KD, KD.ap[rows, :], VN, VN.ap[rows, :])
                dcol = sm.ap[:, 20 + 2 * h + c:21 + 2 * h + c]
                STT(Sh, Sh.ap, Sh, Sh.ap, dcol, pS, pS.ap[:, 0:128], ALU.mult, ALU.add, rd=[sm])
                yield
            A(AF.Copy, TMPI, TMPI.ap, pb, pb.ap[:, 128:256], scale=egam, rd=[sm])
            TT(dve, O, O.ap[:, hs], TMPI, TMPI.ap, pb, pb.ap[:, 256:384], ALU.add)

        for pair in ((0, 1), (2, 3)):
            gens = [head_chain(tt, h, i, SLOTS[i]) for i, h in enumerate(pair)]
            live = list(gens)
            while live:
                for g_ in list(live):
                    try:
                        next(g_)
                    except StopIteration:
                        live.remove(g_)
        p = ps_next()
        for kt in range(KT):
            mm(p, p.ap[:, :], HB, r_(HB.ap[:, kt, cs]), wto, r_(wvo[:, kt, :]), start=(kt == 0), stop=(kt == KT - 1))
        A(AF.Silu, SGt, SGt.ap, p, p.ap[:, :])
        TT(dve, SGt, SGt.ap, SGt, SGt.ap, self.GNG2, self.GNG2.ap[:], ALU.mult)
        for h in range(4):
            hs = slice(h * 128, (h + 1) * 128)
            A(AF.Square, JK, JK.ap, O, O.ap[:, hs], accum=SSQ.ap[:, h:h + 1], wr=[SSQ])
        A(AF.Sqrt, SSQ, SSQ.ap[:, 4:8], SSQ, SSQ.ap[:, 0:4], scale=1.0 / 128.0, bias=eps_c.ap[:], rd=[eps_c])
        K.op(dve, lambda en: en.reciprocal(out=SSQ.ap[:, 4:8], in_=SSQ.ap[:, 4:8]), reads=[SSQ], writes=[SSQ])
        for h in range(4):
            hs = slice(h * 128, (h + 1) * 128)
            STT(YT, YT.ap[:, hs], O, O.ap[:, hs], SSQ.ap[:, 4 + h:5 + h], SGt, SGt.ap[:, hs], ALU.mult, ALU.mult, rd=[SSQ])
        pt = ps_next()
        for h in range(4):
            tr(pt, pt.ap[:, h * 128:(h + 1) * 128], YT, YT.ap[:, h * 128:(h + 1) * 128])
        CP(act, YB, r_(yb[:, :, cs]), pt, pt.ap[:, :].rearrange("p (h t) -> p h t", h=4))
    return _view(YB, yb)


Mixers.run_gdn = _gdn_run


def _view(t, ap):
    t.ap = ap
    return t


def make_mixers(env):
    return Mixers(env)


_CACHE = {}


def _lay_cols(v, n):
    return np.ascontiguousarray(np.asarray(v, np.float32).reshape(n, 128).T)


RUN_DEPTH = DEPTH
RUN_USE = ("gla", "s5", "gdn", "ffn")


def kernel(**inp):
    depth = RUN_DEPTH
    key = (RUN_USE, depth)
    if key not in _CACHE:
        _CACHE[key] = build_program(depth, RUN_USE)
    nc = _CACHE[key]
    f = lambda a: np.ascontiguousarray(np.asarray(a, np.float32)[:depth])
    shared = {
        "w_ada": f(inp["w_ada"]), "w_in": f(inp["w_in"]), "w_out": f(inp["w_out"]),
        "w_branch_gla": f(inp["w_branch_gla"]), "w_branch_s5": f(inp["w_branch_s5"]),
        "w_branch_gdn": f(inp["w_branch_gdn"]), "w_ffn_in": f(inp["w_ffn_in"]), "w_ffn_out": f(inp["w_ffn_out"]),
        "b_ada": np.stack([_lay_cols(inp["b_ada"][l], 48) for l in range(depth)]),
        "norm1_g": np.stack([_lay_cols(inp["norm1_g"][l], KT) for l in range(depth)]),
        "norm2_g": np.stack([_lay_cols(inp["norm2_g"][l], KT) for l in range(depth)]),
        "final_g": _lay_cols(inp["final_g"], KT),
    }
    shared.update({k: v[:depth] for k, v in layout_mixer_params(inp).items()})
    x = np.ascontiguousarray(np.asarray(inp["x"], np.float32))
    c = np.ascontiguousarray(np.asarray(inp["c"], np.float32))
    in_maps = []
    for core in range(8):
        bi = core % 4
        m = dict(shared)
        m["x"] = x[bi]
        m["c"] = _lay_cols(c[bi], KT)
        in_maps.append(m)
    res = run_bass_kernel_spmd(nc, in_maps, core_ids=list(range(8)))
    out = np.stack([np.asarray(res.results[bi]["out"], np.float32) for bi in range(4)], axis=0)
    return out
```

```python
import contextlib
import numpy as np
import concourse.bass as bass
import concourse.mybir as mybir
from concourse.bass_utils import run_bass_kernel_spmd

F32 = mybir.dt.float32
F32R = mybir.dt.float32r
AF = mybir.ActivationFunctionType
ALU = mybir.AluOpType
AX = mybir.AxisListType

D = 1024
SEQ = 4096
DEPTH = 4
TB = 512
NB = SEQ // TB
KT = D // 128
EPS = 1e-6
DFF = 2816
NFF = DFF // 128
D_IN = 7192
O_GQ, O_GK, O_GV, O_GLR, O_GOG = 0, 256, 512, 1024, 1040
O_S5 = 1552
O_DQKV, O_DBETA, O_DA, O_DOG = 2064, 3600, 3604, 3608
O_ZG = 4120
PI2 = 6.283185307179586


class Tk:
    __slots__ = ("ap", "w", "r", "psum")

    def __init__(self, ap, psum=False):
        self.ap = ap
        self.w = None
        self.r = {}
        self.psum = psum


class Eng:
    def __init__(self, name, obj, sem, sync_self):
        self.name = name
        self.obj = obj
        self.sem = sem
        self.count = 0
        self.seen = {}
        self.sync_self = sync_self
        self.lanes = []
        self.k = 0


SYNC_SELF = True


class KB:
    def __init__(self, nc, es):
        self.nc = nc
        self.es = es
        self.sems = {}
        mk = lambda n: es.enter_context(nc.semaphore(n))
        self.pe = Eng("pe", nc.tensor, mk("s_pe"), False)
        self.act = Eng("act", nc.scalar, mk("s_act"), SYNC_SELF)
        self.dve = Eng("dve", nc.vector, mk("s_dve"), SYNC_SELF)
        self.pool = Eng("pool", nc.gpsimd, mk("s_pool"), True)
        self.sp = Eng("sp", nc.sync, mk("s_sp"), False)
        for q, nl in ((self.sp, 8), (self.pool, 6)):
            for i in range(nl):
                q.lanes.append([mk(f"s_{q.name}_l{i}"), 0])
        self.n_sb = 0

    def sb(self, shape, dtype=F32, name=None):
        self.n_sb += 1
        t = self.es.enter_context(self.nc.sbuf_tensor(name or f"sb{self.n_sb}", list(shape), dtype))
        return t

    def psum(self, name):
        return self.es.enter_context(self.nc.psum_tensor(name, [128, 512], F32))

    def _deps(self, reads, writes, eng=None):
        need = {}

        def add(tok):
            s, v = tok
            k = id(s)
            if k not in need or need[k][1] < v:
                need[k] = (s, v)
        for t in reads:
            if t.w is not None:
                add(t.w)
            if t.psum:
                for tok in t.r.values():
                    if eng is None or tok[0] is not eng.sem:
                        add(tok)
        for t in writes:
            if t.w is not None:
                add(t.w)
            for tok in t.r.values():
                add(tok)
        return need

    def _wait(self, eng, need):
        for k, (s, v) in need.items():
            if s is eng.sem and not eng.sync_self:
                continue
            if eng.seen.get(k, 0) < v:
                eng.obj.wait_ge(s, v)
                eng.seen[k] = v

    def _mark(self, tok, reads, writes):
        k = id(tok[0])
        for t in reads:
            t.r[k] = tok
        for t in writes:
            t.w = tok
            t.r = {}

    def op(self, eng, emit, reads=(), writes=()):
        self._wait(eng, self._deps(reads, writes, eng))
        ins = emit(eng.obj)
        eng.count += 1
        ins.then_inc(eng.sem, 1)
        eng.seen[id(eng.sem)] = max(eng.seen.get(id(eng.sem), 0), 0)
        self._mark((eng.sem, eng.count), reads, writes)

    def dma(self, q, out_ap, in_ap, reads=(), writes=()):
        lane = q.lanes[q.k % len(q.lanes)]
        q.k += 1
        need = self._deps(reads, writes)
        if lane[1] > 0:
            k = id(lane[0])
            need[k] = (lane[0], lane[1])
        self._wait(q, need)
        ins = q.obj.dma_start(out=out_ap, in_=in_ap)
        lane[1] += 16
        ins.then_inc(lane[0], 16)
        self._mark((lane[0], lane[1]), reads, writes)

    def barrier(self):
        engs = [self.pe, self.act, self.dve, self.pool, self.sp]
        toks = []
        for e in engs:
            if e.count > 0:
                toks.append((e.sem, e.count))
            for l in e.lanes:
                if l[1] > 0:
                    toks.append((l[0], l[1]))
        for e in engs:
            need = {id(s): (s, v) for s, v in toks if s is not e.sem}
            self._wait(e, need)

    def wait_all(self, eng, tiles):
        need = {}
        for t in tiles:
            if t.w is not None:
                s, v = t.w
                if id(s) not in need or need[id(s)][1] < v:
                    need[id(s)] = (s, v)
        self._wait(eng, need)


def r_(ap):
    return ap.bitcast(F32R)


def build_program(depth=DEPTH, use=("gla", "s5", "gdn", "ffn")):
    nc = bass.Bass("TRN2", target_bir_lowering=False)
    dr = lambda n, s, k="ExternalInput", dt=F32: nc.dram_tensor(n, list(s), dt, kind=k).ap()
    x_in = dr("x", [SEQ, D])
    c_in = dr("c", [128, KT])
    w_ada = dr("w_ada", [depth, D, 6 * D])
    b_ada = dr("b_ada", [depth, 128, 48])
    n1g = dr("norm1_g", [depth, 128, KT])
    n2g = dr("norm2_g", [depth, 128, KT])
    fng = dr("final_g", [128, KT])
    w_in = dr("w_in", [depth, D, D_IN])
    w_out = dr("w_out", [depth, D, D])
    w_br = {"gla": dr("w_branch_gla", [depth, 512, D]), "s5": dr("w_branch_s5", [depth, 512, D]),
            "gdn": dr("w_branch_gdn", [depth, 512, D])}
    w_fi = dr("w_ffn_in", [depth, D, 2 * DFF])
    w_fo = dr("w_ffn_out", [depth, DFF, D])
    PRM = declare_mixer_params(dr, depth)
    y_out = dr("out", [SEQ, D], "ExternalOutput")
    xres = dr("xres", [D, SEQ], "Internal")

    with contextlib.ExitStack() as es:
        K = KB(nc, es)
        pe, act, dve, pool, sp = K.pe, K.act, K.dve, K.pool, K.sp
        psall = es.enter_context(nc.psum_tensor("psall", [128, 4096], F32))
        PS = [Tk(psall[:, 512 * i:512 * (i + 1)], psum=True) for i in range(8)]
        ps_rr = [0]

        def ps_next():
            ps_rr[0] = (ps_rr[0] + 1) % 6
            return PS[ps_rr[0]]

        def mm(o_t, o_ap, l_t, l_ap, r_t, r_ap, start=True, stop=True):
            K.op(pe, lambda e: e.matmul(o_ap, l_ap, r_ap, start=start, stop=stop),
                 reads=[l_t, r_t], writes=[o_t])

        def tr(o_t, o_ap, i_t, i_ap):
            K.op(pe, lambda e: e.transpose(out=o_ap, in_=i_ap, identity=ident.ap[:]),
                 reads=[i_t, ident], writes=[o_t])

        def A(func, o_t, o_ap, i_t, i_ap, scale=1.0, bias=None, rd=(), accum=None, wr=()):
            kw = {}
            if bias is not None:
                kw["bias"] = bias
            if accum is not None:
                kw["accum_out"] = accum
            K.op(act, lambda e: e.activation(out=o_ap, in_=i_ap, func=func, scale=scale, **kw),
                 reads=[i_t] + list(rd), writes=[o_t] + list(wr))

        def TT(eng, o_t, o_ap, a_t, a_ap, b_t, b_ap, op):
            K.op(eng, lambda e: e.tensor_tensor(out=o_ap, in0=a_ap, in1=b_ap, op=op),
                 reads=[a_t, b_t], writes=[o_t])

        def TS(eng, o_t, o_ap, a_t, a_ap, s1, s2, op0, op1=None, rd=()):
            if op1 is None:
                K.op(eng, lambda e: e.tensor_scalar(out=o_ap, in0=a_ap, scalar1=s1, scalar2=None, op0=op0),
                     reads=[a_t] + list(rd), writes=[o_t])
            else:
                K.op(eng, lambda e: e.tensor_scalar(out=o_ap, in0=a_ap, scalar1=s1, scalar2=s2, op0=op0, op1=op1),
                     reads=[a_t] + list(rd), writes=[o_t])

        def STT(o_t, o_ap, a_t, a_ap, scalar, b_t, b_ap, op0, op1, rd=()):
            K.op(dve, lambda e: e.scalar_tensor_tensor(out=o_ap, in0=a_ap, scalar=scalar, in1=b_ap, op0=op0, op1=op1),
                 reads=[a_t, b_t] + list(rd), writes=[o_t])

        def CP(eng, o_t, o_ap, i_t, i_ap):
            if eng is act:
                A(AF.Copy, o_t, o_ap, i_t, i_ap)
            else:
                K.op(eng, lambda e: e.tensor_copy(out=o_ap, in_=i_ap), reads=[i_t], writes=[o_t])

        def MS(eng, t, ap, val):
            K.op(eng, lambda e: e.memset(ap, val), writes=[t])

        ident = Tk(K.sb([128, 128], F32, "ident"))
        ones_r = Tk(K.sb([128, 128], F32, "ones_r"))
        eps_c = Tk(K.sb([128, 1], F32, "eps_c"))
        ones_f = Tk(K.sb([128, 128], F32, "ones_f"))
        MS(dve, ones_f, ones_f.ap[:], 1.0)
        CP(dve, ones_r, r_(ones_r.ap[:]), ones_f, ones_f.ap[:])
        MS(dve, eps_c, eps_c.ap[:], EPS)
        MS(dve, ident, ident.ap[:], 1.0)
        K.op(pool, lambda e: e.affine_select(out=ident.ap[:], in_=ident.ap[:], pattern=[[1, 128]],
                                             compare_op=ALU.is_equal, fill=0.0, base=0,
                                             channel_multiplier=-1), reads=[ident], writes=[ident])

        HB = Tk(K.sb([128, KT, TB], F32, "HB"))
        SLAB = [Tk(K.sb([128, KT * 512], F32, f"slab{i}")) for i in range(2)]
        slab_rr = [0]
        ARENA_R, ARENA_F = 11264, 13056
        arenaR = K.sb([128, ARENA_R], F32, "arenaR")
        arenaF = K.sb([128, ARENA_F], F32, "arenaF")
        MG = Tk(arenaR[:, ARENA_R - KT * TB:ARENA_R].rearrange("p (k t) -> p k t", k=KT))
        XB = Tk(arenaF[:, 0:KT * TB].rearrange("p (k t) -> p k t", k=KT))
        XB_N = KT * TB
        XR = [Tk(None) for _ in range(NB)]

        class Carver:
            def __init__(self, keep_x=True):
                self.o = {True: 0, False: XB_N if keep_x else 0}

            def get(self, n, r=False, parts=128):
                o = self.o[r]
                a = (arenaR if r else arenaF)[0:parts, o:o + n]
                self.o[r] = o + n
                assert self.o[r] <= (ARENA_R if r else ARENA_F), (r, self.o[r])
                return Tk(a)

            def mark(self):
                return dict(self.o)

            def reset(self, m):
                self.o = dict(m)

        cond = Tk(K.sb([128, KT], F32, "cond"))
        modc = Tk(K.sb([128, 48], F32, "modc"))
        bada = Tk(K.sb([128, 48], F32, "bada"))
        g1c = Tk(K.sb([128, KT], F32, "g1c"))
        g2c = Tk(K.sb([128, KT], F32, "g2c"))
        gfc = Tk(K.sb([128, KT], F32, "gfc"))
        a1 = Tk(K.sb([128, KT], F32, "a1"))
        a2 = Tk(K.sb([128, KT], F32, "a2"))
        zero_c = Tk(K.sb([128, 1], F32, "zero_c"))
        MS(dve, zero_c, zero_c.ap[:], 0.0)

        def slab_next():
            slab_rr[0] ^= 1
            return SLAB[slab_rr[0]]

        def slab_load(src_ap, n):
            t = slab_next()
            kk = src_ap.shape[0] // 128
            dst = t.ap[:, 0:kk * n].rearrange("p (k c) -> p k c", k=kk)
            K.dma(pool, r_(dst), src_ap.rearrange("(k p) c -> p k c", p=128), writes=[t])
            return t, dst

        def xview(b):
            return xres[:, b * TB:(b + 1) * TB].rearrange("(k p) t -> p k t", p=128)

        def load_x(b):
            K.dma(sp, XB.ap[:], xview(b), reads=[XR[b]], writes=[XB])

        def store_x(b):
            K.dma(sp, xview(b), XB.ap[:], reads=[XB], writes=[XR[b]])

        def norm_block(acol_t, acol_ap, bcol_t, bcol_ap, out_r=True, OUT=None):
            cv = Carver()
            SQ = cv.get(KT * TB, r=True)
            RS = cv.get(TB)
            sq3 = SQ.ap.rearrange("p (k t) -> p k t", k=KT)
            A(AF.Square, SQ, r_(sq3), XB, XB.ap[:])
            p = ps_next()
            for kt in range(KT):
                mm(p, p.ap[:, :], ones_r, r_(ones_r.ap[:]), SQ, r_(sq3[:, kt, :]), start=(kt == 0), stop=(kt == KT - 1))
            A(AF.Sqrt, RS, RS.ap, p, p.ap[:, :], scale=1.0 / D, bias=eps_c.ap[:], rd=[eps_c])
            K.op(dve, lambda e: e.reciprocal(out=RS.ap, in_=RS.ap), reads=[RS], writes=[RS])
            TT(dve, SQ, r_(sq3), XB, XB.ap[:], RS, RS.ap.unsqueeze(1).to_broadcast([128, KT, TB]), ALU.mult)
            OUT = OUT or HB
            for kt in range(KT):
                o = OUT.ap[:, kt, :]
                A(AF.Identity, OUT, r_(o) if out_r else o, SQ, sq3[:, kt, :], scale=acol_ap[:, kt:kt + 1],
                  bias=(bcol_ap[:, kt:kt + 1] if bcol_ap is not None else zero_c.ap[:]),
                  rd=[acol_t] + ([bcol_t] if bcol_t is not None else [zero_c]))

        K.dma(sp, cond.ap[:], c_in, writes=[cond])
        A(AF.Silu, cond, cond.ap[:], cond, cond.ap[:])
        K.dma(sp, gfc.ap[:], fng, writes=[gfc])
        XT = [Tk(XB.ap[:, 2 * i:2 * i + 2, :].rearrange("p a b -> p (a b)")) for i in range(4)]
        STG = [Tk(arenaF[:, XB_N + 1024 * i:XB_N + 1024 * (i + 1)].rearrange("p (k t) -> p k t", k=KT)) for i in range(4)]
        for tt in range(SEQ // 128):
            xt, sg = XT[tt % 4], STG[tt % 4]
            K.dma(sp, xt.ap, x_in[tt * 128:(tt + 1) * 128, :], writes=[xt])
            for half in range(2):
                p = ps_next()
                for q in range(4):
                    kt = half * 4 + q
                    tr(p, p.ap[:, q * 128:(q + 1) * 128], xt, xt.ap[:, kt * 128:(kt + 1) * 128])
                CP(act if half == 0 else dve, sg, sg.ap[:, half * 4:half * 4 + 4, :], p,
                   p.ap[:, :].rearrange("p (q t) -> p q t", q=4))
            K.dma(sp, xres[:, tt * 128:(tt + 1) * 128].rearrange("(k p) t -> p k t", p=128), sg.ap,
                  reads=[sg], writes=[XR[tt // 4]])
        K.barrier()

        MX = make_mixers(locals())

        for l in range(depth):
            K.dma(sp, bada.ap[:], b_ada[l], writes=[bada])
            K.dma(sp, g1c.ap[:], n1g[l], writes=[g1c])
            K.dma(sp, g2c.ap[:], n2g[l], writes=[g2c])
            pm = ps_next()
            WA = [Tk(arenaF[:, 4096 * i:4096 * (i + 1)]) for i in range(2)]
            for jg in range(12):
                t = WA[jg % 2]
                dst = t.ap[:, :].rearrange("p (k c) -> p k c", k=KT)
                K.dma(sp, dst, w_ada[l][:, jg * 512:(jg + 1) * 512].rearrange("(k p) c -> p k c", p=128), writes=[t])
                for q in range(4):
                    j = jg * 4 + q
                    for kt in range(KT):
                        mm(pm, pm.ap[:, j:j + 1], t, dst[:, kt, q * 128:(q + 1) * 128], cond, cond.ap[:, kt:kt + 1],
                           start=(kt == 0), stop=(kt == KT - 1))
            TT(dve, modc, modc.ap[:], pm, pm.ap[:, 0:48], bada, bada.ap[:], ALU.add)
            STT(a1, a1.ap[:], modc, modc.ap[:, 8:16], 1.0, g1c, g1c.ap[:], ALU.add, ALU.mult)
            STT(a2, a2.ap[:], modc, modc.ap[:, 32:40], 1.0, g2c, g2c.ap[:], ALU.add, ALU.mult)
            K.barrier()
            MX.layer_setup(l)

            for b in range(NB):
                if any(u in use for u in ("gla", "s5", "gdn")):
                    load_x(b)
                    norm_block(a1, a1.ap, modc, modc.ap[:, 0:8])
                    K.barrier()
                    first = True
                    for name in ("gla", "s5", "gdn"):
                        if name not in use:
                            continue
                        YB = MX.run(name, l, b)
                        bi = ("gla", "s5", "gdn").index(name)
                        for oc in range(KT):
                            if oc % 4 == 0:
                                zt, zv = slab_load(w_in[l][:, O_ZG + bi * D + (oc // 4) * 512:O_ZG + bi * D + (oc // 4 + 1) * 512], 512)
                                wt, wv = slab_load(w_br[name][l][:, (oc // 4) * 512:(oc // 4 + 1) * 512], 512)
                            pp, pz = ps_next(), ps_next()
                            for kt in range(4):
                                mm(pp, pp.ap[:, :], wt, r_(wv[:, kt, (oc % 4) * 128:(oc % 4 + 1) * 128]), YB, r_(YB.ap[:, kt, :]),
                                   start=(kt == 0), stop=(kt == 3))
                            for kt in range(KT):
                                mm(pz, pz.ap[:, :], zt, r_(zv[:, kt, (oc % 4) * 128:(oc % 4 + 1) * 128]), HB, r_(HB.ap[:, kt, :]),
                                   start=(kt == 0), stop=(kt == KT - 1))
                            g = MX.GT[oc % 2]
                            A(AF.Sigmoid, g, g.ap, pz, pz.ap[:, :])
                            if first:
                                TT(dve, MG, r_(MG.ap[:, oc, :]), g, g.ap, pp, pp.ap[:, :], ALU.mult)
                            else:
                                TT(dve, g, g.ap, g, g.ap, pp, pp.ap[:, :], ALU.mult)
                                TT(dve, MG, r_(MG.ap[:, oc, :]), MG, MG.ap[:, oc, :], g, g.ap, ALU.add)
                        first = False
                        K.barrier()
                    load_x(b)
                    for oc in range(KT):
                        if oc % 4 == 0:
                            wt, wv = slab_load(w_out[l][:, (oc // 4) * 512:(oc // 4 + 1) * 512], 512)
                        po = ps_next()
                        for kt in range(KT):
                            mm(po, po.ap[:, :], wt, r_(wv[:, kt, (oc % 4) * 128:(oc % 4 + 1) * 128]), MG, r_(MG.ap[:, kt, :]),
                               start=(kt == 0), stop=(kt == KT - 1))
                        STT(XB, XB.ap[:, oc, :], po, po.ap[:, :], modc.ap[:, 16 + oc:17 + oc], XB, XB.ap[:, oc, :],
                            ALU.mult, ALU.add, rd=[modc])
                    if "ffn" not in use:
                        store_x(b)
                    K.barrier()
                elif "ffn" in use:
                    load_x(b)
                if "ffn" in use:
                    norm_block(a2, a2.ap, modc, modc.ap[:, 24:32])
                    K.barrier()
                    cv = Carver()
                    UF = [cv.get(TB, r=True) for _ in range(NFF)]
                    SA = [cv.get(TB) for _ in range(2)]
                    for jg in range(NFF // 2):
                        t = slab_next()
                        dst = t.ap[:, :].rearrange("p (k c) -> p k c", k=KT)
                        K.dma(pool, r_(dst[:, :, 0:256]), w_fi[l][:, jg * 256:(jg + 1) * 256].rearrange("(k p) c -> p k c", p=128), writes=[t])
                        K.dma(pool, r_(dst[:, :, 256:512]), w_fi[l][:, DFF + jg * 256:DFF + (jg + 1) * 256].rearrange("(k p) c -> p k c", p=128), writes=[t])
                        for q in range(2):
                            j = jg * 2 + q
                            pa, pb = ps_next(), ps_next()
                            for kt in range(KT):
                                mm(pa, pa.ap[:, :], t, r_(dst[:, kt, q * 128:(q + 1) * 128]), HB, r_(HB.ap[:, kt, :]),
                                   start=(kt == 0), stop=(kt == KT - 1))
                            for kt in range(KT):
                                mm(pb, pb.ap[:, :], t, r_(dst[:, kt, 256 + q * 128:256 + (q + 1) * 128]), HB, r_(HB.ap[:, kt, :]),
                                   start=(kt == 0), stop=(kt == KT - 1))
                            s = SA[j % 2]
                            A(AF.Silu, s, s.ap, pa, pa.ap[:, :])
                            TT(dve, UF[j], r_(UF[j].ap), s, s.ap, pb, pb.ap[:, :], ALU.mult)
                    for oc in range(KT):
                        wt, wv = slab_load(w_fo[l][:, oc * 128:(oc + 1) * 128], 128)
                        po = ps_next()
                        for j in range(NFF):
                            mm(po, po.ap[:, :], wt, r_(wv[:, j, :]), UF[j], r_(UF[j].ap), start=(j == 0), stop=(j == NFF - 1))
                        STT(XB, XB.ap[:, oc, :], po, po.ap[:, :], modc.ap[:, 40 + oc:41 + oc], XB, XB.ap[:, oc, :],
                            ALU.mult, ALU.add, rd=[modc])
                    store_x(b)
                    K.barrier()

        for b in range(NB):
            load_x(b)
            HN = Tk(arenaF[:, XB_N + 512:XB_N + 512 + KT * TB].rearrange("p (k t) -> p k t", k=KT))
            norm_block(gfc, gfc.ap, None, None, out_r=False, OUT=HN)
            K.barrier()
            OT = [Tk(arenaF[:, 2 * XB_N + 512 + 1024 * i:2 * XB_N + 512 + 1024 * (i + 1)]) for i in range(2)]
            for tt in range(4):
                ot = OT[tt % 2]
                for half in range(2):
                    p = ps_next()
                    for q in range(4):
                        kt = half * 4 + q
                        tr(p, p.ap[:, q * 128:(q + 1) * 128], HN, HN.ap[:, kt, tt * 128:(tt + 1) * 128])
                    CP(act if half == 0 else dve, ot, ot.ap[:, half * 512:(half + 1) * 512], p, p.ap[:, :])
                K.dma(sp, y_out[b * TB + tt * 128:b * TB + (tt + 1) * 128, :], ot.ap, reads=[ot])
            K.barrier()
        K.barrier()
    return nc


def declare_mixer_params(dr, depth=DEPTH):
    P = {}
    P["gla_w_lr"] = dr("gla_w_lr", [depth, 16, 256])
    P["gla_b_lr"] = dr("gla_b_lr", [depth, 1, 256])
    P["gla_ng4"] = dr("gla_ng4", [depth, 1, 512])
    P["s5_lam_col"] = dr("s5_lam_col", [depth, 128, 16, 3])
    P["s5_BR"] = dr("s5_BR", [depth, 128, 16, 128])
    P["s5_BI"] = dr("s5_BI", [depth, 128, 16, 128])
    P["s5_CR"] = dr("s5_CR", [depth, 128, 16, 64])
    P["s5_CI"] = dr("s5_CI", [depth, 128, 16, 64])
    P["s5_dcol"] = dr("s5_dcol", [depth, 128, 4])
    P["s5_w_glu"] = dr("s5_w_glu", [depth, 512, 1024])
    P["gdn_cw"] = dr("gdn_cw", [depth, 128, 12, 4])
    P["gdn_ab"] = dr("gdn_ab", [depth, 1, 8])
    P["gdn_ng4"] = dr("gdn_ng4", [depth, 1, 512])
    return P


def layout_mixer_params(inp):
    f = lambda a: np.ascontiguousarray(np.asarray(a, np.float32))
    out = {}
    out["gla_w_lr"] = f(inp["gla_w_lr"])
    out["gla_b_lr"] = f(inp["gla_b_lr"]).reshape(DEPTH, 1, 256)
    out["gla_ng4"] = np.ascontiguousarray(np.tile(f(inp["gla_norm_g"]), (1, 4)).reshape(DEPTH, 1, 512))
    L = DEPTH
    lam = np.stack([f(inp["s5_lambda_re"]), f(inp["s5_lambda_im"]),
                    np.repeat(f(inp["s5_log_dt"])[:, :, None], 64, axis=2)], axis=-1)
    lam = lam.reshape(L, 16, 2, 64, 3).transpose(0, 2, 3, 1, 4).reshape(L, 128, 16, 3)
    out["s5_lam_col"] = np.ascontiguousarray(lam)
    for nm, src in (("s5_BR", "s5_b_re"), ("s5_BI", "s5_b_im")):
        bsrc = f(inp[src]).reshape(L, 4, 4, 2, 64, 16)
        arr = np.zeros((L, 4, 2, 16, 4, 4, 2, 64), np.float32)
        for q in range(4):
            for g2 in range(2):
                arr[:, q, g2, :, :, q, g2, :] = bsrc[:, :, q, g2].transpose(0, 3, 1, 2)
        out[nm] = np.ascontiguousarray(arr.reshape(L, 128, 16, 128))
    for nm, src in (("s5_CR", "s5_c_re"), ("s5_CI", "s5_c_im")):
        csrc = f(inp[src]).reshape(L, 8, 2, 2, 16, 64)
        arr = np.zeros((L, 2, 64, 8, 2, 2, 2, 16), np.float32)
        for ql in range(2):
            for g2 in range(2):
                arr[:, g2, :, :, ql, ql, g2, :] = csrc[:, :, ql, g2].transpose(0, 3, 1, 2)
        out[nm] = np.ascontiguousarray(arr.reshape(L, 128, 16, 64))
    out["s5_dcol"] = np.ascontiguousarray(f(inp["s5_d"]).reshape(L, 4, 128).transpose(0, 2, 1))
    out["s5_w_glu"] = f(inp["s5_w_glu"])
    out["gdn_cw"] = np.ascontiguousarray(f(inp["gdn_conv_w"]).reshape(L, 4, 12, 128).transpose(0, 3, 2, 1))
    out["gdn_ab"] = np.ascontiguousarray(np.concatenate([f(inp["gdn_a_log"]), f(inp["gdn_dt_bias"])], axis=1).reshape(L, 1, 8))
    out["gdn_ng4"] = np.ascontiguousarray(np.tile(f(inp["gdn_norm_g"]), (1, 4)).reshape(L, 1, 512))
    return out


S5DBG = set()


class Mixers:
    def __init__(self, env):
        self.env = env
        K = env["K"]
        self.GT = [Tk(K.sb([128, TB], F32, f"gt{i}")[:, :]) for i in range(2)]
        MS, dve, pool = env["MS"], env["dve"], env["pool"]
        def tri(name, val, mode, chunked=True):
            t = Tk(K.sb([128, 128], F32, name))
            MS(dve, t, t.ap[:], val)
            if mode == "upper":
                pat, cm, cmp_ = [[1, 128]], -1, ALU.is_ge
            elif mode == "lower_strict":
                pat, cm, cmp_ = [[-1, 128]], 1, ALU.is_gt
            else:
                pat, cm, cmp_ = [[-1, 128]], 1, ALU.is_ge
            K.op(pool, lambda e: e.affine_select(out=t.ap[:], in_=t.ap[:], pattern=pat, compare_op=cmp_,
                                                 fill=0.0, base=0, channel_multiplier=cm), reads=[t], writes=[t])
            if chunked:
                if mode == "upper":
                    MS(dve, t, t.ap[0:64, 64:128], 0.0)
                else:
                    MS(dve, t, t.ap[64:128, 0:64], 0.0)
            return t
        self.TRI_incl = tri("tri_incl", -1.0 / 16.0, "upper")
        self.TRI_after = tri("tri_after", -1.0 / 16.0, "lower_strict")
        self.CMASK = tri("cmask", 1.0, "upper")
        self.UPPER01 = tri("upper01", 1.0, "upper", chunked=False)
        self.MASKGT = tri("maskgt", 1.0, "lower_strict", chunked=False)
        self.AFTER01 = tri("after01", 1.0, "lower_strict")
        self.NEGM = tri("negm", 30000.0, "lower_incl")
        env["TS"](dve, self.NEGM, self.NEGM.ap[:], self.NEGM, self.NEGM.ap[:], -30000.0, None, ALU.add)
        self.CH01 = Tk(K.sb([128, 2], F32, "ch01"))
        MS(dve, self.CH01, self.CH01.ap[:], 0.0)
        MS(dve, self.CH01, self.CH01.ap[0:64, 0:1], 1.0)
        MS(dve, self.CH01, self.CH01.ap[64:128, 1:2], 1.0)
        Q16 = Tk(K.sb([8, 128], F32, "q16"))
        MS(dve, Q16, Q16.ap[:], 1.0)
        K.op(pool, lambda e: e.affine_select(out=Q16.ap[:], in_=Q16.ap[:], pattern=[[1, 128]], compare_op=ALU.is_ge,
                                             fill=0.0, base=0, channel_multiplier=-16), reads=[Q16], writes=[Q16])
        K.op(pool, lambda e: e.affine_select(out=Q16.ap[:], in_=Q16.ap[:], pattern=[[-1, 128]], compare_op=ALU.is_ge,
                                             fill=0.0, base=15, channel_multiplier=16), reads=[Q16], writes=[Q16])
        self.BD16 = Tk(K.sb([128, 128], F32, "bd16"))
        pq = env["ps_next"]()
        env["mm"](pq, pq.ap[:, 0:128], Q16, Q16.ap[:], Q16, Q16.ap[:])
        env["CP"](dve, self.BD16, self.BD16.ap[:], pq, pq.ap[:, 0:128])
        self.gdn_S = Tk(K.sb([128, 4, 128], F32, "gdn_S"))
        self.gdn_Sh = [Tk(self.gdn_S.ap[:, h, :]) for h in range(4)]
        self.HIST = Tk(K.sb([128, 12, 3], F32, "gdn_hist"))
        self.CW = Tk(K.sb([128, 12, 4], F32, "gdn_cw_sb"))
        self.AB = Tk(K.sb([128, 8], F32, "gdn_ab_sb"))
        self.NEGA = Tk(K.sb([128, 4], F32, "gdn_nega"))
        self.GNG2 = Tk(K.sb([128, 512], F32, "gng2"))
        self.ones1 = Tk(K.sb([1, 128], F32, "ones1"))
        MS(dve, self.ones1, self.ones1.ap[:], 1.0)
        self.gla_S = Tk(K.sb([64, 4, 128], F32, "gla_S"))
        self.WLR = Tk(K.sb([16, 256], F32, "wlr"))
        self.BLR = Tk(K.sb([1, 256], F32, "blr"))
        self.GNG = Tk(K.sb([128, 512], F32, "gng"))
        sb2 = lambda shp, n: Tk(K.sb(shp, F32, n)[:])
        self.BwRe = sb2([128, 16, 128], "s5bwre"); self.BwIm = sb2([128, 16, 128], "s5bwim")
        self.CRe = sb2([128, 16, 64], "s5cre"); self.CImN = sb2([128, 16, 64], "s5cimn")
        self.COS = sb2([128, 16, 64], "s5cos"); self.SIN = sb2([128, 16, 64], "s5sin")
        self.Rb = sb2([128, 16, 64], "s5rb"); self.rcol = sb2([128, 16], "s5r")
        self.CAR = sb2([128, 2, 16], "s5car"); self.dcol = sb2([128, 4], "s5d")

    def layer_setup(self, l):
        e = self.env
        K, sp, PRM = e["K"], e["sp"], e["PRM"]
        K.dma(sp, self.WLR.ap[:], PRM["gla_w_lr"][l], writes=[self.WLR])
        K.dma(sp, self.BLR.ap[:], PRM["gla_b_lr"][l], writes=[self.BLR])
        K.dma(sp, self.GNG.ap[:], PRM["gla_ng4"][l].partition_broadcast(128), writes=[self.GNG])
        e["MS"](e["dve"], self.gla_S, self.gla_S.ap[:], 0.0)
        K.barrier()
        if "s5" in e["use"]:
            self.s5_setup(l)
            K.barrier()
        if "gdn" in e["use"]:
            K.dma(sp, self.CW.ap[:], PRM["gdn_cw"][l], writes=[self.CW])
            K.dma(sp, self.AB.ap[:], PRM["gdn_ab"][l].partition_broadcast(128), writes=[self.AB])
            K.dma(sp, self.GNG2.ap[:], PRM["gdn_ng4"][l].partition_broadcast(128), writes=[self.GNG2])
            e["A"](AF.Exp, self.NEGA, self.NEGA.ap[:], self.AB, self.AB.ap[:, 0:4])
            e["TS"](e["dve"], self.NEGA, self.NEGA.ap[:], self.NEGA, self.NEGA.ap[:], -1.0, None, ALU.mult)
            e["MS"](e["dve"], self.gdn_S, self.gdn_S.ap[:], 0.0)
            e["MS"](e["dve"], self.HIST, self.HIST.ap[:], 0.0)
            K.barrier()

    def s5_setup(self, l):
        e = self.env
        K, pe, act, dve, pool, sp, PRM = e["K"], e["pe"], e["act"], e["dve"], e["pool"], e["sp"], e["PRM"]
        mm, A, TT, TS, STT, CP, MS, ps_next = e["mm"], e["A"], e["TT"], e["TS"], e["STT"], e["CP"], e["MS"], e["ps_next"]
        nc = e["nc"]
        cv = e["Carver"](keep_x=False)
        LAM = cv.get(48); lam = LAM.ap.rearrange("p (a c) -> p a c", c=3)
        lr, li, ldt = lam[:, :, 0], lam[:, :, 1], lam[:, :, 2]
        K.dma(sp, lam, PRM["s5_lam_col"][l], writes=[LAM])
        K.dma(sp, self.dcol.ap, PRM["s5_dcol"][l], writes=[self.dcol])
        BR = cv.get(1024); BI = cv.get(1024)
        br3 = BR.ap.rearrange("p (a c) -> p a c", a=8); bi3 = BI.ap.rearrange("p (a c) -> p a c", a=8)
        K.dma(sp, self.CRe.ap, PRM["s5_CR"][l], writes=[self.CRe])
        K.dma(sp, self.CImN.ap, PRM["s5_CI"][l], writes=[self.CImN])
        TS(dve, self.CImN, self.CImN.ap, self.CImN, self.CImN.ap, -1.0, None, ALU.mult)
        W = cv.get(16 * 12); w = W.ap.rearrange("p (n c) -> p n c", c=16)
        dt, th, u, kf, f, g1, c1, s1, wre, wim, t1, t2 = [w[:, i, :] for i in range(12)]
        KI = Tk(K.sb([128, 16], mybir.dt.int32, f"s5ki{l}")[:])
        if "nosetup" in S5DBG:
            for t in (self.COS, self.SIN, self.Rb, self.CAR, self.BwRe, self.BwIm, self.rcol):
                MS(dve, t, t.ap, 0.25)
            return
        A(AF.Exp, W, dt, LAM, ldt)
        TT(dve, W, t1, LAM, lr, W, dt, ALU.mult)
        A(AF.Exp, self.rcol, self.rcol.ap, W, t1)
        TT(dve, W, th, LAM, li, W, dt, ALU.mult)

        def frac_sin(out_ap, off):
            TS(dve, W, u, W, th, 1.0 / PI2, off, ALU.mult, ALU.add)
            CP(dve, KI, KI.ap, W, u)
            CP(dve, W, kf, KI, KI.ap)
            TT(dve, W, f, W, u, W, kf, ALU.subtract)
            TS(dve, W, g1, W, f, 0.5, None, ALU.is_gt)
            TT(dve, W, f, W, f, W, g1, ALU.subtract)
            TS(dve, W, g1, W, f, -0.5, None, ALU.is_lt)
            TT(dve, W, f, W, f, W, g1, ALU.add)
            A(AF.Sin, W, out_ap, W, f, scale=PI2)
        if "nosin" in S5DBG:
            MS(dve, W, s1, 0.6); MS(dve, W, c1, 0.8)
        else:
            frac_sin(s1, 0.0)
            frac_sin(c1, 0.25)
        cos3, sin3 = self.COS.ap, self.SIN.ap
        CP(dve, self.COS, cos3[:, :, 0], W, c1)
        CP(dve, self.SIN, sin3[:, :, 0], W, s1)
        TB1 = cv.get(512); TB2 = cv.get(512); TB3 = cv.get(512); TB4 = cv.get(512)
        n = 1 if "nodbl" not in S5DBG else 64
        while n < 64:
            v16 = lambda t: t.ap.rearrange("p (a c) -> p a c", a=16)[:, :, 0:n]
            x1, x2, cn, sn = v16(TB1), v16(TB2), v16(TB3), v16(TB4)
            CP(dve, TB3, cn, self.COS, cos3[:, :, n - 1:n].to_broadcast([128, 16, n]))
            CP(dve, TB4, sn, self.SIN, sin3[:, :, n - 1:n].to_broadcast([128, 16, n]))
            TT(dve, TB1, x1, self.COS, cos3[:, :, 0:n], TB3, cn, ALU.mult)
            TT(dve, TB2, x2, self.SIN, sin3[:, :, 0:n], TB4, sn, ALU.mult)
            TT(dve, TB1, x1, TB1, x1, TB2, x2, ALU.subtract)
            TT(dve, TB2, x2, self.SIN, sin3[:, :, 0:n], TB3, cn, ALU.mult)
            TT(dve, TB3, cn, self.COS, cos3[:, :, 0:n], TB4, sn, ALU.mult)
            TT(dve, TB2, x2, TB2, x2, TB3, cn, ALU.add)
            CP(dve, self.COS, cos3[:, :, n:2 * n], TB1, x1)
            CP(dve, self.SIN, sin3[:, :, n:2 * n], TB2, x2)
            n *= 2
        MS(dve, self.Rb, self.Rb.ap, 0.0)
        CP(dve, self.Rb, self.Rb.ap[:, :, 1:64], self.rcol, self.rcol.ap.unsqueeze(2).to_broadcast([128, 16, 63]))
        MS(dve, self.CAR, self.CAR.ap, 0.0)
        TT(dve, W, t1, self.rcol, self.rcol.ap, W, c1, ALU.mult)
        TS(dve, W, t1, W, t1, -1.0, None, ALU.add)
        TT(dve, W, t2, self.rcol, self.rcol.ap, W, s1, ALU.mult)
        TT(dve, W, u, LAM, lr, LAM, lr, ALU.mult)
        TT(dve, W, kf, LAM, li, LAM, li, ALU.mult)
        TT(dve, W, u, W, u, W, kf, ALU.add)
        K.op(dve, lambda en: en.reciprocal(out=u, in_=u), reads=[W], writes=[W])
        TT(dve, W, wre, W, t1, LAM, lr, ALU.mult)
        TT(dve, W, kf, W, t2, LAM, li, ALU.mult)
        TT(dve, W, wre, W, wre, W, kf, ALU.add)
        TT(dve, W, wre, W, wre, W, u, ALU.mult)
        TT(dve, W, wim, W, t2, LAM, lr, ALU.mult)
        TT(dve, W, kf, W, t1, LAM, li, ALU.mult)
        TT(dve, W, wim, W, wim, W, kf, ALU.subtract)
        TT(dve, W, wim, W, wim, W, u, ALU.mult)
        DG = [cv.get(128) for _ in range(2)]
        ident, ones_f = e["ident"], e["ones_f"]
        psall, PS = e["psall"], e["PS"]
        WR = cv.get(1024); WI = cv.get(1024); T3 = cv.get(1024)
        for hh in range(2):
            K.dma(sp, br3, PRM["s5_BR"][l][:, 8 * hh:8 * hh + 8, :], writes=[BR])
            K.dma(sp, bi3, PRM["s5_BI"][l][:, 8 * hh:8 * hh + 8, :], writes=[BI])
            for which, (src, base, pts) in enumerate(((wre, 0, [PS[0], PS[1]]), (wim, 1024, [PS[2], PS[3]]))):
                for j in range(8):
                    pi = 8 * hh + j
                    dg = DG[pi % 2]
                    TS(dve, dg, dg.ap, ident, ident.ap[:], src[:, pi:pi + 1], None, ALU.mult, rd=[W])
                    mm(pts[j // 4], psall[:, base + j * 128:base + (j + 1) * 128], ones_f, ones_f.ap[:], dg, dg.ap)
            K.op(act, lambda en: en.activation(out=WR.ap, in_=psall[:, 0:1024], func=AF.Copy), reads=[PS[0], PS[1]], writes=[WR])
            K.op(act, lambda en: en.activation(out=WI.ap, in_=psall[:, 1024:2048], func=AF.Copy), reads=[PS[2], PS[3]], writes=[WI])
            bwre = self.BwRe.ap[:, 8 * hh:8 * hh + 8, :].rearrange("p a c -> p (a c)")
            bwim = self.BwIm.ap[:, 8 * hh:8 * hh + 8, :].rearrange("p a c -> p (a c)")
            TT(dve, self.BwRe, bwre, WR, WR.ap, BR, BR.ap, ALU.mult)
            TT(dve, T3, T3.ap, WI, WI.ap, BI, BI.ap, ALU.mult)
            TT(dve, self.BwRe, bwre, self.BwRe, bwre, T3, T3.ap, ALU.subtract)
            TT(dve, self.BwIm, bwim, WR, WR.ap, BI, BI.ap, ALU.mult)
            TT(dve, T3, T3.ap, WI, WI.ap, BR, BR.ap, ALU.mult)
            TT(dve, self.BwIm, bwim, self.BwIm, bwim, T3, T3.ap, ALU.add)

    def run(self, name, l, b):
        return getattr(self, "run_" + name)(l, b)

    def run_gla(self, l, b):
        e = self.env
        K, pe, act, dve, pool, sp = e["K"], e["pe"], e["act"], e["dve"], e["pool"], e["sp"]
        mm, tr, A, TT, TS, STT, CP, MS = e["mm"], e["tr"], e["A"], e["TT"], e["TS"], e["STT"], e["CP"], e["MS"]
        ps_next, slab_load, HB, w_in = e["ps_next"], e["slab_load"], e["HB"], e["w_in"]
        cv = e["Carver"](keep_x=False)
        QE = cv.get(4 * TB, parts=64); KE = cv.get(4 * TB, parts=64)
        qe = QE.ap.rearrange("p (h t) -> p h t", h=4); ke = KE.ap.rearrange("p (h t) -> p h t", h=4)
        KTM = [cv.get(256) for _ in range(4)]
        LT = [cv.get(256) for _ in range(4)]
        VS = [cv.get(512) for _ in range(2)]
        SG = [cv.get(512) for _ in range(2)]
        YT = [cv.get(512) for _ in range(2)]
        EB = cv.get(512, parts=64); ENB = cv.get(512, parts=64)
        EA = cv.get(256)
        SCM = [cv.get(128) for _ in range(2)]
        DEC = cv.get(32, parts=64)
        dec = DEC.ap.rearrange("p (h c) -> p h c", h=4)
        SSQ = cv.get(8)
        LRT = cv.get(512, parts=16)
        JK = cv.get(128)
        YB = cv.get(4 * TB, r=True)
        yb = YB.ap.rearrange("p (h t) -> p h t", h=4)
        S = self.gla_S
        wt, wv = slab_load(w_in[l][:, O_GQ:O_GQ + 512], 512)
        for h in range(4):
            for which, dst, sc in ((0, qe, 0.125), (1, ke, 1.0)):
                p = ps_next()
                for kt in range(KT):
                    mm(p, p.ap[0:64, :], wt, r_(wv[:, kt, which * 256 + h * 64:which * 256 + (h + 1) * 64]), HB, r_(HB.ap[:, kt, :]),
                       start=(kt == 0), stop=(kt == KT - 1))
                A(AF.Copy, QE if which == 0 else KE, dst[:, h, :], p, p.ap[0:64, :], scale=sc)
        for tt in range(4):
            p = ps_next()
            for kt in range(KT):
                mm(p, p.ap[:, 0:256], HB, r_(HB.ap[:, kt, tt * 128:(tt + 1) * 128]), wt, r_(wv[:, kt, 256:512]),
                   start=(kt == 0), stop=(kt == KT - 1))
            CP(dve, KTM[tt], KTM[tt].ap, p, p.ap[:, 0:256])
        wt2, wv2 = slab_load(w_in[l][:, O_GLR:O_GLR + 16], 16)
        p = ps_next()
        for kt in range(KT):
            mm(p, p.ap[0:16, :], wt2, r_(wv2[:, kt, :]), HB, r_(HB.ap[:, kt, :]), start=(kt == 0), stop=(kt == KT - 1))
        CP(act, LRT, LRT.ap, p, p.ap[0:16, :])
        for tt in range(4):
            p = ps_next()
            mm(p, p.ap[:, 0:256], LRT, LRT.ap[:, tt * 128:(tt + 1) * 128], self.WLR, self.WLR.ap[:], start=True, stop=False)
            mm(p, p.ap[:, 0:256], self.ones1, self.ones1.ap[:], self.BLR, self.BLR.ap[:], start=False, stop=True)
            A(AF.Exp, LT[tt], LT[tt].ap, p, p.ap[:, 0:256], scale=-1.0)
            A(AF.Ln, LT[tt], LT[tt].ap, LT[tt], LT[tt].ap, bias=1.0)
            p2 = ps_next()
            mm(p2, p2.ap[:, 0:256], self.TRI_after, self.TRI_after.ap[:], LT[tt], LT[tt].ap)
            A(AF.Exp, EA, EA.ap, p2, p2.ap[:, 0:256])
            TT(dve, KTM[tt], KTM[tt].ap, KTM[tt], KTM[tt].ap, EA, EA.ap, ALU.mult)
        for h in range(4):
            p = ps_next()
            for tt in range(4):
                mm(p, p.ap[0:64, tt * 128:(tt + 1) * 128], LT[tt], LT[tt].ap[:, h * 64:(h + 1) * 64], self.TRI_incl, self.TRI_incl.ap[:])
            A(AF.Exp, EB, EB.ap, p, p.ap[0:64, :])
            A(AF.Exp, ENB, ENB.ap, p, p.ap[0:64, :], scale=-1.0)
            TT(dve, QE, qe[:, h, :], QE, qe[:, h, :], EB, EB.ap, ALU.mult)
            TT(dve, KE, ke[:, h, :], KE, ke[:, h, :], ENB, ENB.ap, ALU.mult)
            CP(dve, DEC, dec[:, h, :], EB, EB.ap.rearrange("p (c t) -> p c t", t=64)[:, :, 63])
        wtv, wvv = slab_load(w_in[l][:, O_GV:O_GV + 512], 512)
        wto, wvo = slab_load(w_in[l][:, O_GOG:O_GOG + 512], 512)
        for tt in range(4):
            vs, sg, yt = VS[tt % 2], SG[tt % 2], YT[tt % 2]
            ts_ = slice(tt * 128, (tt + 1) * 128)
            p = ps_next()
            for kt in range(KT):
                mm(p, p.ap[:, :], HB, r_(HB.ap[:, kt, ts_]), wtv, r_(wvv[:, kt, :]), start=(kt == 0), stop=(kt == KT - 1))
            CP(act, vs, vs.ap, p, p.ap[:, :])
            p = ps_next()
            for kt in range(KT):
                mm(p, p.ap[:, :], HB, r_(HB.ap[:, kt, ts_]), wto, r_(wvo[:, kt, :]), start=(kt == 0), stop=(kt == KT - 1))
            A(AF.Silu, sg, sg.ap, p, p.ap[:, :])
            TT(dve, sg, sg.ap, sg, sg.ap, self.GNG, self.GNG.ap[:], ALU.mult)
            po = e["PS"][6 + tt % 2]
            for h in range(4):
                hs = slice(h * 128, (h + 1) * 128)
                pa = ps_next()
                mm(pa, pa.ap[:, 0:128], KE, ke[:, h, ts_], QE, qe[:, h, ts_])
                scm = SCM[h % 2]
                TT(dve, scm, scm.ap, pa, pa.ap[:, 0:128], self.CMASK, self.CMASK.ap[:], ALU.mult)
                mm(po, po.ap[:, hs], scm, scm.ap, vs, vs.ap[:, hs], start=True, stop=False)
                for c in range(2):
                    cs = slice(tt * 128 + c * 64, tt * 128 + (c + 1) * 64)
                    rows = slice(c * 64, (c + 1) * 64)
                    mm(po, po.ap[rows, hs], QE, qe[:, h, cs], S, S.ap[:, h, :], start=False, stop=True)
                    pk = ps_next()
                    mm(pk, pk.ap[0:64, 0:128], KTM[tt], KTM[tt].ap[rows, h * 64:(h + 1) * 64], vs, vs.ap[rows, hs])
                    ch = tt * 2 + c
                    STT(S, S.ap[:, h, :], S, S.ap[:, h, :], dec[:, h, ch:ch + 1], pk, pk.ap[0:64, 0:128], ALU.mult, ALU.add, rd=[DEC])
            for h in range(4):
                hs = slice(h * 128, (h + 1) * 128)
                A(AF.Square, JK, JK.ap, po, po.ap[:, hs], accum=SSQ.ap[:, h:h + 1], wr=[SSQ])
            A(AF.Sqrt, SSQ, SSQ.ap[:, 4:8], SSQ, SSQ.ap[:, 0:4], scale=1.0 / 128.0, bias=e["eps_c"].ap[:], rd=[e["eps_c"]])
            K.op(dve, lambda en: en.reciprocal(out=SSQ.ap[:, 4:8], in_=SSQ.ap[:, 4:8]), reads=[SSQ], writes=[SSQ])
            for h in range(4):
                hs = slice(h * 128, (h + 1) * 128)
                STT(yt, yt.ap[:, hs], po, po.ap[:, hs], SSQ.ap[:, 4 + h:5 + h], sg, sg.ap[:, hs], ALU.mult, ALU.mult, rd=[SSQ])
            pt = ps_next()
            for h in range(4):
                tr(pt, pt.ap[:, h * 128:(h + 1) * 128], yt, yt.ap[:, h * 128:(h + 1) * 128])
            CP(act, YB, r_(yb[:, :, ts_]), pt, pt.ap[:, :].rearrange("p (h t) -> p h t", h=4))
        return Tk(yb) if False else _view(YB, yb)


def _s5_run(self, l, b):
    e = self.env
    K, pe, act, dve, pool, sp = e["K"], e["pe"], e["act"], e["dve"], e["pool"], e["sp"]
    mm, A, TT, TS, STT, CP, MS = e["mm"], e["A"], e["TT"], e["TS"], e["STT"], e["CP"], e["MS"]
    ps_next, slab_load, HB, w_in, PS, psall, PRM = e["ps_next"], e["slab_load"], e["HB"], e["w_in"], e["PS"], e["psall"], e["PRM"]
    cv = e["Carver"](keep_x=False)
    UT = cv.get(4 * TB); ut = UT.ap.rearrange("p (c t) -> p c t", c=4)
    YS = cv.get(4 * TB); ys = YS.ap.rearrange("p (c t) -> p c t", c=4)
    GIN2 = cv.get(2048); GINr = Tk(GIN2.ap[:, 0:1024]); GINi = Tk(GIN2.ap[:, 1024:2048]); _CAT[id(GINr)] = GIN2.ap
    Gr = cv.get(1024); Gi = cv.get(1024)
    Hr = cv.get(1024); Hi = cv.get(1024); TMPa = cv.get(1024); TMPb = cv.get(1024)
    ginre, ginim, gre, gim, hre, him = GINr.ap, GINi.ap, Gr.ap, Gi.ap, Hr.ap, Hi.ap
    v3 = lambda ap: ap.rearrange("p (a t) -> p a t", a=16)
    cosf = self.COS.ap.rearrange("p a t -> p (a t)"); sinf = self.SIN.ap.rearrange("p a t -> p (a t)")
    rbf = self.Rb.ap.rearrange("p a t -> p (a t)")
    YG = cv.get(4 * TB, r=True); yg = YG.ap.rearrange("p (c t) -> p c t", c=4)
    YB = cv.get(4 * TB, r=True); yb = YB.ap.rearrange("p (c t) -> p c t", c=4)
    wt, wv = slab_load(w_in[l][:, O_S5:O_S5 + 512], 512)
    for ct in range(4):
        p = ps_next()
        for kt in range(KT):
            mm(p, p.ap[:, :], wt, r_(wv[:, kt, ct * 128:(ct + 1) * 128]), HB, r_(HB.ap[:, kt, :]), start=(kt == 0), stop=(kt == KT - 1))
        CP(act if ct % 2 == 0 else dve, UT, ut[:, ct, :], p, p.ap[:, :])
    bure, buim = psall[:, 0:1024], psall[:, 1024:2048]
    PRE, PIM = [PS[0], PS[1]], [PS[2], PS[3]]
    for sc in range(TB // 64 if "noloop" not in S5DBG else 0):
        cs = slice(sc * 64, (sc + 1) * 64)
        for pi in range(16):
            a = pi // 4
            mm(PS[pi // 8], bure[:, pi * 64:(pi + 1) * 64], self.BwRe, self.BwRe.ap[:, pi, :], UT, ut[:, a, cs])
            mm(PS[2 + pi // 8], buim[:, pi * 64:(pi + 1) * 64], self.BwIm, self.BwIm.ap[:, pi, :], UT, ut[:, a, cs])

        if "st1" in S5DBG:
            continue
        def tt2(o_t, o_ap, a_ts, a_ap, b_t, b_ap, op):
            K.op(dve, lambda en: en.tensor_tensor(out=o_ap, in0=a_ap, in1=b_ap, op=op), reads=list(a_ts) + [b_t], writes=[o_t])
        tt2(GINr, ginre, PRE, bure, self.COS, cosf, ALU.mult)
        tt2(GINi, ginim, PIM, buim, self.COS, cosf, ALU.mult)
        tt2(TMPa, TMPa.ap, PIM, buim, self.SIN, sinf, ALU.mult)
        tt2(TMPb, TMPb.ap, PRE, bure, self.SIN, sinf, ALU.mult)
        TT(dve, GINr, ginre, GINr, ginre, TMPa, TMPa.ap, ALU.add)
        TT(dve, GINi, ginim, GINi, ginim, TMPb, TMPb.ap, ALU.subtract)
        TT(dve, TMPa, TMPa.ap[:, 0:16], self.CAR, self.CAR.ap[:, 0, :], self.rcol, self.rcol.ap, ALU.mult)
        TT(dve, TMPb, TMPb.ap[:, 0:16], self.CAR, self.CAR.ap[:, 1, :], self.rcol, self.rcol.ap, ALU.mult)
        TT(dve, GINr, v3(ginre)[:, :, 0], GINr, v3(ginre)[:, :, 0], TMPa, TMPa.ap[:, 0:16], ALU.add)
        TT(dve, GINi, v3(ginim)[:, :, 0], GINi, v3(ginim)[:, :, 0], TMPb, TMPb.ap[:, 0:16], ALU.add)
        for git, gap, got, oap in ((GINr, ginre, Gr, gre), (GINi, ginim, Gi, gim)):
            K.op(dve, lambda en, gap=gap, oap=oap: en.tensor_tensor_scan(out=oap, data0=rbf, data1=gap, initial=0.0, op0=ALU.mult, op1=ALU.add),
                 reads=[git, self.Rb], writes=[got])
        TT(dve, Hr, hre, Gr, gre, self.COS, cosf, ALU.mult)
        TT(dve, Hi, him, Gi, gim, self.COS, cosf, ALU.mult)
        TT(dve, TMPa, TMPa.ap, Gi, gim, self.SIN, sinf, ALU.mult)
        TT(dve, TMPb, TMPb.ap, Gr, gre, self.SIN, sinf, ALU.mult)
        TT(dve, Hr, hre, Hr, hre, TMPa, TMPa.ap, ALU.subtract)
        TT(dve, Hi, him, Hi, him, TMPb, TMPb.ap, ALU.add)
        CP(dve, self.CAR, self.CAR.ap[:, 0, :], Hr, v3(hre)[:, :, 63])
        CP(dve, self.CAR, self.CAR.ap[:, 1, :], Hi, v3(him)[:, :, 63])
        py = PS[4 + sc % 2]
        for ct in range(4):
            for half in range(2):
                o = py.ap[64 * half:64 * half + 64, ct * 64:(ct + 1) * 64]
                for ql in range(2):
                    pi = 4 * ct + 2 * half + ql
                    mm(py, o, self.CRe, self.CRe.ap[:, pi, :], Hr, v3(hre)[:, pi, :], start=(ql == 0), stop=False)
                    mm(py, o, self.CImN, self.CImN.ap[:, pi, :], Hi, v3(him)[:, pi, :], start=False, stop=(ql == 1))
        if "st6" in S5DBG:
            continue
        for ct in range(4):
            STT(YS, ys[:, ct, cs], UT, ut[:, ct, cs], self.dcol.ap[:, ct:ct + 1], py, py.ap[:, ct * 64:(ct + 1) * 64],
                ALU.mult, ALU.add, rd=[self.dcol])
    K.barrier()
    T1 = Tk(arenaF_view(GINr, GINi))
    A(AF.Square, T1, T1.ap, YS, YS.ap)
    TS(dve, T1, T1.ap, T1, T1.ap, 0.044715, 1.0, ALU.mult, ALU.add)
    TT(dve, T1, T1.ap, T1, T1.ap, YS, YS.ap, ALU.mult)
    A(AF.Sigmoid, T1, T1.ap, T1, T1.ap, scale=1.5957691216057308)
    TT(dve, YG, r_(YG.ap), T1, T1.ap, YS, YS.ap, ALU.mult)
    wa, wva = slab_load(PRM["s5_w_glu"][l][:, 0:512], 512)
    wb, wvb = slab_load(PRM["s5_w_glu"][l][:, 512:1024], 512)
    for j in range(4):
        pa, pb = ps_next(), ps_next()
        for kt in range(4):
            mm(pa, pa.ap[:, :], wa, r_(wva[:, kt, j * 128:(j + 1) * 128]), YG, r_(yg[:, kt, :]), start=(kt == 0), stop=(kt == 3))
        for kt in range(4):
            mm(pb, pb.ap[:, :], wb, r_(wvb[:, kt, j * 128:(j + 1) * 128]), YG, r_(yg[:, kt, :]), start=(kt == 0), stop=(kt == 3))
        g = self.GT[j % 2]
        A(AF.Sigmoid, g, g.ap, pb, pb.ap[:, :])
        TT(dve, YB, r_(yb[:, j, :]), g, g.ap, pa, pa.ap[:, :], ALU.mult)
    return _view(YB, yb)


Mixers.run_s5 = _s5_run


def _gdn_run(self, l, b):
    e = self.env
    K, pe, act, dve, pool, sp = e["K"], e["pe"], e["act"], e["dve"], e["pool"], e["sp"]
    mm, tr, A, TT, TS, STT, CP, MS = e["mm"], e["tr"], e["A"], e["TT"], e["TS"], e["STT"], e["CP"], e["MS"]
    ps_next, slab_load, HB, w_in, PS = e["ps_next"], e["slab_load"], e["HB"], e["w_in"], e["PS"]
    ident, ones_f, eps_c = e["ident"], e["ones_f"], e["eps_c"]
    cv = e["Carver"](keep_x=False)
    QK = cv.get(8 * TB); qk = QK.ap.rearrange("p (c t) -> p c t", c=8)
    VTM = cv.get(4 * 512); vtm = VTM.ap.rearrange("p (t c) -> p t c", t=4)
    SM = [cv.get(64) for _ in range(4)]
    YB = cv.get(4 * TB, r=True); yb = YB.ap.rearrange("p (h t) -> p h t", h=4)
    mk = cv.mark()
    XC = cv.get(520); VB = cv.get(512); SQ = cv.get(512); RSs = cv.get(512); GBC = cv.get(128)
    S, hist, cw = self.gdn_S, self.HIST.ap, self.CW.ap
    for g3 in range(3):
        wt, wv = slab_load(w_in[l][:, O_DQKV + g3 * 512:O_DQKV + (g3 + 1) * 512], 512)
        for c4 in range(4):
            ct = g3 * 4 + c4
            p = ps_next()
            for kt in range(KT):
                mm(p, p.ap[:, :], wt, r_(wv[:, kt, c4 * 128:(c4 + 1) * 128]), HB, r_(HB.ap[:, kt, :]), start=(kt == 0), stop=(kt == KT - 1))
            CP(act, XC, XC.ap[:, 3:515], p, p.ap[:, :])
            CP(dve, XC, XC.ap[:, 0:3], self.HIST, hist[:, ct, :])
            dt_, dap = (QK, qk[:, ct, :]) if g3 < 2 else (VB, VB.ap)
            TS(dve, dt_, dap, XC, XC.ap[:, 3:515], cw[:, ct, 3:4], None, ALU.mult, rd=[self.CW])
            for j in (2, 1, 0):
                STT(dt_, dap, XC, XC.ap[:, j:j + 512], cw[:, ct, j:j + 1], dt_, dap, ALU.mult, ALU.add, rd=[self.CW])
            CP(dve, self.HIST, hist[:, ct, :], XC, XC.ap[:, 512:515])
            A(AF.Silu, dt_, dap, dt_, dap)
            if g3 == 2:
                pt = ps_next()
                for tt in range(4):
                    tr(pt, pt.ap[:, tt * 128:(tt + 1) * 128], VB, VB.ap[:, tt * 128:(tt + 1) * 128])
                CP(act, VTM, vtm[:, :, c4 * 128:(c4 + 1) * 128], pt, pt.ap[:, :].rearrange("p (t c) -> p t c", t=4))
    if "g1" in S5DBG:
        return _view(YB, yb)
    for ct in range(8):
        A(AF.Square, SQ, SQ.ap, QK, qk[:, ct, :])
        p = ps_next()
        mm(p, p.ap[:, :], ones_f, ones_f.ap[:], SQ, SQ.ap)
        A(AF.Sqrt, RSs, RSs.ap, p, p.ap[:, :], bias=eps_c.ap[:], rd=[eps_c])
        K.op(dve, lambda en: en.reciprocal(out=RSs.ap, in_=RSs.ap), reads=[RSs], writes=[RSs])
        if ct < 4:
            STT(QK, qk[:, ct, :], QK, qk[:, ct, :], 128.0 ** -0.5, RSs, RSs.ap, ALU.mult, ALU.mult)
        else:
            TT(dve, QK, qk[:, ct, :], QK, qk[:, ct, :], RSs, RSs.ap, ALU.mult)
    if "g2" in S5DBG:
        return _view(YB, yb)
    wt2, wv2 = slab_load(w_in[l][:, O_DBETA:O_DBETA + 8], 8)
    for tt in range(4):
        sm = SM[tt]
        p = ps_next()
        for kt in range(KT):
            mm(p, p.ap[:, 0:8], HB, r_(HB.ap[:, kt, tt * 128:(tt + 1) * 128]), wt2, r_(wv2[:, kt, :]), start=(kt == 0), stop=(kt == KT - 1))
        A(AF.Sigmoid, sm, sm.ap[:, 0:4], p, p.ap[:, 0:4])
        TT(dve, sm, sm.ap[:, 52:56], p, p.ap[:, 4:8], self.AB, self.AB.ap[:, 4:8], ALU.add)
        A(AF.Exp, sm, sm.ap[:, 52:56], sm, sm.ap[:, 52:56])
        A(AF.Ln, sm, sm.ap[:, 52:56], sm, sm.ap[:, 52:56], bias=1.0)
        TT(dve, sm, sm.ap[:, 4:8], sm, sm.ap[:, 52:56], self.NEGA, self.NEGA.ap[:], ALU.mult)
        p2 = ps_next()
        mm(p2, p2.ap[:, 0:4], self.AFTER01, self.AFTER01.ap[:], sm, sm.ap[:, 4:8])
        mm(p2, p2.ap[:, 4:8], self.CMASK, self.CMASK.ap[:], sm, sm.ap[:, 4:8])
        A(AF.Exp, sm, sm.ap[:, 8:16], p2, p2.ap[:, 0:8])
        TT(dve, sm, sm.ap[:, 16:20], sm, sm.ap[:, 0:4], sm, sm.ap[:, 12:16], ALU.mult)
        p3 = ps_next()
        for h in range(4):
            TS(dve, GBC, GBC.ap, ones_f, ones_f.ap[:], sm.ap[:, 4 + h:5 + h], None, ALU.mult, rd=[sm])
            mm(p3, p3.ap[:, 2 * h:2 * h + 2], GBC, GBC.ap, self.CH01, self.CH01.ap[:])
        A(AF.Exp, sm, sm.ap[:, 20:28], p3, p3.ap[:, 0:8])
    K.barrier()
    if "g3" in S5DBG:
        return _view(YB, yb)
    cv.reset(mk)
    KTMt = cv.get(512); O = cv.get(512); YT = cv.get(512); SGt = cv.get(512); JK = cv.get(128); SSQ = cv.get(8)
    SLOTS = [[cv.get(128) for _ in range(11)] + [cv.get(256) for _ in range(3)] for _ in range(2)]
    wto, wvo = slab_load(w_in[l][:, O_DOG:O_DOG + 512], 512)
    f1 = lambda t: t.ap
    pcopy = [0]

    def evac(dst, src_t, src_ap):
        pcopy[0] += 1
        CP(act if pcopy[0] % 2 else dve, dst, dst.ap[:, 0:src_ap.shape[1]] if False else dst.ap, src_t, src_ap)
    for tt in range(4):
        sm = SM[tt]
        cs = slice(tt * 128, (tt + 1) * 128)
        pt = ps_next()
        for h in range(4):
            tr(pt, pt.ap[:, h * 128:(h + 1) * 128], QK, qk[:, 4 + h, cs])
        CP(act, KTMt, KTMt.ap, pt, pt.ap[:, :])
        def head_chain(tt, h, par, slots, sm=sm, cs=cs):
            Sh = self.gdn_Sh[h]
            hs = slice(h * 128, (h + 1) * 128)
            beta, gcol, egl, egam, be = (sm.ap[:, h:h + 1], sm.ap[:, 4 + h:5 + h], sm.ap[:, 8 + h:9 + h],
                                         sm.ap[:, 12 + h:13 + h], sm.ap[:, 16 + h:17 + h])
            s1, s2, s3, s4, s5, s6, s7, s8, s9, s10, s11, r1, r2, r3 = slots
            UG, EL, ATT, Lm, Ld, Lo, Md, ATT_T, Ld2, Y0, Y1 = s1, s2, s3, s4, s5, s6, s7, s8, s9, s10, s11
            TS(dve, UG, UG.ap, self.UPPER01, self.UPPER01.ap[:], gcol, None, ALU.mult, rd=[sm])
            pD = ps_next()
            mm(pD, pD.ap[:, 0:128], UG, UG.ap, self.MASKGT, self.MASKGT.ap[:], start=True, stop=False)
            mm(pD, pD.ap[:, 0:128], ident, ident.ap[:], self.NEGM, self.NEGM.ap[:], start=False, stop=True)
            A(AF.Exp, EL, EL.ap, pD, pD.ap[:, 0:128])
            yield
            pG = ps_next()
            mm(pG, pG.ap[:, 0:128], QK, qk[:, 4 + h, cs], QK, qk[:, 4 + h, cs])
            mm(pG, pG.ap[:, 128:256], QK, qk[:, h, cs], QK, qk[:, 4 + h, cs])
            TT(dve, ATT, ATT.ap, pG, pG.ap[:, 128:256], EL, EL.ap, ALU.mult)
            STT(Lm, Lm.ap, pG, pG.ap[:, 0:128], beta, EL, EL.ap, ALU.mult, ALU.mult, rd=[sm])
            TT(dve, Lm, Lm.ap, Lm, Lm.ap, self.AFTER01, self.AFTER01.ap[:], ALU.mult)
            TT(dve, Ld, Ld.ap, Lm, Lm.ap, self.BD16, self.BD16.ap[:], ALU.mult)
            TT(dve, Lo, Lo.ap, Lm, Lm.ap, Ld, Ld.ap, ALU.subtract)
            yield
            pT = ps_next()
            tr(pT, pT.ap[:, 0:128], Ld, Ld.ap)
            tr(pT, pT.ap[:, 128:256], ATT, ATT.ap)
            CP(act, Md, Md.ap, pT, pT.ap[:, 0:128])
            TT(dve, Y0, Y0.ap, ident, ident.ap[:], Md, Md.ap, ALU.subtract)
            CP(act, ATT_T, ATT_T.ap, pT, pT.ap[:, 128:256])
            yield
            Md2, Md4, Ld4, Ld8 = s1, s2, s4, s3
            p = ps_next()
            mm(p, p.ap[:, 0:128], Ld, Ld.ap, Md, Md.ap)
            mm(p, p.ap[:, 128:256], Md, Md.ap, Ld, Ld.ap)
            CP(act, Md2, Md2.ap, p, p.ap[:, 0:128]); CP(dve, Ld2, Ld2.ap, p, p.ap[:, 128:256])
            yield
            p = ps_next()
            mm(p, p.ap[:, 0:128], Ld2, Ld2.ap, Md2, Md2.ap)
            mm(p, p.ap[:, 128:256], Md2, Md2.ap, Ld2, Ld2.ap)
            CP(act, Md4, Md4.ap, p, p.ap[:, 0:128]); CP(dve, Ld4, Ld4.ap, p, p.ap[:, 128:256])
            yield
            p = ps_next()
            mm(p, p.ap[:, 0:128], Md4, Md4.ap, Ld4, Ld4.ap)
            CP(act, Ld8, Ld8.ap, p, p.ap[:, 0:128])
            yield
            Ya, Yb = Y0, Y1
            for Lp in (Ld2, Ld4, Ld8):
                p = ps_next()
                mm(p, p.ap[:, 0:128], Lp, Lp.ap, Ya, Ya.ap)
                TT(dve, Yb, Yb.ap, Ya, Ya.ap, p, p.ap[:, 0:128], ALU.add)
                yield
                Ya, Yb = Yb, Ya
            DinvT = Ya
            Pm = s1
            p = ps_next()
            mm(p, p.ap[:, 0:128], Lo, Lo.ap, DinvT, DinvT.ap)
            CP(act, Pm, Pm.ap, p, p.ap[:, 0:128])
            yield
            R, Z0, A1 = r1, r2, r3
            TS(dve, R, R.ap[:, 0:128], VTM, vtm[:, tt, hs], beta, None, ALU.mult, rd=[sm])
            TS(dve, R, R.ap[:, 128:256], KTMt, KTMt.ap[:, hs], be, None, ALU.mult, rd=[sm])
            p = ps_next()
            mm(p, p.ap[:, 0:256], DinvT, DinvT.ap, R, R.ap)
            CP(act, Z0, Z0.ap, p, p.ap[:, 0:256])
            yield
            p = ps_next()
            mm(p, p.ap[:, 0:256], Pm, Pm.ap, Z0, Z0.ap)
            CP(act, A1, A1.ap, p, p.ap[:, 0:256])
            yield
            Z1 = r1
            p = ps_next()
            mm(p, p.ap[:, 0:256], Pm, Pm.ap, A1, A1.ap)
            TT(dve, Z1, Z1.ap, Z0, Z0.ap, p, p.ap[:, 0:256], ALU.add)
            yield
            Z2 = r2
            p = ps_next()
            mm(p, p.ap[:, 0:256], Pm, Pm.ap, Z1, Z1.ap)
            STT(Z2, Z2.ap, p, p.ap[:, 0:256], -1.0, Z1, Z1.ap, ALU.mult, ALU.add)
            yield
            WT, KD, VN, TMPI = s2, s3, s4, s5
            p = ps_next()
            tr(p, p.ap[:, 0:128], Z2, Z2.ap[:, 128:256])
            CP(act, WT, WT.ap, p, p.ap[:, 0:128])
            TS(dve, KD, KD.ap, KTMt, KTMt.ap[:, hs], egl, None, ALU.mult, rd=[sm])
            yield
            pb = PS[6 + par]
            for c in range(2):
                rows = slice(c * 64, (c + 1) * 64)
                ccs = slice(tt * 128 + c * 64, tt * 128 + (c + 1) * 64)
                mm(pb, pb.ap[rows, 0:128], WT, WT.ap[:, c * 64:(c + 1) * 64], Sh, Sh.ap)
                STT(VN, VN.ap[rows, :], pb, pb.ap[rows, 0:128], -1.0, Z2, Z2.ap[rows, 0:128], ALU.mult, ALU.add)
                mm(pb, pb.ap[rows, 128:256], QK, qk[:, h, ccs], Sh, Sh.ap)
                mm(pb, pb.ap[rows, 256:384], ATT_T, ATT_T.ap[rows, c * 64:(c + 1) * 64], VN, VN.ap[rows, :])
                pS = ps_next()
                mm(pS, pS.ap[:, 0:128], KD, KD.ap[rows, :], VN, VN.ap[rows, :])
                dcol = sm.ap[:, 20 + 2 * h + c:21 + 2 * h + c]
                STT(Sh, Sh.ap, Sh, Sh.ap, dcol, pS, pS.ap[:, 0:128], ALU.mult, ALU.add, rd=[sm])
                yield
            A(AF.Copy, TMPI, TMPI.ap, pb, pb.ap[:, 128:256], scale=egam, rd=[sm])
            TT(dve, O, O.ap[:, hs], TMPI, TMPI.ap, pb, pb.ap[:, 256:384], ALU.add)

        for pair in ((0, 1), (2, 3)):
            gens = [head_chain(tt, h, i, SLOTS[i]) for i, h in enumerate(pair)]
            live = list(gens)
            while live:
                for g_ in list(live):
                    try:
                        next(g_)
                    except StopIteration:
                        live.remove(g_)
        p = ps_next()
        for kt in range(KT):
            mm(p, p.ap[:, :], HB, r_(HB.ap[:, kt, cs]), wto, r_(wvo[:, kt, :]), start=(kt == 0), stop=(kt == KT - 1))
        A(AF.Silu, SGt, SGt.ap, p, p.ap[:, :])
        TT(dve, SGt, SGt.ap, SGt, SGt.ap, self.GNG2, self.GNG2.ap[:], ALU.mult)
        for h in range(4):
            hs = slice(h * 128, (h + 1) * 128)
            A(AF.Square, JK, JK.ap, O, O.ap[:, hs], accum=SSQ.ap[:, h:h + 1], wr=[SSQ])
        A(AF.Sqrt, SSQ, SSQ.ap[:, 4:8], SSQ, SSQ.ap[:, 0:4], scale=1.0 / 128.0, bias=eps_c.ap[:], rd=[eps_c])
        K.op(dve, lambda en: en.reciprocal(out=SSQ.ap[:, 4:8], in_=SSQ.ap[:, 4:8]), reads=[SSQ], writes=[SSQ])
        for h in range(4):
            hs = slice(h * 128, (h + 1) * 128)
            STT(YT, YT.ap[:, hs], O, O.ap[:, hs], SSQ.ap[:, 4 + h:5 + h], SGt, SGt.ap[:, hs], ALU.mult, ALU.mult, rd=[SSQ])
        pt = ps_next()
        for h in range(4):
            tr(pt, pt.ap[:, h * 128:(h + 1) * 128], YT, YT.ap[:, h * 128:(h + 1) * 128])
        CP(act, YB, r_(yb[:, :, cs]), pt, pt.ap[:, :].rearrange("p (h t) -> p h t", h=4))
    return _view(YB, yb)


Mixers.run_gdn = _gdn_run


def arenaF_view(a, b):
    t = a.ap.tensor
    off = a.ap.offset
    assert b.ap.offset == off + 1024
    return a.ap.tensor[:, :] if False else _cat_view(a, b)


def _cat_view(a, b):
    return _CAT[id(a)]


_CAT = {}


def _view(t, ap):
    t.ap = ap
    return t


def make_mixers(env):
    return Mixers(env)


_CACHE = {}


def _lay_cols(v, n):
    return np.ascontiguousarray(np.asarray(v, np.float32).reshape(n, 128).T)


RUN_DEPTH = DEPTH
RUN_USE = ("gla", "s5", "gdn", "ffn")


def kernel(**inp):
    depth = RUN_DEPTH
    key = (RUN_USE, depth)
    if key not in _CACHE:
        _CACHE[key] = build_program(depth, RUN_USE)
    nc = _CACHE[key]
    f = lambda a: np.ascontiguousarray(np.asarray(a, np.float32)[:depth])
    shared = {
        "w_ada": f(inp["w_ada"]), "w_in": f(inp["w_in"]), "w_out": f(inp["w_out"]),
        "w_branch_gla": f(inp["w_branch_gla"]), "w_branch_s5": f(inp["w_branch_s5"]),
        "w_branch_gdn": f(inp["w_branch_gdn"]), "w_ffn_in": f(inp["w_ffn_in"]), "w_ffn_out": f(inp["w_ffn_out"]),
        "b_ada": np.stack([_lay_cols(inp["b_ada"][l], 48) for l in range(depth)]),
        "norm1_g": np.stack([_lay_cols(inp["norm1_g"][l], KT) for l in range(depth)]),
        "norm2_g": np.stack([_lay_cols(inp["norm2_g"][l], KT) for l in range(depth)]),
        "final_g": _lay_cols(inp["final_g"], KT),
    }
    shared.update({k: v[:depth] for k, v in layout_mixer_params(inp).items()})
    x = np.ascontiguousarray(np.asarray(inp["x"], np.float32))
    c = np.ascontiguousarray(np.asarray(inp["c"], np.float32))
    in_maps = []
    for core in range(8):
        bi = core % 4
        m = dict(shared)
        m["x"] = x[bi]
        m["c"] = _lay_cols(c[bi], KT)
        in_maps.append(m)
    res = run_bass_kernel_spmd(nc, in_maps, core_ids=list(range(8)))
    out = np.stack([np.asarray(res.results[bi]["out"], np.float32) for bi in range(4)], axis=0)
    return out
```
